# Optimizing a Trainium2 kernel written in Bass

```python
import jax, jax.numpy as jnp
from jax import lax
import numpy as np

D_MODEL = 1024
BATCH = 8
SEQ = 2048
DEPTH = 1
DEC_BATCH = 128
DEC_SEQ = 8
PAST_LEN = 16384
PAGE_SIZE = 128

D_MIX = D_MODEL
D_A = D_MIX // 2
D_B = D_MIX - D_A
H_A = 4
DK_A = D_A // H_A
DV_A = D_A // H_A
H_B = 4
DH_B = D_B // H_B
CMLP_CHUNK = 128
GLA_CHUNK = 64
D_FF = -(-8 * D_MODEL // (3 * 256)) * 256
N_IN = 4 * D_A + 2 * D_B
EPS = 1e-6

kernel_name = "hymba_hgrn2_chunkmlp_decoder_step"


def rmsnorm(x, g):
    xf = x.astype(jnp.float32)
    y = xf * lax.rsqrt(jnp.mean(xf * xf, axis=-1, keepdims=True) + EPS)
    return (y * g.astype(jnp.float32)).astype(x.dtype)


def layernorm(x, g, b):
    xf = x.astype(jnp.float32)
    mu = jnp.mean(xf, axis=-1, keepdims=True)
    xc = xf - mu
    y = xc * lax.rsqrt(jnp.mean(xc * xc, axis=-1, keepdims=True) + EPS)
    return (y * g.astype(jnp.float32) + b.astype(jnp.float32)).astype(x.dtype)


def hgrn2_recurrence(q, logf, k, v, s0):
    B, L = q.shape[0], q.shape[1]
    C = GLA_CHUNK
    N = -(-L // C)
    pad = N * C - L
    padw = ((0, 0), (0, pad), (0, 0), (0, 0))
    q, logf, k, v = [jnp.pad(a.astype(jnp.float32), padw).reshape((B, N, C) + a.shape[2:])
                     for a in (q, logf, k, v)]
    b = jnp.cumsum(logf, axis=2)
    b_last = b[:, :, -1:]
    qd = q * jnp.exp(b)
    kd = k * jnp.exp(-b)
    causal = jnp.tril(jnp.ones((C, C), dtype=bool))
    scores = jnp.einsum('bnthk,bnshk->bnhts', qd, kd)
    scores = jnp.where(causal, scores, 0.0)
    o_intra = jnp.einsum('bnhts,bnshv->bnthv', scores, v)
    k_end = k * jnp.exp(b_last - b)
    ds = jnp.einsum('bnshk,bnshv->bnhkv', k_end, v)
    decay = jnp.exp(b_last[:, :, 0])

    def step(s, inp):
        d, dsn = inp
        return d[..., None] * s + dsn, s

    s_final, s_start = lax.scan(step, s0, (jnp.moveaxis(decay, 1, 0), jnp.moveaxis(ds, 1, 0)))
    s_start = jnp.moveaxis(s_start, 0, 1)
    o_inter = jnp.einsum('bnthk,bnhkv->bnthv', qd, s_start)
    o = (o_intra + o_inter).reshape(B, N * C, q.shape[3], v.shape[4])[:, :L]
    return o, s_final


def hgrn2_mixer(zq, zf, zi, zg, lb, gnorm_w, s0):
    B, L, _ = zq.shape
    q = zq.reshape(B, L, H_A, DK_A)
    lbh = lb.reshape(H_A, DK_A)
    f = lbh + (1.0 - lbh) * jax.nn.sigmoid(zf.reshape(B, L, H_A, DK_A).astype(jnp.float32))
    logf = jnp.log(f)
    k = 1.0 - f
    v = zi.reshape(B, L, H_A, DV_A)
    o, s_new = hgrn2_recurrence(q, logf, k, v, s0.astype(jnp.float32))
    o = rmsnorm(o, gnorm_w) * jax.nn.silu(zg.reshape(B, L, H_A, DV_A).astype(jnp.float32))
    return o.reshape(B, L, D_A).astype(zq.dtype), s_new


def chunk_mlp_mixer(zu, zv, ln_g, ln_b, w_s, b_s):
    B, L, _ = zu.shape
    C = CMLP_CHUNK
    u = jax.nn.gelu(zu).reshape(B, L, H_B, DH_B)
    v = layernorm(jax.nn.gelu(zv), ln_g, ln_b).reshape(B, L, H_B, DH_B)
    N = -(-L // C)
    pad = N * C - L
    vp = jnp.pad(v, ((0, 0), (0, pad), (0, 0), (0, 0))).reshape(B, N, C, H_B, DH_B)
    ws = jnp.where(jnp.tril(jnp.ones((C, C), dtype=bool)), w_s, 0.0)
    mixed = jnp.einsum('hts,bnshd->bnthd', ws, vp) + jnp.transpose(b_s)[None, None, :, :, None]
    mixed = mixed.reshape(B, N * C, H_B, DH_B)[:, :L]
    out = (u * mixed).reshape(B, L, D_B)
    v_rows = v[:, ((L - 1) // C) * C:]
    return out, v_rows


def decoder_layer(x, c, s0, lb, w_ada, b_ada, norm_pre_mix, norm_post_mix, w_in, gnorm_w,
                  ln_v_g, ln_v_b, w_spatial, b_spatial, w_out, norm_pre_ffn, norm_post_ffn,
                  w_gate, w_up, w_down):
    mod = (jax.nn.silu(c) @ w_ada + b_ada)[:, None, :]
    sh1, sc1, g1, sh2, sc2, g2 = jnp.split(mod, 6, axis=-1)
    h = rmsnorm(x, norm_pre_mix) * (1.0 + sc1) + sh1
    z = h @ w_in
    zq, zf, zi, zg, zu, zv = jnp.split(
        z, [D_A, 2 * D_A, 3 * D_A, 4 * D_A, 4 * D_A + D_B], axis=-1)
    oa, s_new = hgrn2_mixer(zq, zf, zi, zg, lb, gnorm_w, s0)
    ob, v_rows = chunk_mlp_mixer(zu, zv, ln_v_g, ln_v_b, w_spatial, b_spatial)
    m = jnp.concatenate([oa, ob], axis=-1) @ w_out
    x = x + g1 * rmsnorm(m, norm_post_mix)
    h2 = rmsnorm(x, norm_pre_ffn) * (1.0 + sc2) + sh2
    f = (jax.nn.silu(h2 @ w_gate) * (h2 @ w_up)) @ w_down
    x = x + g2 * rmsnorm(f, norm_post_ffn)
    return x, s_new, v_rows


def setup_inputs(seed: int = 0) -> dict:
    key = jax.random.key(seed)
    ks = jax.random.split(key, 24)

    def nrm(k, shape, s):
        return s * jax.random.normal(k, shape, jnp.float32)

    return {
        "x_prompt": nrm(ks[0], (BATCH, SEQ, D_MODEL), 1.0),
        "x_sample": nrm(ks[1], (DEC_BATCH, DEC_SEQ, D_MODEL), 1.0),
        "state_hgrn": nrm(ks[2], (DEPTH, DEC_BATCH, H_A, DK_A, DV_A), 0.5),
        "c_prompt": nrm(ks[3], (BATCH, D_MODEL), 1.0),
        "c_sample": nrm(ks[4], (DEC_BATCH, D_MODEL), 1.0),
        "lb_logits": nrm(ks[5], (DEPTH + 1, D_A), 0.1),
        "w_ada": nrm(ks[6], (DEPTH, D_MODEL, 6 * D_MODEL), 0.3 * D_MODEL ** -0.5),
        "b_ada": nrm(ks[7], (DEPTH, 6 * D_MODEL), 0.1),
        "norm_pre_mix": 1.0 + nrm(ks[8], (DEPTH, D_MODEL), 0.1),
        "norm_post_mix": 1.0 + nrm(ks[9], (DEPTH, D_MODEL), 0.1),
        "w_in": nrm(ks[10], (DEPTH, D_MODEL, N_IN), D_MODEL ** -0.5),
        "gnorm_w": 1.0 + nrm(ks[11], (DEPTH, DV_A), 0.1),
        "ln_v_g": 1.0 + nrm(ks[12], (DEPTH, D_B), 0.1),
        "ln_v_b": nrm(ks[13], (DEPTH, D_B), 0.1),
        "w_spatial": nrm(ks[14], (DEPTH, H_B, CMLP_CHUNK, CMLP_CHUNK), CMLP_CHUNK ** -0.5),
        "b_spatial": 1.0 + nrm(ks[15], (DEPTH, H_B, CMLP_CHUNK), 0.1),
        "w_out": nrm(ks[16], (DEPTH, D_MIX, D_MODEL), D_MIX ** -0.5),
        "norm_pre_ffn": 1.0 + nrm(ks[17], (DEPTH, D_MODEL), 0.1),
        "norm_post_ffn": 1.0 + nrm(ks[18], (DEPTH, D_MODEL), 0.1),
        "w_gate": nrm(ks[19], (DEPTH, D_MODEL, D_FF), D_MODEL ** -0.5),
        "w_up": nrm(ks[20], (DEPTH, D_MODEL, D_FF), D_MODEL ** -0.5),
        "w_down": nrm(ks[21], (DEPTH, D_FF, D_MODEL), D_FF ** -0.5),
    }


def reference(x_prompt, x_sample, state_hgrn, c_prompt, c_sample, lb_logits, w_ada, b_ada,
              norm_pre_mix, norm_post_mix, w_in, gnorm_w, ln_v_g, ln_v_b, w_spatial, b_spatial,
              w_out, norm_pre_ffn, norm_post_ffn, w_gate, w_up, w_down):
    lb_all = jnp.cumsum(jax.nn.softmax(lb_logits.astype(jnp.float32), axis=0), axis=0)
    xp, xs = x_prompt, x_sample
    s_p_list, s_s_list, v_p_list, v_s_list = [], [], [], []
    for l in range(DEPTH):
        params = (w_ada[l], b_ada[l], norm_pre_mix[l], norm_post_mix[l], w_in[l], gnorm_w[l],
                  ln_v_g[l], ln_v_b[l], w_spatial[l], b_spatial[l], w_out[l], norm_pre_ffn[l],
                  norm_post_ffn[l], w_gate[l], w_up[l], w_down[l])
        s0_prompt = jnp.zeros((xp.shape[0], H_A, DK_A, DV_A), jnp.float32)
        xp, sp, vp = decoder_layer(xp, c_prompt, s0_prompt, lb_all[l], *params)
        xs, ss, vs = decoder_layer(xs, c_sample, state_hgrn[l], lb_all[l], *params)
        s_p_list.append(sp.astype(x_prompt.dtype))
        s_s_list.append(ss.astype(state_hgrn.dtype))
        v_p_list.append(vp)
        v_s_list.append(vs)
    state_hgrn_prompt = jnp.stack(s_p_list)
    state_hgrn_sample = jnp.stack(s_s_list)
    state_cmlp_v_prompt = jnp.stack(v_p_list)
    state_cmlp_v_sample = jnp.stack(v_s_list)
    return (xp, xs, state_hgrn_prompt, state_hgrn_sample, state_cmlp_v_prompt, state_cmlp_v_sample)
```

```python
import numpy as np
from contextlib import ExitStack
import concourse.bass as bass
import concourse.mybir as mybir
from concourse.bass_utils import run_bass_kernel_spmd

F32 = mybir.dt.float32
BF16 = mybir.dt.bfloat16
AF = mybir.ActivationFunctionType
ALU = mybir.AluOpType

D = 1024
NCORES = 8
SEQ = 2048
NSAMP = 16
DFF = 2816
NJ = DFF // 128
EPS = 1e-6
TBMAX = 768
NTL = 6


class Buf:
    __slots__ = ("name", "lw", "rd")

    def __init__(self, name):
        self.name = name
        self.lw = None
        self.rd = []


class Sched:
    ENGS = ("pe", "act", "dve", "pool", "sp")

    def __init__(self, nc, es, n_dma_sems=32):
        self.nc = nc
        self.lists = {e: [] for e in self.ENGS}
        self.sems = {}
        for e in ("pe", "act", "dve", "pool"):
            self.sems[e] = es.enter_context(nc.semaphore("s_" + e))
        self.cnt = {e: 0 for e in ("pe", "act", "dve", "pool")}
        self.dsems = [es.enter_context(nc.semaphore("s_dma%d" % i)) for i in range(n_dma_sems)]
        self.dcnt = [0] * n_dma_sems
        self.dnext = 0
        self.dnext_pool = 0
        self.waited = {e: {} for e in self.ENGS}
        self.final_waits = []

    def _sem(self, key):
        return self.sems[key] if isinstance(key, str) else self.dsems[key]

    def _need(self, eng, tk):
        if tk is None:
            return
        key, val = tk
        if key == eng:
            if eng == "pe":
                return
            if val > self.cnt[eng]:
                return
        w = self.waited[eng]
        if w.get(key, 0) >= val:
            return
        w[key] = val
        sem = self._sem(key)
        self.lists[eng].append(lambda e, sem=sem, val=val: e.wait_ge(sem, val))

    def _deps(self, eng, reads, writes):
        for b in reads:
            self._need(eng, b.lw)
        for b in writes:
            self._need(eng, b.lw)
            for t in b.rd:
                self._need(eng, t)

    def op(self, eng, fn, reads=(), writes=(), signal=True):
        self._deps(eng, reads, writes)
        if signal:
            self.cnt[eng] += 1
            tk = (eng, self.cnt[eng])
            sem = self.sems[eng]
            self.lists[eng].append(lambda e, fn=fn, sem=sem: fn(e).then_inc(sem, 1))
        else:
            tk = (eng, self.cnt[eng] + 1)
            self.lists[eng].append(lambda e, fn=fn: fn(e))
        for b in reads:
            b.rd.append(tk)
        for b in writes:
            b.lw = tk
            b.rd = []
        return tk

    def dma(self, eng, fn, reads=(), writes=(), final=False):
        self._deps(eng, reads, writes)
        half = len(self.dsems) // 2
        if eng == "pool":
            k = half + self.dnext_pool
            self.dnext_pool = (self.dnext_pool + 1) % (len(self.dsems) - half)
        else:
            k = self.dnext
            self.dnext = (self.dnext + 1) % half
        if self.dcnt[k] > 0:
            self._need(eng, (k, self.dcnt[k]))
        self.dcnt[k] += 16
        tk = (k, self.dcnt[k])
        sem = self.dsems[k]
        self.lists[eng].append(lambda e, fn=fn, sem=sem: fn(e).then_inc(sem, 16))
        for b in reads:
            b.rd.append(tk)
        for b in writes:
            b.lw = tk
            b.rd = []
        if final:
            self.final_waits.append(tk)
        return tk

    def emit(self):
        for tk in self.final_waits:
            self._need("sp", tk)
        nc = self.nc
        lists = self.lists
        with nc.Block() as block:
            @block.tensor
            def _(e):
                for f in lists["pe"]:
                    f(e)

            @block.scalar
            def _(e):
                for f in lists["act"]:
                    f(e)

            @block.vector
            def _(e):
                for f in lists["dve"]:
                    f(e)

            @block.gpsimd
            def _(e):
                for f in lists["pool"]:
                    f(e)

            @block.sync
            def _(e):
                for f in lists["sp"]:
                    f(e)


def bc(ap, dims):
    return bass.AP(ap.tensor, ap.offset, [ap.ap[0]] + [list(d) for d in dims])


R_NPRE, R_NPOST, R_FPRE, R_FPOST, R_BADA, R_LB0, R_LB1, R_GN, R_LNG, R_LNB, R_BS = 0, 8, 16, 24, 32, 80, 84, 88, 89, 93, 97


def build_nc(debug=False):
    nc = bass.Bass("TRN2", target_bir_lowering=False)
    din = lambda n, s: nc.dram_tensor(n, s, F32, kind="ExternalInput").ap()
    dout = lambda n, s: nc.dram_tensor(n, s, F32, kind="ExternalOutput").ap()
    xp = din("xp", [SEQ, D]); xs = din("xs", [128, D]); s0 = din("s0", [NSAMP, 4, 128, 128])
    call = din("call", [17, D]); vecs = din("vecs", [128, 128]); wspat = din("wspat", [4, 128, 128])
    w_ada = din("w_ada", [D, 6 * D]); w_in = din("w_in", [D, 3072]); w_out = din("w_out", [D, D])
    w_gate = din("w_gate", [D, DFF]); w_up = din("w_up", [D, DFF]); w_down = din("w_down", [DFF, D])
    yp = dout("yp", [SEQ, D]); ys = dout("ys", [128, D]); o_sp = dout("o_sp", [4, 128, 128])
    o_ss = dout("o_ss", [NSAMP, 4, 128, 128]); o_vp = dout("o_vp", [128, 512]); o_vs = dout("o_vs", [128, 512])
    if debug:
        d_x1 = dout("d_x1", [NTL * 128, D])
        d_oa = nc.dram_tensor("d_oa", [128, 4, TBMAX], BF16, kind="ExternalOutput").ap()
        d_ob = nc.dram_tensor("d_ob", [128, 4, TBMAX], BF16, kind="ExternalOutput").ap()
        d_h2 = nc.dram_tensor("d_h2", [128, 8, TBMAX], BF16, kind="ExternalOutput").ap()

    with ExitStack() as es:
        S = Sched(nc, es)
        T = lambda name, shape, dt: es.enter_context(nc.sbuf_tensor(name, shape, dt))

        xres = T("xres", [128, NTL, D], F32);          b_x = [Buf("x%d" % i) for i in range(NTL)]
        aT = T("aT", [128, 8, TBMAX], BF16);           b_aT = [Buf("aT%d" % i) for i in range(NTL)]
        oaT = T("oaT", [128, 4, TBMAX], BF16);         b_oa = [Buf("oa%d" % i) for i in range(NTL)]
        ARENA = 23552
        arena = T("arena", [128, ARENA], BF16)
        b_act = [[Buf("act%d_%d" % (j, g)) for g in range(3)] for j in range(NJ)]
        warena = T("warena", [128, NJ * D], BF16)
        b_wd = [Buf("wd%d" % j) for j in range(NJ)]
        wd_tail = b_wd[16:NJ]
        wring = T("wring", [128, 4, 8, 128], BF16);    b_wring = [Buf("wr%d" % i) for i in range(4)]
        b_ada = [Buf("ada%d" % i) for i in range(2)]
        ident = T("ident", [128, 128], F32);           b_c = Buf("consts")
        identb = T("identb", [128, 128], BF16)
        ones = T("ones", [128, 128], F32)
        Mc = T("Mc", [128, 128], F32); M64 = T("M64", [128, 128], F32); Ms = T("Ms", [128, 128], F32)
        Esel = T("Esel", [128, 16], F32); E16 = T("E16", [16, 128], F32)
        nhalf = T("nhalf", [128, 1], F32)
        vecs_sb = T("vecs_sb", [128, 128], F32);       b_vecs = Buf("vecs")
        colv = T("colv", [128, 128], F32);             b_colv = Buf("colv")
        AB = T("AB", [128, 8], F32);                   b_AB = Buf("AB")
        nB = T("nB", [128, 4], F32)
        gnh = T("gnh", [128, 1], F32)
        st4 = T("st4", [128, 4], F32);                 b_st4 = Buf("st4")
        scT = T("scT", [128, 8, 17], BF16);            b_scT = Buf("scT")
        modT = T("modT", [128, 48, 17], F32);          b_mod = Buf("modT")
        Gpre = T("Gpre", [128, 8, 17], F32); Gpf = T("Gpf", [128, 8, 17], F32)
        G1f = T("G1f", [128, 8, 17], F32); G2f = T("G2f", [128, 8, 17], F32)
        b_G = Buf("Gs")
        Lm = T("Lm", [128, 2, 128], F32);              b_Lm = [Buf("Lm0"), Buf("Lm1")]
        G1bc = T("G1bc", [128, 2, D], F32); G2bc = T("G2bc", [128, 2, D], F32)
        b_Gbc = Buf("Gbc")
        lnbc = T("lnbc", [128, 2, 512], F32)
        bsbc = T("bsbc", [128, 2, 4, 128], F32)
        wsT = T("wsT", [128, 2, 4, 128], BF16);        b_ws = Buf("ws")
        Rep8 = T("Rep8", [8, 128], F32)
        Sst = T("Sst", [128, 4, 128], F32);            b_S = Buf("S")
        Sbf = T("Sbf", [128, 2, 4, 128], BF16);        b_Sbf = [Buf("Sbf0"), Buf("Sbf1")]
        Stmp = T("Stmp", [128, 4, 128], F32);          b_Stmp = Buf("Stmp")
        b_s0 = [Buf("s0_%d" % i) for i in range(4)]
        s0bf = T("s0bf", [128, 2, 4, 128], BF16);      b_s0bf = [Buf("s0bf0"), Buf("s0bf1")]
        b_snew = [Buf("snew0"), Buf("snew1")]
        PL = T("PL", [128, 40, 4], F32);               b_PL = Buf("PL")
        ssq = T("ssq", [128, 64], F32);                b_st = [Buf("st%d" % i) for i in range(64)]
        rstd = T("rstd", [128, 64], F32)
        junk = T("junk", [128, D], BF16);              b_junk = Buf("junk")
        xsr = T("xsr", [128, 2, D], F32);              b_xsr = [Buf("xsr0"), Buf("xsr1")]
        ftmp = T("ftmp", [128, 2, 512], F32);          b_ft = [Buf("ft0"), Buf("ft1")]
        csb = xsr[0:17, 0, :]; cth = xsr[0:17, 1, :]; b_csb = b_xsr[0]; b_cth = b_xsr[1]
        wsraw = xsr[:, 0, 0:512].rearrange("p (h t) -> p h t", h=4); b_wsraw = b_xsr[0]
        wtmp = xsr[:, 0, 512:1024].rearrange("p (h t) -> p h t", h=4); b_wtmp = b_xsr[0]
        s0slot = [xsr[:, 0, 0:512], xsr[:, 1, 0:512], xres[:, 5, 0:512], xres[:, 5, 512:1024]]

        def s0_load(j):
            sl = j % 4
            guards = ([b_xsr[sl]] if sl < 2 else [b_x[5]]) if j < 4 else []
            S.dma("sp", lambda e, j=j, sl=sl: e.dma_start(out=s0slot[sl].rearrange("p (h v) -> p h v", h=4), in_=s0[j].rearrange("h k v -> k h v")),
                  writes=[b_s0[sl]] + guards)
        snew = [xsr[:, r, 512:1024].rearrange("p (h v) -> p h v", h=4) for r in range(2)]
        ftmp2 = T("ftmp2", [128, 2, 512], F32);        b_ft2 = [Buf("ft20"), Buf("ft21")]
        vout = T("vout", [128, 512], F32);             b_vout = Buf("vout")
        bnst = T("bnst", [128, 5, 6], F32); bnag = T("bnag", [128, 5, 4], F32); b_bn = [Buf("bn%d" % i) for i in range(5)]
        b_scm = [Buf("scm0"), Buf("scm1")]; b_ktok = [Buf("kt0"), Buf("kt1")]; b_ktm = [Buf("ktm0"), Buf("ktm1")]
        b_ket = [Buf("ket0"), Buf("ket1")]

        off = [0]

        def carve(n_elems, dt):
            mult = 2 if dt == F32 else 1
            start = off[0]
            off[0] += n_elems * mult
            assert off[0] <= ARENA, off[0]
            reg = arena[:, start:start + n_elems * mult]
            return reg.bitcast(dt) if dt == F32 else reg

        TB = TBMAX
        assert off[0] == 0
        qd_all = carve(4 * TB, BF16)
        kd_all = carve(4 * TB, BF16)
        qdT = [qd_all[:, h * TB:(h + 1) * TB] for h in range(4)]
        kdT = [kd_all[:, h * TB:(h + 1) * TB] for h in range(4)]
        uu = warena[:, 16384:16384 + 4 * TB]
        vln = warena[:, 16384 + 4 * TB:16384 + 4 * TB + NTL * 512]
        sg2 = carve(4 * TB, BF16)
        vtok = carve(NTL * 512, BF16)
        zq = [carve(512, F32) for _ in range(2)]
        thb = [carve(512, F32) for _ in range(2)]
        fb = thb
        kb = [carve(512, F32)] * 2
        Pb = [carve(512, F32)] * 2
        Pinv = [carve(512, F32)] * 2
        Eb = Pinv
        gv = zq
        et1 = thb
        et2 = [kb[0], Pb[0]]
        ket = [carve(512, BF16) for _ in range(2)]
        scm_ = [carve(512, BF16) for _ in range(2)]
        ktok_ = [carve(512, BF16) for _ in range(2)]
        ktm_ = [carve(512, BF16) for _ in range(2)]
        b_qd = [Buf("qd%d" % h) for h in range(4)]; b_kd = [Buf("kd%d" % h) for h in range(4)]
        b_sg = Buf("sg2"); b_uu = Buf("uu")
        b_vtok = [Buf("vtok%d" % i) for i in range(NTL)]; b_vln = [Buf("vln%d" % i) for i in range(NTL)]
        b_zq = [Buf("zq0"), Buf("zq1")]; b_th = [Buf("th0"), Buf("th1")]; b_fb = b_th
        b_kb = [Buf("kb")] * 2; b_Pb = [Buf("Pb")] * 2; b_Pi = [Buf("Pi")] * 2; b_Eb = b_Pi
        b_gv = b_zq; b_e1 = b_th; b_e2 = [b_kb[0], b_Pb[0]]
        mix_bufs = (b_qd + b_kd + [b_sg, b_uu] + b_vtok + b_vln + b_zq + b_th + [b_kb[0], b_Pb[0], b_Pi[0]]
                    + b_scm + b_ktok + b_ktm + b_ket)
        act_bufs = [b for row in b_act for b in row]

        def v4(ap):
            return ap.rearrange("p (h k) -> p h k", h=4)

        assert NJ * TBMAX <= ARENA

        def actT(j, c0, n):
            return arena[:, j * TBMAX + c0: j * TBMAX + c0 + n]

        WZI0 = NJ * TBMAX
        wz_i = arena[:, WZI0:WZI0 + 4096]
        wz_v = warena[:, 4096:8192]
        wo_sb = warena[:, 8192:16384]
        b_wzi = Buf("wzi"); b_wzv = Buf("wzv"); b_wo = Buf("wo")

        def wd_sb(j):
            return warena[:, j * D:(j + 1) * D]

        adaring = arena[:, 0:8192].rearrange("p (r k n) -> p r k n", r=2, k=8)

        ps = [es.enter_context(nc.psum_tensor("ps%d" % i, [128, 512], F32)) for i in range(8)]
        b_ps = [Buf("ps%d" % i) for i in range(8)]
        mmring = [0]

        def next_bank(n=2):
            k = mmring[0] % n
            mmring[0] += 1
            return k

        S.op("pool", lambda e: e.memset(ones[:], 1.0), writes=[b_c])
        S.op("pool", lambda e: e.memset(nhalf[:], -0.5), writes=[b_c])
        S.op("pool", lambda e: e.affine_select(out=ident[:], in_=ones[:], pattern=[[1, 128]], compare_op=ALU.is_equal,
                                               fill=0.0, base=0, channel_multiplier=-1), reads=[b_c], writes=[b_c])
        S.op("pool", lambda e: e.affine_select(out=Mc[:], in_=ones[:], pattern=[[1, 128]], compare_op=ALU.is_ge,
                                               fill=0.0, base=0, channel_multiplier=-1), reads=[b_c], writes=[b_c])
        S.op("pool", lambda e: e.tensor_copy(out=identb[:], in_=ident[:]), reads=[b_c], writes=[b_c])
        S.op("pool", lambda e: e.memset(M64[:], 0.0), writes=[b_c])
        S.op("pool", lambda e: e.tensor_copy(out=M64[0:64, 0:64], in_=Mc[0:64, 0:64]), reads=[b_c], writes=[b_c])
        S.op("pool", lambda e: e.tensor_copy(out=M64[64:128, 64:128], in_=Mc[64:128, 64:128]), reads=[b_c], writes=[b_c])
        S.op("pool", lambda e: e.tensor_copy(out=E16[:].rearrange("p (j i) -> p j i", i=8),
                                             in_=bc(ident[0:16, 0:16], [[1, 16], [0, 8]])), reads=[b_c], writes=[b_c])
        S.op("pool", lambda e: e.tensor_copy(out=Rep8[:].rearrange("p (j i) -> p j i", i=8),
                                             in_=bc(ident[0:8, 0:8], [[0, 16], [1, 8]])), reads=[b_c], writes=[b_c])
        S.op("pe", lambda e: e.matmul(ps[7][:, 0:128], lhsT=E16[:], rhs=E16[:], start=True, stop=True), reads=[b_c], writes=[b_ps[7]])
        S.op("dve", lambda e: e.tensor_tensor(out=Ms[:], in0=ps[7][:, 0:128], in1=Mc[:], op=ALU.mult), reads=[b_ps[7], b_c], writes=[b_c])
        S.op("pe", lambda e: e.transpose(out=ps[7][:, 128:144], in_=E16[:], identity=ident[0:16, 0:16]), reads=[b_c], writes=[b_ps[7]])
        S.op("dve", lambda e: e.tensor_copy(out=Esel[:], in_=ps[7][:, 128:144]), reads=[b_ps[7]], writes=[b_c])

        S.dma("sp", lambda e: e.dma_start(out=vecs_sb[:], in_=vecs), writes=[b_vecs])
        S.op("pe", lambda e: e.transpose(out=ps[6][:, 0:128], in_=vecs_sb[:], identity=ident[:]), reads=[b_vecs, b_c], writes=[b_ps[6]])
        S.op("act", lambda e: e.activation(out=colv[:], in_=ps[6][:, 0:128], func=AF.Copy), reads=[b_ps[6]], writes=[b_colv])
        S.op("dve", lambda e: e.tensor_tensor(out=AB[:, 0:4], in0=colv[:, R_LB0:R_LB0 + 4], in1=colv[:, R_LB1:R_LB1 + 4], op=ALU.subtract),
             reads=[b_colv], writes=[b_AB])
        S.op("act", lambda e: e.activation(out=AB[:, 4:8], in_=AB[:, 0:4], func=AF.Tanh, scale=0.5), reads=[b_AB], writes=[b_AB])
        S.op("dve", lambda e: e.tensor_scalar(out=AB[:, 0:4], in0=AB[:, 4:8], scalar1=0.25, scalar2=0.75, op0=ALU.mult, op1=ALU.add),
             reads=[b_AB], writes=[b_AB])
        S.op("dve", lambda e: e.tensor_scalar(out=nB[:], in0=AB[:, 4:8], scalar1=0.25, scalar2=-0.25, op0=ALU.mult, op1=ALU.add),
             reads=[b_AB], writes=[b_AB])
        S.op("dve", lambda e: e.tensor_scalar(out=AB[:, 4:8], in0=nB[:], scalar1=-1.0, scalar2=None, op0=ALU.mult),
             reads=[b_AB], writes=[b_AB])
        S.op("dve", lambda e: e.tensor_scalar(out=gnh[:], in0=colv[:, R_GN:R_GN + 1], scalar1=0.5, scalar2=None, op0=ALU.mult),
             reads=[b_colv], writes=[b_AB])

        S.dma("sp", lambda e: e.dma_start(out=csb, in_=call), writes=[b_csb])
        S.op("act", lambda e: e.activation(out=cth, in_=csb, func=AF.Tanh, scale=0.5), reads=[b_csb], writes=[b_cth])
        S.op("dve", lambda e: e.tensor_scalar(out=cth, in0=cth, scalar1=0.5, scalar2=0.5, op0=ALU.mult, op1=ALU.add),
             reads=[b_cth], writes=[b_cth])
        S.op("dve", lambda e: e.tensor_tensor(out=cth, in0=cth, in1=csb, op=ALU.mult), reads=[b_cth, b_csb], writes=[b_cth])
        for kc in range(8):
            S.op("pe", lambda e, kc=kc: e.transpose(out=ps[6][:, 128 + kc * 17:128 + (kc + 1) * 17], in_=cth[:, kc * 128:(kc + 1) * 128],
                                                   identity=ident[0:17, 0:17]), reads=[b_cth, b_c], writes=[b_ps[6]])
        S.op("act", lambda e: e.activation(out=scT[:], in_=ps[6][:, 128:128 + 136].rearrange("p (k s) -> p k s", s=17), func=AF.Copy),
             reads=[b_ps[6]], writes=[b_scT])

        wada_v = w_ada.rearrange("(kc p) n -> p kc n", p=128)
        for ch in range(12):
            r = ch % 2
            S.dma("pool", lambda e, ch=ch, r=r: e.dma_start(out=adaring[:, r], in_=wada_v[:, :, ch * 512:(ch + 1) * 512]), writes=[b_ada[r]])
            for jj in range(4):
                j = ch * 4 + jj
                bank = 6 + (j // 24)
                col = (j % 24) * 17
                for kc in range(8):
                    S.op("pe", lambda e, r=r, jj=jj, kc=kc, bank=bank, col=col: e.matmul(
                        ps[bank][:, col:col + 17], lhsT=adaring[:, r, kc, jj * 128:(jj + 1) * 128], rhs=scT[:, kc, :],
                        start=(kc == 0), stop=(kc == 7)), reads=[b_ada[r], b_scT], writes=[b_ps[bank]], signal=(kc == 7))
        for half in range(2):
            S.op("dve", lambda e, half=half: e.tensor_tensor(
                out=modT[:, half * 24:(half + 1) * 24, :], in0=ps[6 + half][:, 0:408].rearrange("p (j s) -> p j s", s=17),
                in1=bc(colv[:, R_BADA + half * 24:R_BADA + half * 24 + 24], [[1, 24], [0, 17]]), op=ALU.add),
                reads=[b_ps[6 + half], b_colv], writes=[b_mod])
        for (dst, mo, ro) in ((Gpre, 8, R_NPRE), (Gpf, 32, R_FPRE)):
            S.op("dve", lambda e, dst=dst, mo=mo: e.tensor_scalar(out=dst[:], in0=modT[:, mo:mo + 8, :], scalar1=1.0, scalar2=None, op0=ALU.add),
                 reads=[b_mod], writes=[b_G])
            S.op("dve", lambda e, dst=dst, ro=ro: e.tensor_tensor(out=dst[:], in0=dst[:], in1=bc(colv[:, ro:ro + 8], [[1, 8], [0, 17]]), op=ALU.mult),
                 reads=[b_G, b_colv], writes=[b_G])
        for (dst, mo, ro) in ((G1f, 16, R_NPOST), (G2f, 40, R_FPOST)):
            S.op("dve", lambda e, dst=dst, mo=mo, ro=ro: e.tensor_tensor(out=dst[:], in0=modT[:, mo:mo + 8, :],
                                                                          in1=bc(colv[:, ro:ro + 8], [[1, 8], [0, 17]]), op=ALU.mult),
                 reads=[b_mod, b_colv], writes=[b_G])

        lmi = [0]

        def bcast_block(src_fn, dst_ap, dst_buf, bank, col, extra_reads=()):
            r = lmi[0] % 2
            lmi[0] += 1
            S.op("dve", lambda e, r=r: src_fn(e, Lm[:, r, :]), reads=list(extra_reads), writes=[b_Lm[r]])
            S.op("pe", lambda e, r=r: e.matmul(ps[bank][:, col:col + 128], lhsT=Lm[:, r, :], rhs=ident[:], start=True, stop=True),
                 reads=[b_Lm[r], b_c], writes=[b_ps[bank]])

        for (Gf, Gb) in ((G1f, G1bc), (G2f, G2bc)):
            for smp in range(2):
                for half in range(2):
                    bank = 6 + half
                    for cc in range(4):
                        c = half * 4 + cc
                        if smp == 0:
                            fn = lambda e, Lo, Gf=Gf, c=c: e.tensor_copy(out=Lo, in_=bc(Gf[:, c, 0:1], [[0, 128]]))
                        else:
                            fn = lambda e, Lo, Gf=Gf, c=c: e.tensor_copy(out=Lo.rearrange("p (j i) -> p j i", i=8),
                                                                        in_=bc(Gf[:, c, 1:17], [[1, 16], [0, 8]]))
                        bcast_block(fn, None, None, bank, cc * 128, extra_reads=[b_G])
                    S.op("act", lambda e, Gb=Gb, smp=smp, half=half, bank=bank: e.activation(
                        out=Gb[:, smp, half * 512:(half + 1) * 512], in_=ps[bank][:, :], func=AF.Copy), reads=[b_ps[bank]], writes=[b_Gbc])
        for which in range(2):
            for cc in range(4):
                row = (R_LNG if which == 0 else R_LNB) + cc
                fn = lambda e, Lo, row=row: e.tensor_copy(out=Lo, in_=bc(colv[:, row:row + 1], [[0, 128]]))
                bcast_block(fn, None, None, 6, cc * 128, extra_reads=[b_colv])
            S.op("act", lambda e, which=which: e.activation(out=lnbc[:, which, :], in_=ps[6][:, :], func=AF.Copy), reads=[b_ps[6]], writes=[b_Gbc])
        for h in range(4):
            fn = lambda e, Lo, h=h: e.tensor_copy(out=Lo, in_=bc(colv[:, R_BS + h:R_BS + h + 1], [[0, 128]]))
            bcast_block(fn, None, None, 7, h * 128, extra_reads=[b_colv])
        S.op("act", lambda e: e.activation(out=bsbc[:, 0].rearrange("p h t -> p (h t)"), in_=ps[7][:, :], func=AF.Copy), reads=[b_ps[7]], writes=[b_Gbc])
        S.op("dve", lambda e: e.tensor_copy(out=bsbc[:, 1].rearrange("p h (j i) -> p h j i", i=8),
                                            in_=bc(bsbc[:, 0, 0, 0:8], [[128, 4], [0, 16], [1, 8]])), reads=[b_Gbc], writes=[b_Gbc])

        S.dma("sp", lambda e: e.dma_start(out=wsraw, in_=wspat.rearrange("h t s -> t h s")), writes=[b_wsraw])
        for h in range(4):
            S.op("pe", lambda e, h=h: e.transpose(out=ps[6][:, h * 128:(h + 1) * 128], in_=wsraw[:, h, :], identity=ident[:]),
                 reads=[b_wsraw, b_c], writes=[b_ps[6]])
        S.op("act", lambda e: e.activation(out=xsr[:, 0, 512:1024], in_=ps[6][:, :], func=AF.Copy), reads=[b_ps[6]], writes=[b_wtmp])
        S.op("dve", lambda e: e.tensor_tensor(out=wsT[:, 0], in0=wtmp, in1=bc(Mc[:, :], [[0, 4], [1, 128]]), op=ALU.mult),
             reads=[b_wtmp, b_c], writes=[b_ws])
        S.op("dve", lambda e: e.tensor_copy(out=xsr[0:8, 1, 0:512].rearrange("p (h j i) -> p h j i", h=4, i=8),
                                            in_=bc(xsr[0:8, 0, 512:520], [[128, 4], [0, 16], [1, 8]])), reads=[b_wtmp], writes=[b_xsr[1]])
        S.op("pe", lambda e: e.matmul(ps[7][:, :], lhsT=Rep8[:], rhs=xsr[0:8, 1, 0:512], start=True, stop=True),
             reads=[b_xsr[1], b_c], writes=[b_ps[7]])
        S.op("dve", lambda e: e.tensor_tensor(out=wsT[:, 1], in0=ps[7][:, :].rearrange("p (h t) -> p h t", h=4),
                                              in1=bc(Ms[:, :], [[0, 4], [1, 128]]), op=ALU.mult), reads=[b_ps[7], b_c], writes=[b_ws])
        S.op("pool", lambda e: e.memset(Sst[:], 0.0), writes=[b_S])
        S.op("pool", lambda e: e.memset(Sbf[:, 0], 0.0), writes=[b_Sbf[0]])

        xp_t = xp.rearrange("(n p) d -> n p d", p=128)
        yp_t = yp.rearrange("(n p) d -> n p d", p=128)
        win_v = w_in.rearrange("(kc p) n -> p kc n", p=128)
        wout_v = w_out.rearrange("(kc p) n -> p kc n", p=128)
        wg_v = w_gate.rearrange("(kc p) n -> p kc n", p=128)
        wu_v = w_up.rearrange("(kc p) n -> p kc n", p=128)
        ssq_i = [0]
        po_i = [0]
        wr_i = [0]
        rr = {"xsr": 0, "ys": 0, "ft": 0, "zq": 0, "th": 0, "scm": 0, "kt": 0, "gv": 0, "e": 0}

        def rot(name, n=2):
            v = rr[name] % n
            rr[name] += 1
            return v

        def pre_norm(tiles, sample_flags, Gt, sh_off, hook=None):
            base = (ssq_i[0] % 6) * 8
            ssq_i[0] += 1
            nt = len(tiles)
            stb = [b_st[base + i] for i in range(nt)]
            for i, li in enumerate(tiles):
                col = base + i
                S.op("act", lambda e, li=li, col=col: e.activation(out=junk[:], in_=xres[:, li, :], func=AF.Square,
                                                                  accum_out=ssq[:, col:col + 1]), reads=[b_x[li]], writes=[b_st[col], b_junk])
            S.op("dve", lambda e: e.tensor_scalar(out=rstd[:, base:base + nt], in0=ssq[:, base:base + nt], scalar1=1.0 / D, scalar2=EPS,
                                                  op0=ALU.mult, op1=ALU.add), reads=stb, writes=stb)
            S.op("pool", lambda e: e.tensor_tensor(out=rstd[:, base:base + nt], in0=rstd[:, base:base + nt], in1=bc(nhalf[:, 0:1], [[0, nt]]), op=ALU.pow),
                 reads=stb + [b_c], writes=stb)
            slots = {}

            def scale_tile(i):
                li = tiles[i]
                col = base + i
                r = rot("xsr")
                slots[i] = r
                S.op("dve", lambda e, li=li, r=r, col=col: e.tensor_scalar(out=xsr[:, r, :], in0=xres[:, li, :], scalar1=rstd[:, col:col + 1],
                                                                          scalar2=None, op0=ALU.mult),
                     reads=[b_x[li], b_st[col]], writes=[b_xsr[r], b_s0[r], b_snew[r]])

            scale_tile(0)
            for i, (li, is_s) in enumerate(zip(tiles, sample_flags)):
                r = slots[i]
                for half in range(2):
                    bank = next_bank(6)
                    for cc in range(4):
                        c = half * 4 + cc
                        S.op("pe", lambda e, r=r, c=c, cc=cc, bank=bank: e.transpose(out=ps[bank][:, cc * 128:(cc + 1) * 128],
                                                                                   in_=xsr[:, r, c * 128:(c + 1) * 128], identity=ident[:]),
                             reads=[b_xsr[r], b_c], writes=[b_ps[bank]], signal=(cc == 3))
                    if half == 0 and i + 1 < nt:
                        scale_tile(i + 1)
                    k = rot("ft")
                    if not is_s:
                        g_ap = bc(Gt[:, half * 4, 0:1], [[17, 4], [0, 128]])
                        s_ap = bc(modT[:, sh_off + half * 4, 0:1], [[17, 4], [0, 128]])
                        pat = "p (c t) -> p c t"
                        kw = dict(c=4)
                        opat = "p c t -> p c t"
                    else:
                        g_ap = bc(Gt[:, half * 4, 1:17], [[17, 4], [1, 16], [0, 8]])
                        s_ap = bc(modT[:, sh_off + half * 4, 1:17], [[17, 4], [1, 16], [0, 8]])
                        pat = "p (c j i) -> p c j i"
                        kw = dict(c=4, i=8)
                        opat = "p c (j i) -> p c j i"
                    S.op("dve", lambda e, bank=bank, k=k, g_ap=g_ap, pat=pat, kw=kw: e.tensor_tensor(
                        out=ftmp[:, k, :].rearrange(pat, **kw), in0=ps[bank][:, :].rearrange(pat, **kw), in1=g_ap, op=ALU.mult),
                        reads=[b_ps[bank], b_G], writes=[b_ft[k]])
                    okw = dict(i=8) if is_s else {}
                    S.op("pool", lambda e, half=half, k=k, li=li, s_ap=s_ap, pat=pat, kw=kw, opat=opat, okw=okw: e.tensor_tensor(
                        out=aT[:, half * 4:half * 4 + 4, li * 128:(li + 1) * 128].rearrange(opat, **okw) if okw else aT[:, half * 4:half * 4 + 4, li * 128:(li + 1) * 128],
                        in0=ftmp[:, k, :].rearrange(pat, **kw), in1=s_ap, op=ALU.add),
                        reads=[b_ft[k], b_mod], writes=[b_aT[li]])
                if hook is not None:
                    hook(i)

        WQ = []
        WQ_tag = []
        for _p in range(3):
            for hp in range(2):
                WQ += [(win_v, hp * 256), (win_v, 512 + hp * 256)]
            WQ += [(win_v, 1536 + hp * 256) for hp in range(2)]
            WQ += [(win_v, 2048 + hp * 256) for hp in range(2)]
            WQ_tag += [(_p, "B")] * 6 + [(_p, "B") if _p < 2 else (_p, "U")] * 2
            for jp in range(NJ // 2):
                WQ += [(wg_v, jp * 256), (wu_v, jp * 256)]
            WQ_tag += [(_p, "F")] * NJ
        xsr_ok = set()
        wq_pos = [0]
        wq_issued = [0]
        wq_restrict = [True]
        wq_rr = [0]
        wring_flat = wring[:].rearrange("p a k n -> p (a k n)")
        wslot = [wring_flat[:, 0:2048], wring_flat[:, 2048:4096], xsr[:, 0, :].bitcast(BF16), xsr[:, 1, :].bitcast(BF16)]
        wslot_r = [b_wring[0], b_wring[1], b_xsr[0], b_xsr[1]]
        wslot_w = [[b_wring[0]], [b_wring[1]], [b_xsr[0], b_s0[0], b_snew[0]], [b_xsr[1], b_s0[1], b_snew[1]]]
        slot_owner = [None] * 4
        chunk_slot = {}
        consumed = set()

        def wq_issue(upto):
            while wq_issued[0] <= min(upto, len(WQ) - 1):
                cands = [0, 1, 2, 3] if WQ_tag[wq_issued[0]] in xsr_ok else [0, 1]
                free = [sl for sl in cands if slot_owner[sl] is None or slot_owner[sl] in consumed]
                if not free:
                    return
                free.sort(key=lambda sl: -1 if slot_owner[sl] is None else slot_owner[sl])
                sl = free[0]
                k = wq_issued[0]
                wq_issued[0] += 1
                slot_owner[sl] = k
                chunk_slot[k] = sl
                src_v, col0 = WQ[k]
                S.dma("pool", lambda e, sl=sl, src_v=src_v, col0=col0: e.dma_start(out=wslot[sl].rearrange("p (k n) -> p k n", k=8),
                                                                                  in_=src_v[:, :, col0:col0 + 256]),
                      writes=wslot_w[sl])

        def wring_load(src_v, col0):
            k = wq_pos[0]
            wq_pos[0] += 1
            assert WQ[k][1] == col0 and WQ[k][0] is src_v, (k, col0)
            wq_issue(k)
            assert k in chunk_slot, "weight chunk could not be issued (no free staging slot)"
            return k

        def wq_done():
            for k in range(wq_pos[0]):
                consumed.add(k)
            wq_issue(wq_pos[0] - 1 + 4)

        def fm_matmul(k, sub, grp, bank):
            c0, n, tl, _s = grp
            sl = chunk_slot[k]
            for kc in range(8):
                S.op("pe", lambda e, kc=kc: e.matmul(ps[bank][:, 0:n], lhsT=wslot[sl][:, kc * 256 + sub * 128:kc * 256 + (sub + 1) * 128],
                                                    rhs=aT[:, kc, c0:c0 + n], start=(kc == 0), stop=(kc == 7)),
                     reads=[wslot_r[sl]] + [b_aT[t] for t in tl], writes=[b_ps[bank]], signal=(kc == 7))

        PASS_TILES = [6, 6, 5]
        for pas in range(3):
            ntile = PASS_TILES[pas]
            tiles = list(range(ntile))
            sflags = [False] * ntile
            if pas == 2:
                sflags[4] = True
                groups = [(0, 512, [0, 1, 2, 3], False), (512, 128, [4], True)]
            else:
                groups = [(0, 512, [0, 1, 2, 3], False), (512, 256, [4, 5], False)]
            gtile0 = pas * 6

            for li in tiles:
                src = xs if sflags[li] else xp_t[gtile0 + li]
                S.dma("sp", lambda e, li=li, src=src: e.dma_start(out=xres[:, li, :], in_=src), writes=[b_x[li]])
            wzi_guard = [b_kb[0], b_Pb[0], b_Pi[0]] + b_ket + b_scm
            if pas == 0:
                S.dma("pool", lambda e: e.dma_start(out=wz_i.rearrange("p (k n) -> p k n", k=8), in_=win_v[:, :, 1024:1536]), writes=[b_wzi] + wzi_guard)
            S.dma("pool", lambda e: e.dma_start(out=wz_v.rearrange("p (k n) -> p k n", k=8), in_=win_v[:, :, 2560:3072]),
                  writes=[b_wzv] + b_wd[4:8] + (b_ada if pas == 0 else []))
            S.dma("pool", lambda e: e.dma_start(out=wo_sb.rearrange("p (k n) -> p k n", k=8), in_=wout_v), writes=[b_wo] + b_wd[8:16])

            def wd_load(j, guards):
                S.dma("pool", lambda e, j=j: e.dma_start(out=wd_sb(j), in_=w_down[j * 128:(j + 1) * 128, :]), writes=[b_wd[j]] + list(guards))

            for j in range(0, 4):
                wd_load(j, b_ada if pas == 0 else [])
            wq_issue(wq_pos[0] + 3)
            first_mix = [True]

            def mixw(bufs):
                if first_mix[0]:
                    first_mix[0] = False
                    return list(bufs) + act_bufs + b_ada
                return list(bufs)

            def zi_tile(li):
                bank = 6 + (li % 2)
                for kc in range(8):
                    S.op("pe", lambda e, kc=kc, li=li, bank=bank: e.matmul(ps[bank][:, :], lhsT=aT[:, kc, li * 128:(li + 1) * 128],
                                                                         rhs=wz_i[:, kc * 512:(kc + 1) * 512], start=(kc == 0), stop=(kc == 7)),
                         reads=[b_aT[li], b_wzi], writes=[b_ps[bank]], signal=(kc == 7))
                S.op("act", lambda e, li=li, bank=bank: e.activation(out=vtok[:, li * 512:(li + 1) * 512], in_=ps[bank][:, :], func=AF.Copy),
                     reads=[b_ps[bank]], writes=mixw([b_vtok[li]]))

            pre_norm(tiles, sflags, Gpre, 0, hook=lambda i: zi_tile(tiles[i - 1]) if i >= 1 else None)
            zi_tile(tiles[-1])
            xsr_ok.add((pas, "B"))
            wq_issue(wq_pos[0] + 3)

            S.op("pool", lambda e: e.memset(vout[:], 0.0), writes=[b_vout])
            for h in range(4):
                if h % 2 == 0:
                    rq = wring_load(win_v, (h // 2) * 256)
                    rf = wring_load(win_v, 512 + (h // 2) * 256)
                for gi, grp in enumerate(groups):
                    c0, n, tl, is_s = grp
                    bq = next_bank()
                    fm_matmul(rq, h % 2, grp, bq)
                    zi_ = rot("zq")
                    S.op("act", lambda e, bq=bq, zi_=zi_, n=n: e.activation(out=zq[zi_][:, 0:n], in_=ps[bq][:, 0:n], func=AF.Copy),
                         reads=[b_ps[bq]], writes=[b_zq[zi_]])
                    bf_ = next_bank()
                    fm_matmul(rf, h % 2, grp, bf_)
                    if gi == len(groups) - 1 and h % 2 == 1:
                        wq_done()
                    ti = rot("th")
                    S.op("act", lambda e, bf_=bf_, ti=ti, n=n: e.activation(out=thb[ti][:, 0:n], in_=ps[bf_][:, 0:n], func=AF.Tanh, scale=0.5),
                         reads=[b_ps[bf_]], writes=[b_th[ti]])
                    S.op("act", lambda e, ti=ti, n=n, h=h: e.activation(out=fb[ti][:, 0:n], in_=thb[ti][:, 0:n], func=AF.Identity,
                                                                       scale=AB[:, 4 + h:5 + h], bias=AB[:, h:h + 1]),
                         reads=[b_th[ti], b_AB], writes=[b_fb[ti]])
                    S.op("pool", lambda e, ti=ti, n=n: e.tensor_scalar(out=kb[ti][:, 0:n], in0=fb[ti][:, 0:n], scalar1=-1.0,
                                                                      scalar2=1.0, op0=ALU.mult, op1=ALU.add),
                         reads=[b_fb[ti]], writes=[b_kb[ti]])
                    csz = 8 if is_s else 64
                    nch = n // csz
                    S.op("act", lambda e, ti=ti, csz=csz, nch=nch: e.activation(out=bc(vout[:, 0:1], [[csz, nch]]), in_=bc(fb[ti][:, 0:1], [[csz, nch]]), func=AF.Copy),
                         reads=[b_fb[ti]], writes=[b_vout])
                    S.op("dve", lambda e, ti=ti, n=n: e.tensor_tensor_scan(out=Pb[ti][:, 0:n], data0=fb[ti][:, 0:n], data1=vout[:, 0:n],
                                                                          initial=1.0, op0=ALU.mult, op1=ALU.max), reads=[b_fb[ti], b_vout], writes=[b_Pb[ti]])
                    if is_s:
                        S.op("pool", lambda e: e.memset(vout[:, 0:128], 0.0), writes=[b_vout])
                    pl0 = 16 if is_s else gi * 8
                    S.op("pool", lambda e, ti=ti, csz=csz, nch=nch, pl0=pl0, h=h: e.tensor_copy(
                        out=PL[:, pl0:pl0 + nch, h], in_=bc(Pb[ti][:, csz - 1:csz], [[csz, nch]])), reads=[b_Pb[ti]], writes=[b_PL])
                    S.op("dve", lambda e, ti=ti, n=n: e.reciprocal(out=Pinv[ti][:, 0:n], in_=Pb[ti][:, 0:n]), reads=[b_Pb[ti]], writes=[b_Pi[ti]])
                    S.op("pool", lambda e, ti=ti, zi_=zi_, n=n, c0=c0, h=h: e.tensor_tensor(out=qdT[h][:, c0:c0 + n], in0=zq[zi_][:, 0:n],
                                                                                          in1=Pb[ti][:, 0:n], op=ALU.mult),
                         reads=[b_zq[zi_], b_Pb[ti]], writes=[b_qd[h]])
                    S.op("pool", lambda e, ti=ti, n=n, c0=c0, h=h: e.tensor_tensor(out=kdT[h][:, c0:c0 + n], in0=kb[ti][:, 0:n],
                                                                                  in1=Pinv[ti][:, 0:n], op=ALU.mult),
                         reads=[b_kb[ti], b_Pi[ti]], writes=[b_kd[h]])

            for h in range(4):
                if h % 2 == 0:
                    rg = wring_load(win_v, 1536 + (h // 2) * 256)
                for gi, grp in enumerate(groups):
                    c0, n, tl, is_s = grp
                    bg = next_bank()
                    fm_matmul(rg, h % 2, grp, bg)
                    if gi == len(groups) - 1 and h % 2 == 1:
                        wq_done()
                    ti = rot("th")
                    S.op("act", lambda e, bg=bg, ti=ti, n=n: e.activation(out=thb[ti][:, 0:n], in_=ps[bg][:, 0:n], func=AF.Tanh, scale=0.5),
                         reads=[b_ps[bg]], writes=[b_th[ti]])
                    S.op("dve", lambda e, bg=bg, ti=ti, n=n, h=h, c0=c0: e.scalar_tensor_tensor(
                        out=sg2[:, h * TB + c0:h * TB + c0 + n], in0=thb[ti][:, 0:n], scalar=1.0, in1=ps[bg][:, 0:n], op0=ALU.add, op1=ALU.mult),
                        reads=[b_th[ti], b_ps[bg]], writes=[b_sg])

            tstate = {}

            def front(li):
                is_s = sflags[li]
                c0 = li * 128
                mask = Ms if is_s else M64
                for h in range(4):
                    S.op("pe", lambda e, h=h, c0=c0: e.matmul(ps[2][:, h * 128:(h + 1) * 128], lhsT=kdT[h][:, c0:c0 + 128], rhs=qdT[h][:, c0:c0 + 128],
                                                            start=True, stop=True), reads=[b_kd[h], b_qd[h]], writes=[b_ps[2]], signal=(h == 3))
                si = rot("scm")
                S.op("dve", lambda e, si=si, mask=mask: e.tensor_tensor(out=v4(scm_[si]), in0=ps[2][:, :].rearrange("p (h t) -> p h t", h=4),
                                                                       in1=bc(mask[:, :], [[0, 4], [1, 128]]), op=ALU.mult),
                     reads=[b_ps[2], b_c], writes=[b_scm[si]])
                ke_i = rot("kt")
                if not is_s:
                    S.op("pool", lambda e, ke_i=ke_i, c0=c0, li=li: e.tensor_tensor(
                        out=ket[ke_i].rearrange("p (h c i) -> p h c i", h=4, i=64), in0=bc(kd_all[:, c0:c0 + 1], [[TB, 4], [64, 2], [1, 64]]),
                        in1=bc(PL[:, li * 2, 0:1], [[1, 4], [4, 2], [0, 64]]), op=ALU.mult), reads=b_kd + [b_PL], writes=[b_ket[ke_i]])
                else:
                    S.op("pool", lambda e, ke_i=ke_i, c0=c0: e.tensor_tensor(
                        out=ket[ke_i].rearrange("p (h c i) -> p h c i", h=4, i=8), in0=bc(kd_all[:, c0:c0 + 1], [[TB, 4], [8, 16], [1, 8]]),
                        in1=bc(PL[:, 16, 0:1], [[1, 4], [4, 16], [0, 8]]), op=ALU.mult), reads=b_kd + [b_PL], writes=[b_ket[ke_i]])
                tstate[li] = {"si": si, "ki": ke_i, "ob": 4 + (li % 2)}

            def frontB(li):
                ke_i = tstate[li]["ki"]
                psk = ps[3].bitcast(BF16)
                for h in range(4):
                    S.op("pe", lambda e, h=h, ke_i=ke_i, psk=psk: e.transpose(out=psk[:, h * 128:(h + 1) * 128], in_=ket[ke_i][:, h * 128:(h + 1) * 128],
                                                                            identity=identb[:]),
                         reads=[b_ket[ke_i], b_c], writes=[b_ps[3]], signal=(h == 3))
                S.op("act", lambda e, ki=ke_i, psk=psk: e.activation(out=ktok_[ki], in_=psk[:, 0:512], func=AF.Copy),
                     reads=[b_ps[3]], writes=[b_ktok[ke_i]])

            def ds_and_update(li, c, nxt):
                ki = tstate[li]["ki"]
                chunk = (li * 2 + c)
                S.op("dve", lambda e, chunk=chunk: e.tensor_tensor(out=Stmp[:], in0=Sst[:], in1=bc(PL[:, chunk, 0:1], [[1, 4], [0, 128]]), op=ALU.mult),
                     reads=[b_S, b_PL], writes=[b_Stmp])
                for h in range(4):
                    S.op("pe", lambda e, h=h, c=c, li=li, ki=ki: e.matmul(
                        ps[6][:, h * 128:(h + 1) * 128], lhsT=ktok_[ki][c * 64:(c + 1) * 64, h * 128:(h + 1) * 128],
                        rhs=vtok[c * 64:(c + 1) * 64, li * 512 + h * 128:li * 512 + (h + 1) * 128], start=True, stop=True),
                        reads=[b_ktok[ki], b_vtok[li]], writes=[b_ps[6]], signal=(h == 3))
                S.op("dve", lambda e, nxt=nxt: e.tensor_tensor(out=Sbf[:, nxt], in0=Stmp[:], in1=ps[6][:, :].rearrange("p (h v) -> p h v", h=4), op=ALU.add),
                     reads=[b_Stmp, b_ps[6]], writes=[b_Sbf[nxt]])
                S.op("dve", lambda e: e.tensor_tensor(out=Sst[:], in0=Stmp[:], in1=ps[6][:, :].rearrange("p (h v) -> p h v", h=4), op=ALU.add),
                     reads=[b_Stmp, b_ps[6]], writes=[b_S])

            def intra(li, h, stop):
                st = tstate[li]
                si, ob = st["si"], st["ob"]
                S.op("pe", lambda e, h=h, li=li, si=si, ob=ob, stop=stop: e.matmul(
                    ps[ob][:, h * 128:(h + 1) * 128], lhsT=vtok[:, li * 512 + h * 128:li * 512 + (h + 1) * 128],
                    rhs=scm_[si][:, h * 128:(h + 1) * 128], start=True, stop=stop, skip_group_check=True),
                    reads=[b_vtok[li], b_scm[si]], writes=[b_ps[ob]], signal=stop)

            def chainA(li):
                if not sflags[li]:
                    ds_and_update(li, 0, 1)

            def chainB(li):
                st = tstate[li]
                si, ki, ob = st["si"], st["ki"], st["ob"]
                c0 = li * 128
                if not sflags[li]:
                    for h in range(4):
                        intra(li, h, False)
                        for c in range(2):
                            S.op("pe", lambda e, h=h, c=c, c0=c0, ob=ob: e.matmul(
                                ps[ob][:, h * 128 + c * 64:h * 128 + (c + 1) * 64], lhsT=Sbf[:, c, h, :],
                                rhs=qdT[h][:, c0 + c * 64:c0 + (c + 1) * 64], start=False, stop=True, skip_group_check=True),
                                reads=[b_Sbf[c], b_qd[h]], writes=[b_ps[ob]], signal=(h == 3 and c == 1))
                    ds_and_update(li, 1, 0)
                    if pas == 2 and li == 3:
                        S.dma("sp", lambda e: e.dma_start(out=o_sp.rearrange("h k v -> k h v"), in_=Sst[:]), reads=[b_S], final=True)
                    st["o_src"], st["o_buf"] = ps[ob][:, :], b_ps[ob]
                else:
                    for h in range(4):
                        intra(li, h, True)
                    for j in range(NSAMP):
                        r = j % 2
                        sl = j % 4
                        db = 6 if r == 0 else 3
                        S.op("act", lambda e, r=r, sl=sl: e.activation(out=s0bf[:, r].rearrange("p h v -> p (h v)"), in_=s0slot[sl],
                                                                      func=AF.Copy), reads=[b_s0[sl]], writes=[b_s0bf[r]])
                        for h in range(4):
                            S.op("pe", lambda e, h=h, j=j, r=r, c0=c0: e.matmul(
                                ps[2][:, h * 128 + j * 8:h * 128 + (j + 1) * 8], lhsT=s0bf[:, r, h, :], rhs=qdT[h][:, c0 + j * 8:c0 + (j + 1) * 8],
                                start=True, stop=True, skip_group_check=True), reads=[b_s0bf[r], b_qd[h]], writes=[b_ps[2]],
                                signal=(h == 3))
                        S.op("dve", lambda e, j=j, r=r, ki=ki: e.tensor_scalar(out=ktm_[r], in0=ktok_[ki],
                                                                              scalar1=Esel[:, j:j + 1], scalar2=None, op0=ALU.mult),
                             reads=[b_ktok[ki], b_c], writes=[b_ktm[r]])
                        for h in range(4):
                            S.op("pe", lambda e, h=h, r=r, li=li, db=db: e.matmul(ps[db][:, h * 128:(h + 1) * 128], lhsT=ktm_[r][:, h * 128:(h + 1) * 128],
                                                                                 rhs=vtok[:, li * 512 + h * 128:li * 512 + (h + 1) * 128], start=True, stop=True),
                                 reads=[b_ktm[r], b_vtok[li]], writes=[b_ps[db]], signal=(h == 3))
                        S.op("dve", lambda e, j=j, r=r, sl=sl: e.tensor_tensor(out=snew[r], in0=s0slot[sl].rearrange("p (h v) -> p h v", h=4),
                                                                              in1=bc(PL[:, 16 + j, 0:1], [[1, 4], [0, 128]]), op=ALU.mult),
                             reads=[b_s0[sl], b_PL], writes=[b_snew[r]] + ([b_xsr[r]] if j < 2 else []))
                        if j + 4 < NSAMP:
                            s0_load(j + 4)
                        S.op("dve", lambda e, r=r, db=db: e.tensor_tensor(out=snew[r], in0=snew[r], in1=ps[db][:, :].rearrange("p (h v) -> p h v", h=4), op=ALU.add),
                             reads=[b_snew[r], b_ps[db]], writes=[b_snew[r]])
                        S.dma("sp", lambda e, j=j, r=r: e.dma_start(out=o_ss[j].rearrange("h k v -> k h v"), in_=snew[r]), reads=[b_snew[r]], final=True)
                    S.op("act", lambda e: e.activation(out=ftmp2[:, 0, :], in_=ps[2][:, :], func=AF.Copy), reads=[b_ps[2]], writes=[b_ft2[0]])
                    S.op("dve", lambda e, ob=ob: e.tensor_tensor(out=ftmp2[:, 0, :], in0=ps[ob][:, :], in1=ftmp2[:, 0, :], op=ALU.add),
                         reads=[b_ps[ob], b_ft2[0]], writes=[b_ft2[0]])
                    st["o_src"], st["o_buf"] = ftmp2[:, 0, :], b_ft2[0]

            def epi(li):
                st = tstate[li]
                o_src, o_buf = st["o_src"], st["o_buf"]
                c0 = li * 128
                e1 = rot("e")
                S.op("act", lambda e, o_src=o_src, e1=e1: e.activation(out=et1[e1][:, :], in_=o_src, func=AF.Square), reads=[o_buf], writes=[b_e1[e1]])
                for h in range(4):
                    S.op("pe", lambda e, e1=e1, h=h: e.matmul(ps[7][:, h:h + 1], lhsT=et1[e1][:, h * 128:(h + 1) * 128], rhs=ones[:, 0:1],
                                                            start=True, stop=True), reads=[b_c, b_e1[e1]], writes=[b_ps[7]], signal=(h == 3))
                S.op("dve", lambda e: e.tensor_scalar(out=st4[:], in0=ps[7][:, 0:4], scalar1=1.0 / 128, scalar2=EPS, op0=ALU.mult, op1=ALU.add),
                     reads=[b_ps[7]], writes=[b_st4])
                S.op("pool", lambda e: e.tensor_tensor(out=st4[:], in0=st4[:], in1=bc(nhalf[:, 0:1], [[0, 4]]), op=ALU.pow),
                     reads=[b_st4, b_c], writes=[b_st4])
                S.op("pool", lambda e, e1=e1: e.tensor_tensor(out=et2[e1][:, :].rearrange("p (h t) -> p h t", h=4), in0=bc(ident[:, :], [[0, 4], [1, 128]]),
                                                              in1=bc(st4[:, 0:4], [[1, 4], [0, 128]]), op=ALU.mult), reads=[b_st4, b_c], writes=[b_e2[e1]])
                st["e1"] = e1

            def epiB(li):
                st = tstate[li]
                o_src, o_buf = st["o_src"], st["o_buf"]
                c0 = li * 128
                e1 = st["e1"]
                S.op("pe", lambda e, e1=e1: e.matmul(ps[7][:, :], lhsT=ones[:], rhs=et2[e1][:, :], start=True, stop=True),
                     reads=[b_c, b_e2[e1]], writes=[b_ps[7]])
                S.op("act", lambda e, e1=e1: e.activation(out=et2[e1][:, :], in_=ps[7][:, :], func=AF.Copy), reads=[b_ps[7]], writes=[b_e2[e1]])
                S.op("dve", lambda e, e1=e1, o_src=o_src: e.tensor_tensor(out=et1[e1][:, :], in0=o_src, in1=et2[e1][:, :], op=ALU.mult),
                     reads=[o_buf, b_e2[e1]], writes=[b_e1[e1]])
                S.op("dve", lambda e, e1=e1, c0=c0: e.scalar_tensor_tensor(
                    out=oaT[:, 0:4, c0:c0 + 128], in0=et1[e1][:, :].rearrange("p (h t) -> p h t", h=4), scalar=gnh[:, 0:1],
                    in1=bc(sg2[:, c0:c0 + 1], [[TB, 4], [1, 128]]), op0=ALU.mult, op1=ALU.mult),
                    reads=[b_e1[e1], b_sg, b_AB], writes=[b_oa[li]])


            for h in range(4):
                if h % 2 == 0:
                    ru = wring_load(win_v, 2048 + (h // 2) * 256)
                for gi, grp in enumerate(groups):
                    c0, n, tl, is_s = grp
                    bu = next_bank()
                    fm_matmul(ru, h % 2, grp, bu)
                    if gi == len(groups) - 1 and h % 2 == 1:
                        wq_done()
                    S.op("act", lambda e, bu=bu, n=n, h=h, c0=c0: e.activation(out=uu[:, h * TB + c0:h * TB + c0 + n], in_=ps[bu][:, 0:n],
                                                                              func=AF.Gelu_apprx_tanh), reads=[b_ps[bu]], writes=[b_uu] + wd_tail + b_ada)
            lnst = {}
            gvr = [zq[0], zq[1], Pinv[0], kb[0], Pb[0]]
            b_gvr = [b_zq[0], b_zq[1], b_Pi[0], b_kb[0], b_Pb[0]]
            lnc = [0]

            def ln_front(li):
                bank = 0
                for kc in range(8):
                    S.op("pe", lambda e, kc=kc, li=li, bank=bank: e.matmul(ps[bank][:, :], lhsT=aT[:, kc, li * 128:(li + 1) * 128],
                                                                         rhs=wz_v[:, kc * 512:(kc + 1) * 512], start=(kc == 0), stop=(kc == 7)),
                         reads=[b_aT[li], b_wzv], writes=[b_ps[bank]], signal=(kc == 7))
                g_ = lnc[0] % 5
                lnc[0] += 1
                lnst[li] = g_
                S.op("act", lambda e, g_=g_, bank=bank: e.activation(out=gvr[g_][:, :], in_=ps[bank][:, :], func=AF.Gelu_apprx_tanh),
                     reads=[b_ps[bank]], writes=[b_gvr[g_]])
                S.op("dve", lambda e, g_=g_: e.bn_stats(out=bnst[:, g_, :], in_=gvr[g_][:, :]), reads=[b_gvr[g_]], writes=[b_bn[g_]])
                S.op("dve", lambda e, g_=g_: e.bn_aggr(out=bnag[:, g_, 0:2], in_=bnst[:, g_, :]), reads=[b_bn[g_]], writes=[b_bn[g_]])
                S.op("dve", lambda e, g_=g_: e.tensor_scalar(out=bnag[:, g_, 2:3], in0=bnag[:, g_, 1:2], scalar1=EPS, scalar2=None, op0=ALU.add),
                     reads=[b_bn[g_]], writes=[b_bn[g_]])
                S.op("pool", lambda e, g_=g_: e.tensor_tensor(out=bnag[:, g_, 3:4], in0=bnag[:, g_, 2:3], in1=nhalf[:], op=ALU.pow),
                     reads=[b_bn[g_], b_c], writes=[b_bn[g_]])

            def ln_back(li):
                g_ = lnst[li]
                S.op("dve", lambda e, g_=g_: e.scalar_tensor_tensor(out=gvr[g_][:, :], in0=gvr[g_][:, :], scalar=bnag[:, g_, 0:1], in1=lnbc[:, 0, :],
                                                                    op0=ALU.subtract, op1=ALU.mult), reads=[b_gvr[g_], b_bn[g_], b_Gbc], writes=[b_gvr[g_]])
                is_out = (pas == 2 and li in (3, 4))
                if is_out:
                    S.op("dve", lambda e, g_=g_: e.scalar_tensor_tensor(out=vout[:], in0=gvr[g_][:, :], scalar=bnag[:, g_, 3:4], in1=lnbc[:, 1, :],
                                                                        op0=ALU.mult, op1=ALU.add), reads=[b_gvr[g_], b_bn[g_], b_Gbc], writes=[b_vout])
                    dst = o_vp if li == 3 else o_vs
                    S.dma("sp", lambda e, dst=dst: e.dma_start(out=dst, in_=vout[:]), reads=[b_vout], final=True)
                    S.op("act", lambda e, li=li: e.activation(out=vln[:, li * 512:(li + 1) * 512], in_=vout[:], func=AF.Copy),
                         reads=[b_vout], writes=[b_vln[li]] + wd_tail + b_ada)
                else:
                    S.op("dve", lambda e, g_=g_, li=li: e.scalar_tensor_tensor(out=vln[:, li * 512:(li + 1) * 512], in0=gvr[g_][:, :], scalar=bnag[:, g_, 3:4],
                                                                              in1=lnbc[:, 1, :], op0=ALU.mult, op1=ALU.add),
                         reads=[b_gvr[g_], b_bn[g_], b_Gbc], writes=[b_vln[li]] + wd_tail + b_ada)
                wsel = 1 if sflags[li] else 0
                cb = 1
                for h in range(4):
                    S.op("pe", lambda e, h=h, li=li, wsel=wsel, cb=cb: e.matmul(ps[cb][:, h * 128:(h + 1) * 128], lhsT=vln[:, li * 512 + h * 128:li * 512 + (h + 1) * 128],
                                                                               rhs=wsT[:, wsel, h, :], start=True, stop=True),
                         reads=[b_vln[li], b_ws], writes=[b_ps[cb]], signal=(h == 3))

            def ln_tail(li):
                c0 = li * 128
                wsel = 1 if sflags[li] else 0
                cb = 1
                g_ = lnst[li]
                S.op("dve", lambda e, g_=g_, wsel=wsel, cb=cb: e.tensor_tensor(out=gvr[g_][:, :], in0=ps[cb][:, :], in1=bsbc[:, wsel].rearrange("p h t -> p (h t)"), op=ALU.add),
                     reads=[b_ps[cb], b_Gbc], writes=[b_gvr[g_]])
                S.op("pool", lambda e, g_=g_, c0=c0: e.tensor_tensor(out=aT[:, 4:8, c0:c0 + 128], in0=gvr[g_][:, :].rearrange("p (h t) -> p h t", h=4),
                                                                    in1=bc(uu[:, c0:c0 + 1], [[TB, 4], [1, 128]]), op=ALU.mult),
                     reads=[b_gvr[g_], b_uu], writes=[b_aT[li]])

            LA = 4
            nt_ = len(tiles)
            if pas == 2:
                for j in range(4):
                    s0_load(j)
            front(tiles[0])
            frontB(tiles[0])
            for idx, li in enumerate(tiles):
                chainA(li)
                if idx >= 2:
                    epiB(tiles[idx - 2])
                if idx + 1 < nt_:
                    front(tiles[idx + 1])
                if idx >= 1:
                    epi(tiles[idx - 1])
                if idx + 1 < nt_:
                    frontB(tiles[idx + 1])
                chainB(li)
            if nt_ >= 2:
                epiB(tiles[nt_ - 2])
            epi(tiles[-1])
            epiB(tiles[-1])


            def post_norm_res(li, banks, Gbc_t, is_s):
                col = 48 + (po_i[0] % 8) * 2
                po_i[0] += 1
                for half in range(2):
                    S.op("act", lambda e, half=half, col=col: e.activation(out=junk[:, half * 512:(half + 1) * 512], in_=ps[banks[half]][:, :], func=AF.Square,
                                                                          accum_out=ssq[:, col + half:col + half + 1]),
                         reads=[b_ps[banks[half]]], writes=[b_st[col], b_junk])
                S.op("dve", lambda e, col=col: e.tensor_tensor(out=rstd[:, col:col + 1], in0=ssq[:, col:col + 1], in1=ssq[:, col + 1:col + 2], op=ALU.add),
                     reads=[b_st[col]], writes=[b_st[col]])
                S.op("dve", lambda e, col=col: e.tensor_scalar(out=rstd[:, col:col + 1], in0=rstd[:, col:col + 1], scalar1=1.0 / D, scalar2=EPS,
                                                               op0=ALU.mult, op1=ALU.add), reads=[b_st[col]], writes=[b_st[col]])
                S.op("pool", lambda e, col=col: e.tensor_tensor(out=rstd[:, col:col + 1], in0=rstd[:, col:col + 1], in1=nhalf[:], op=ALU.pow),
                     reads=[b_st[col], b_c], writes=[b_st[col]])
                smp = 1 if is_s else 0
                for half in range(2):
                    k = rot("ft")
                    S.op("dve", lambda e, half=half, k=k, col=col, smp=smp: e.scalar_tensor_tensor(
                        out=ftmp[:, k, :], in0=ps[banks[half]][:, :], scalar=rstd[:, col:col + 1], in1=Gbc_t[:, smp, half * 512:(half + 1) * 512],
                        op0=ALU.mult, op1=ALU.mult), reads=[b_ps[banks[half]], b_st[col], b_Gbc], writes=[b_ft[k]])
                    S.op("pool" if half == 0 else "dve", lambda e, half=half, k=k, li=li: e.tensor_tensor(out=xres[:, li, half * 512:(half + 1) * 512], in0=ftmp[:, k, :],
                                                                                 in1=xres[:, li, half * 512:(half + 1) * 512], op=ALU.add),
                         reads=[b_ft[k], b_x[li]], writes=[b_x[li]])

            dbank = [0]

            def stageD_tile(li):
                banks = [2 + (dbank[0] % 4), 2 + ((dbank[0] + 1) % 4)]
                dbank[0] += 2
                for half in range(2):
                    for kc in range(8):
                        S.op("pe", lambda e, kc=kc, li=li, half=half, bank=banks[half]: e.matmul(
                            ps[bank][:, :], lhsT=(oaT if kc < 4 else aT)[:, kc, li * 128:(li + 1) * 128],
                            rhs=wo_sb[:, kc * D + half * 512:kc * D + (half + 1) * 512],
                            start=(kc == 0), stop=(kc == 7)), reads=[b_aT[li], b_oa[li], b_wo], writes=[b_ps[banks[half]]], signal=(kc == 7))
                post_norm_res(li, banks, G1bc, sflags[li])

            for i in range(min(LA, len(tiles))):
                ln_front(tiles[i])
            for idx, li in enumerate(tiles):
                if idx >= 1:
                    ln_tail(tiles[idx - 1])
                if idx + LA < len(tiles):
                    ln_front(tiles[idx + LA])
                ln_back(li)
                if idx >= 2:
                    stageD_tile(tiles[idx - 2])
            ln_tail(tiles[-1])
            if len(tiles) >= 2:
                stageD_tile(tiles[-2])
            stageD_tile(tiles[-1])

            if debug and pas == 2:
                for li in tiles:
                    S.dma("sp", lambda e, li=li: e.dma_start(out=d_x1[li * 128:(li + 1) * 128, :], in_=xres[:, li, :]), reads=[b_x[li]], final=True)
                S.dma("sp", lambda e: e.dma_start(out=d_oa, in_=oaT[:]), reads=b_oa, final=True)
                S.dma("sp", lambda e: e.dma_start(out=d_ob, in_=aT[:, 4:8, :]), reads=b_aT, final=True)
            pre_norm(tiles, sflags, Gpf, 24)
            xsr_ok.add((pas, "F"))
            wq_issue(wq_pos[0] + 3)
            if debug and pas == 2:
                S.dma("sp", lambda e: e.dma_start(out=d_h2, in_=aT[:]), reads=b_aT, final=True)

            first_act = [True]
            for j in range(NJ):
                if j % 2 == 0:
                    rg = wring_load(wg_v, (j // 2) * 256)
                    ru = wring_load(wu_v, (j // 2) * 256)
                for gi, grp in enumerate(groups):
                    c0, n, tl, is_s = grp
                    bg = 2 + (mmring[0] % 2) * 2
                    mmring[0] += 1
                    bu = bg + 1
                    fm_matmul(rg, j % 2, grp, bg)
                    fm_matmul(ru, j % 2, grp, bu)
                    if gi == len(groups) - 1:
                        if j % 2 == 1:
                            wq_done()
                        if j >= 4:
                            wd_load(j, ([b_wzv, b_wo, b_uu] + b_vln + b_ada) if j == 4 else [])
                    k = rot("ft")
                    S.op("act", lambda e, bg=bg, k=k, n=n: e.activation(out=ftmp[:, k, 0:n], in_=ps[bg][:, 0:n], func=AF.Tanh, scale=0.5),
                         reads=[b_ps[bg]], writes=[b_ft[k]])
                    S.op("dve", lambda e, bg=bg, k=k, n=n: e.scalar_tensor_tensor(out=ftmp2[:, k, 0:n], in0=ftmp[:, k, 0:n], scalar=1.0, in1=ps[bg][:, 0:n],
                                                                                 op0=ALU.add, op1=ALU.mult), reads=[b_ft[k], b_ps[bg]], writes=[b_ft2[k]])
                    wl = [b_act[j][gi]]
                    if first_act[0]:
                        first_act[0] = False
                        wl = wl + mix_bufs
                    S.op("dve", lambda e, bu=bu, k=k, n=n, j=j, c0=c0: e.scalar_tensor_tensor(out=actT(j, c0, n), in0=ftmp2[:, k, 0:n], scalar=0.5, in1=ps[bu][:, 0:n],
                                                                                              op0=ALU.mult, op1=ALU.mult), reads=[b_ft2[k], b_ps[bu]], writes=wl)

            if pas < 2:
                S.dma("pool", lambda e: e.dma_start(out=wz_i.rearrange("p (k n) -> p k n", k=8), in_=win_v[:, :, 1024:1536]), writes=[b_wzi] + wzi_guard)
            for li in tiles:
                gi = li // 4
                banks = [next_bank(6), next_bank(6)]
                for half in range(2):
                    for j in range(NJ):
                        S.op("pe", lambda e, j=j, li=li, half=half, bank=banks[half]: e.matmul(
                            ps[bank][:, :], lhsT=actT(j, li * 128, 128), rhs=wd_sb(j)[:, half * 512:(half + 1) * 512],
                            start=(j == 0), stop=(j == NJ - 1)), reads=[b_act[j][gi], b_wd[j]], writes=[b_ps[banks[half]]], signal=(j == NJ - 1))
                post_norm_res(li, banks, G2bc, sflags[li])
                dst = ys if sflags[li] else yp_t[gtile0 + li]
                S.dma("sp", lambda e, li=li, dst=dst: e.dma_start(out=dst, in_=xres[:, li, :]), reads=[b_x[li]], final=True)

        S.emit()
    return nc


_NC_CACHE = {}
_IN_MAPS_ONLY = False


def kernel(x_prompt, x_sample, state_hgrn, c_prompt, c_sample, lb_logits, w_ada, b_ada, norm_pre_mix, norm_post_mix,
           w_in, gnorm_w, ln_v_g, ln_v_b, w_spatial, b_spatial, w_out, norm_pre_ffn, norm_post_ffn, w_gate, w_up, w_down):
    f = lambda a: np.ascontiguousarray(np.asarray(a, dtype=np.float32))
    x_prompt, x_sample, state_hgrn, c_prompt, c_sample = map(f, (x_prompt, x_sample, state_hgrn, c_prompt, c_sample))
    vec = np.zeros((128, 128), np.float32)
    vec[R_NPRE:R_NPRE + 8] = f(norm_pre_mix)[0].reshape(8, 128)
    vec[R_NPOST:R_NPOST + 8] = f(norm_post_mix)[0].reshape(8, 128)
    vec[R_FPRE:R_FPRE + 8] = f(norm_pre_ffn)[0].reshape(8, 128)
    vec[R_FPOST:R_FPOST + 8] = f(norm_post_ffn)[0].reshape(8, 128)
    vec[R_BADA:R_BADA + 48] = f(b_ada)[0].reshape(48, 128)
    vec[R_LB0:R_LB0 + 4] = f(lb_logits)[0].reshape(4, 128)
    vec[R_LB1:R_LB1 + 4] = f(lb_logits)[1].reshape(4, 128)
    vec[R_GN] = f(gnorm_w)[0]
    vec[R_LNG:R_LNG + 4] = f(ln_v_g)[0].reshape(4, 128)
    vec[R_LNB:R_LNB + 4] = f(ln_v_b)[0].reshape(4, 128)
    vec[R_BS:R_BS + 4] = f(b_spatial)[0]
    shared = {"vecs": vec, "wspat": f(w_spatial)[0], "w_ada": f(w_ada)[0], "w_in": f(w_in)[0], "w_out": f(w_out)[0],
              "w_gate": f(w_gate)[0], "w_up": f(w_up)[0], "w_down": f(w_down)[0]}
    in_maps = []
    for c in range(NCORES):
        m = dict(shared)
        m["xp"] = x_prompt[c]
        m["xs"] = x_sample[c * NSAMP:(c + 1) * NSAMP].reshape(128, D)
        m["s0"] = state_hgrn[0, c * NSAMP:(c + 1) * NSAMP]
        m["call"] = np.concatenate([c_prompt[c:c + 1], c_sample[c * NSAMP:(c + 1) * NSAMP]], axis=0)
        in_maps.append(m)
    if _IN_MAPS_ONLY:
        return in_maps
    if "nc" not in _NC_CACHE:
        _NC_CACHE["nc"] = build_nc()
    nc = _NC_CACHE["nc"]
    res = run_bass_kernel_spmd(nc, in_maps, core_ids=list(range(NCORES)))
    R = res.results
    y_p = np.stack([R[c]["yp"] for c in range(NCORES)], 0)
    y_s = np.concatenate([R[c]["ys"].reshape(NSAMP, 8, D) for c in range(NCORES)], 0)
    s_p = np.stack([R[c]["o_sp"] for c in range(NCORES)], 0)[None]
    s_s = np.concatenate([R[c]["o_ss"] for c in range(NCORES)], 0)[None]
    v_p = np.stack([R[c]["o_vp"].reshape(128, 4, 128) for c in range(NCORES)], 0)[None]
    v_s = np.concatenate([R[c]["o_vs"].reshape(NSAMP, 8, 4, 128) for c in range(NCORES)], 0)[None]
    return (y_p.astype(np.float32), y_s.astype(np.float32), s_p.astype(np.float32), s_s.astype(np.float32),
            v_p.astype(np.float32), v_s.astype(np.float32))
```

```python
import numpy as np
from contextlib import ExitStack
import concourse.bass as bass
import concourse.mybir as mybir
from concourse.bass_utils import run_bass_kernel_spmd

F32 = mybir.dt.float32
BF16 = mybir.dt.bfloat16
AF = mybir.ActivationFunctionType
ALU = mybir.AluOpType

D = 1024
NCORES = 8
SEQ = 2048
NSAMP = 16
DFF = 2816
NJ = DFF // 128
EPS = 1e-6
TBMAX = 768
NTL = 6


class Buf:
    __slots__ = ("name", "lw", "rd")

    def __init__(self, name):
        self.name = name
        self.lw = None
        self.rd = []


class Sched:
    ENGS = ("pe", "act", "dve", "pool", "sp")

    def __init__(self, nc, es, n_dma_sems=32):
        self.nc = nc
        self.lists = {e: [] for e in self.ENGS}
        self.sems = {}
        for e in ("pe", "act", "dve", "pool"):
            self.sems[e] = es.enter_context(nc.semaphore("s_" + e))
        self.cnt = {e: 0 for e in ("pe", "act", "dve", "pool")}
        self.dsems = [es.enter_context(nc.semaphore("s_dma%d" % i)) for i in range(n_dma_sems)]
        self.dcnt = [0] * n_dma_sems
        self.dnext = 0
        self.dnext_pool = 0
        self.waited = {e: {} for e in self.ENGS}
        self.final_waits = []

    def _sem(self, key):
        return self.sems[key] if isinstance(key, str) else self.dsems[key]

    def _need(self, eng, tk):
        if tk is None:
            return
        key, val = tk
        if key == eng:
            if eng == "pe":
                return
            if val > self.cnt[eng]:
                return
        w = self.waited[eng]
        if w.get(key, 0) >= val:
            return
        w[key] = val
        sem = self._sem(key)
        self.lists[eng].append(lambda e, sem=sem, val=val: e.wait_ge(sem, val))

    def _deps(self, eng, reads, writes):
        for b in reads:
            self._need(eng, b.lw)
        for b in writes:
            self._need(eng, b.lw)
            for t in b.rd:
                self._need(eng, t)

    def op(self, eng, fn, reads=(), writes=(), signal=True):
        self._deps(eng, reads, writes)
        if signal:
            self.cnt[eng] += 1
            tk = (eng, self.cnt[eng])
            sem = self.sems[eng]
            self.lists[eng].append(lambda e, fn=fn, sem=sem: fn(e).then_inc(sem, 1))
        else:
            tk = (eng, self.cnt[eng] + 1)
            self.lists[eng].append(lambda e, fn=fn: fn(e))
        for b in reads:
            b.rd.append(tk)
        for b in writes:
            b.lw = tk
            b.rd = []
        return tk

    def dma(self, eng, fn, reads=(), writes=(), final=False):
        self._deps(eng, reads, writes)
        half = len(self.dsems) // 2
        if eng == "pool":
            k = half + self.dnext_pool
            self.dnext_pool = (self.dnext_pool + 1) % (len(self.dsems) - half)
        else:
            k = self.dnext
            self.dnext = (self.dnext + 1) % half
        if self.dcnt[k] > 0:
            self._need(eng, (k, self.dcnt[k]))
        self.dcnt[k] += 16
        tk = (k, self.dcnt[k])
        sem = self.dsems[k]
        self.lists[eng].append(lambda e, fn=fn, sem=sem: fn(e).then_inc(sem, 16))
        for b in reads:
            b.rd.append(tk)
        for b in writes:
            b.lw = tk
            b.rd = []
        if final:
            self.final_waits.append(tk)
        return tk

    def emit(self):
        for tk in self.final_waits:
            self._need("sp", tk)
        nc = self.nc
        lists = self.lists
        with nc.Block() as block:
            @block.tensor
            def _(e):
                for f in lists["pe"]:
                    f(e)

            @block.scalar
            def _(e):
                for f in lists["act"]:
                    f(e)

            @block.vector
            def _(e):
                for f in lists["dve"]:
                    f(e)

            @block.gpsimd
            def _(e):
                for f in lists["pool"]:
                    f(e)

            @block.sync
            def _(e):
                for f in lists["sp"]:
                    f(e)


def bc(ap, dims):
    return bass.AP(ap.tensor, ap.offset, [ap.ap[0]] + [list(d) for d in dims])


R_NPRE, R_NPOST, R_FPRE, R_FPOST, R_BADA, R_LB0, R_LB1, R_GN, R_LNG, R_LNB, R_BS = 0, 8, 16, 24, 32, 80, 84, 88, 89, 93, 97


def build_nc(debug=False):
    nc = bass.Bass("TRN2", target_bir_lowering=False)
    din = lambda n, s: nc.dram_tensor(n, s, F32, kind="ExternalInput").ap()
    dout = lambda n, s: nc.dram_tensor(n, s, F32, kind="ExternalOutput").ap()
    xp = din("xp", [SEQ, D]); xs = din("xs", [128, D]); s0 = din("s0", [NSAMP, 4, 128, 128])
    call = din("call", [17, D]); vecs = din("vecs", [128, 128]); wspat = din("wspat", [4, 128, 128])
    w_ada = din("w_ada", [D, 6 * D]); w_in = din("w_in", [D, 3072]); w_out = din("w_out", [D, D])
    w_gate = din("w_gate", [D, DFF]); w_up = din("w_up", [D, DFF]); w_down = din("w_down", [DFF, D])
    yp = dout("yp", [SEQ, D]); ys = dout("ys", [128, D]); o_sp = dout("o_sp", [4, 128, 128])
    o_ss = dout("o_ss", [NSAMP, 4, 128, 128]); o_vp = dout("o_vp", [128, 512]); o_vs = dout("o_vs", [128, 512])
    if debug:
        d_x1 = dout("d_x1", [NTL * 128, D])
        d_oa = nc.dram_tensor("d_oa", [128, 4, TBMAX], BF16, kind="ExternalOutput").ap()
        d_ob = nc.dram_tensor("d_ob", [128, 4, TBMAX], BF16, kind="ExternalOutput").ap()
        d_h2 = nc.dram_tensor("d_h2", [128, 8, TBMAX], BF16, kind="ExternalOutput").ap()

    with ExitStack() as es:
        S = Sched(nc, es)
        T = lambda name, shape, dt: es.enter_context(nc.sbuf_tensor(name, shape, dt))

        xres = T("xres", [128, NTL, D], F32);          b_x = [Buf("x%d" % i) for i in range(NTL)]
        aT = T("aT", [128, 8, TBMAX], BF16);           b_aT = [Buf("aT%d" % i) for i in range(NTL)]
        oaT = T("oaT", [128, 4, TBMAX], BF16);         b_oa = [Buf("oa%d" % i) for i in range(NTL)]
        ARENA = 23552
        arena = T("arena", [128, ARENA], BF16)
        b_act = [[Buf("act%d_%d" % (j, g)) for g in range(3)] for j in range(NJ)]
        warena = T("warena", [128, NJ * D], BF16)
        b_wd = [Buf("wd%d" % j) for j in range(NJ)]
        wd_tail = b_wd[16:NJ]
        wring = T("wring", [128, 4, 8, 128], BF16);    b_wring = [Buf("wr%d" % i) for i in range(4)]
        b_ada = [Buf("ada%d" % i) for i in range(2)]
        ident = T("ident", [128, 128], F32);           b_c = Buf("consts")
        identb = T("identb", [128, 128], BF16)
        ones = T("ones", [128, 128], F32)
        Mc = T("Mc", [128, 128], F32); M64 = T("M64", [128, 128], F32); Ms = T("Ms", [128, 128], F32)
        Esel = T("Esel", [128, 16], F32); E16 = T("E16", [16, 128], F32)
        nhalf = T("nhalf", [128, 1], F32)
        vecs_sb = T("vecs_sb", [128, 128], F32);       b_vecs = Buf("vecs")
        colv = T("colv", [128, 128], F32);             b_colv = Buf("colv")
        AB = T("AB", [128, 8], F32);                   b_AB = Buf("AB")
        nB = T("nB", [128, 4], F32)
        gnh = T("gnh", [128, 1], F32)
        st4 = T("st4", [128, 4], F32);                 b_st4 = Buf("st4")
        scT = T("scT", [128, 8, 17], BF16);            b_scT = Buf("scT")
        modT = T("modT", [128, 48, 17], F32);          b_mod = Buf("modT")
        Gpre = T("Gpre", [128, 8, 17], F32); Gpf = T("Gpf", [128, 8, 17], F32)
        G1f = T("G1f", [128, 8, 17], F32); G2f = T("G2f", [128, 8, 17], F32)
        b_G = Buf("Gs")
        Lm = T("Lm", [128, 2, 128], F32);              b_Lm = [Buf("Lm0"), Buf("Lm1")]
        G1bc = T("G1bc", [128, 2, D], F32); G2bc = T("G2bc", [128, 2, D], F32)
        b_Gbc = Buf("Gbc")
        lnbc = T("lnbc", [128, 2, 512], F32)
        bsbc = T("bsbc", [128, 2, 4, 128], F32)
        wsT = T("wsT", [128, 2, 4, 128], BF16);        b_ws = Buf("ws")
        Rep8 = T("Rep8", [8, 128], F32)
        Sst = T("Sst", [128, 4, 128], F32);            b_S = Buf("S")
        Sbf = T("Sbf", [128, 2, 4, 128], BF16);        b_Sbf = [Buf("Sbf0"), Buf("Sbf1")]
        Stmp = T("Stmp", [128, 4, 128], F32);          b_Stmp = Buf("Stmp")
        b_s0 = [Buf("s0_%d" % i) for i in range(4)]
        s0bf = T("s0bf", [128, 2, 4, 128], BF16);      b_s0bf = [Buf("s0bf0"), Buf("s0bf1")]
        b_snew = [Buf("snew0"), Buf("snew1")]
        PL = T("PL", [128, 40, 4], F32);               b_PL = Buf("PL")
        ssq = T("ssq", [128, 64], F32);                b_st = [Buf("st%d" % i) for i in range(64)]
        rstd = T("rstd", [128, 64], F32)
        junk = T("junk", [128, D], BF16);              b_junk = Buf("junk")
        xsr = T("xsr", [128, 2, D], F32);              b_xsr = [Buf("xsr0"), Buf("xsr1")]
        ftmp = T("ftmp", [128, 2, 512], F32);          b_ft = [Buf("ft0"), Buf("ft1")]
        csb = xsr[0:17, 0, :]; cth = xsr[0:17, 1, :]; b_csb = b_xsr[0]; b_cth = b_xsr[1]
        wsraw = xsr[:, 0, 0:512].rearrange("p (h t) -> p h t", h=4); b_wsraw = b_xsr[0]
        wtmp = xsr[:, 0, 512:1024].rearrange("p (h t) -> p h t", h=4); b_wtmp = b_xsr[0]
        s0slot = [xsr[:, 0, 0:512], xsr[:, 1, 0:512], xres[:, 5, 0:512], xres[:, 5, 512:1024]]

        def s0_load(j):
            sl = j % 4
            guards = ([b_xsr[sl]] if sl < 2 else [b_x[5]]) if j < 4 else []
            S.dma("sp", lambda e, j=j, sl=sl: e.dma_start(out=s0slot[sl].rearrange("p (h v) -> p h v", h=4), in_=s0[j].rearrange("h k v -> k h v")),
                  writes=[b_s0[sl]] + guards)
        snew = [xsr[:, r, 512:1024].rearrange("p (h v) -> p h v", h=4) for r in range(2)]
        ftmp2 = T("ftmp2", [128, 2, 512], F32);        b_ft2 = [Buf("ft20"), Buf("ft21")]
        vout = T("vout", [128, 512], F32);             b_vout = Buf("vout")
        bnst = T("bnst", [128, 5, 6], F32); bnag = T("bnag", [128, 5, 4], F32); b_bn = [Buf("bn%d" % i) for i in range(5)]
        b_scm = [Buf("scm0"), Buf("scm1")]; b_ktok = [Buf("kt0"), Buf("kt1")]; b_ktm = [Buf("ktm0"), Buf("ktm1")]
        b_ket = [Buf("ket0"), Buf("ket1")]

        off = [0]

        def carve(n_elems, dt):
            mult = 2 if dt == F32 else 1
            start = off[0]
            off[0] += n_elems * mult
            assert off[0] <= ARENA, off[0]
            reg = arena[:, start:start + n_elems * mult]
            return reg.bitcast(dt) if dt == F32 else reg

        TB = TBMAX
        assert off[0] == 0
        qd_all = carve(4 * TB, BF16)
        kd_all = carve(4 * TB, BF16)
        qdT = [qd_all[:, h * TB:(h + 1) * TB] for h in range(4)]
        kdT = [kd_all[:, h * TB:(h + 1) * TB] for h in range(4)]
        uu = warena[:, 16384:16384 + 4 * TB]
        vln = warena[:, 16384 + 4 * TB:16384 + 4 * TB + NTL * 512]
        sg2 = carve(4 * TB, BF16)
        vtok = carve(NTL * 512, BF16)
        zq = [carve(512, F32) for _ in range(2)]
        thb = [carve(512, F32) for _ in range(2)]
        fb = thb
        kb = [carve(512, F32)] * 2
        Pb = [carve(512, F32)] * 2
        Pinv = [carve(512, F32)] * 2
        Eb = Pinv
        gv = zq
        et1 = thb
        et2 = [kb[0], Pb[0]]
        ket = [carve(512, BF16) for _ in range(2)]
        scm_ = [carve(512, BF16) for _ in range(2)]
        ktok_ = [carve(512, BF16) for _ in range(2)]
        ktm_ = [carve(512, BF16) for _ in range(2)]
        b_qd = [Buf("qd%d" % h) for h in range(4)]; b_kd = [Buf("kd%d" % h) for h in range(4)]
        b_sg = Buf("sg2"); b_uu = Buf("uu")
        b_vtok = [Buf("vtok%d" % i) for i in range(NTL)]; b_vln = [Buf("vln%d" % i) for i in range(NTL)]
        b_zq = [Buf("zq0"), Buf("zq1")]; b_th = [Buf("th0"), Buf("th1")]; b_fb = b_th
        b_kb = [Buf("kb")] * 2; b_Pb = [Buf("Pb")] * 2; b_Pi = [Buf("Pi")] * 2; b_Eb = b_Pi
        b_gv = b_zq; b_e1 = b_th; b_e2 = [b_kb[0], b_Pb[0]]
        mix_bufs = (b_qd + b_kd + [b_sg, b_uu] + b_vtok + b_vln + b_zq + b_th + [b_kb[0], b_Pb[0], b_Pi[0]]
                    + b_scm + b_ktok + b_ktm + b_ket)
        act_bufs = [b for row in b_act for b in row]

        def v4(ap):
            return ap.rearrange("p (h k) -> p h k", h=4)

        assert NJ * TBMAX <= ARENA

        def actT(j, c0, n):
            return arena[:, j * TBMAX + c0: j * TBMAX + c0 + n]

        WZI0 = NJ * TBMAX
        wz_i = arena[:, WZI0:WZI0 + 4096]
        wz_v = warena[:, 4096:8192]
        wo_sb = warena[:, 8192:16384]
        b_wzi = Buf("wzi"); b_wzv = Buf("wzv"); b_wo = Buf("wo")

        def wd_sb(j):
            return warena[:, j * D:(j + 1) * D]

        adaring = arena[:, 0:8192].rearrange("p (r k n) -> p r k n", r=2, k=8)

        ps = [es.enter_context(nc.psum_tensor("ps%d" % i, [128, 512], F32)) for i in range(8)]
        b_ps = [Buf("ps%d" % i) for i in range(8)]
        mmring = [0]

        def next_bank(n=2):
            k = mmring[0] % n
            mmring[0] += 1
            return k

        S.op("pool", lambda e: e.memset(ones[:], 1.0), writes=[b_c])
        S.op("pool", lambda e: e.memset(nhalf[:], -0.5), writes=[b_c])
        S.op("pool", lambda e: e.affine_select(out=ident[:], in_=ones[:], pattern=[[1, 128]], compare_op=ALU.is_equal,
                                               fill=0.0, base=0, channel_multiplier=-1), reads=[b_c], writes=[b_c])
        S.op("pool", lambda e: e.affine_select(out=Mc[:], in_=ones[:], pattern=[[1, 128]], compare_op=ALU.is_ge,
                                               fill=0.0, base=0, channel_multiplier=-1), reads=[b_c], writes=[b_c])
        S.op("pool", lambda e: e.tensor_copy(out=identb[:], in_=ident[:]), reads=[b_c], writes=[b_c])
        S.op("pool", lambda e: e.memset(M64[:], 0.0), writes=[b_c])
        S.op("pool", lambda e: e.tensor_copy(out=M64[0:64, 0:64], in_=Mc[0:64, 0:64]), reads=[b_c], writes=[b_c])
        S.op("pool", lambda e: e.tensor_copy(out=M64[64:128, 64:128], in_=Mc[64:128, 64:128]), reads=[b_c], writes=[b_c])
        S.op("pool", lambda e: e.tensor_copy(out=E16[:].rearrange("p (j i) -> p j i", i=8),
                                             in_=bc(ident[0:16, 0:16], [[1, 16], [0, 8]])), reads=[b_c], writes=[b_c])
        S.op("pool", lambda e: e.tensor_copy(out=Rep8[:].rearrange("p (j i) -> p j i", i=8),
                                             in_=bc(ident[0:8, 0:8], [[0, 16], [1, 8]])), reads=[b_c], writes=[b_c])
        S.op("pe", lambda e: e.matmul(ps[7][:, 0:128], lhsT=E16[:], rhs=E16[:], start=True, stop=True), reads=[b_c], writes=[b_ps[7]])
        S.op("dve", lambda e: e.tensor_tensor(out=Ms[:], in0=ps[7][:, 0:128], in1=Mc[:], op=ALU.mult), reads=[b_ps[7], b_c], writes=[b_c])
        S.op("pe", lambda e: e.transpose(out=ps[7][:, 128:144], in_=E16[:], identity=ident[0:16, 0:16]), reads=[b_c], writes=[b_ps[7]])
        S.op("dve", lambda e: e.tensor_copy(out=Esel[:], in_=ps[7][:, 128:144]), reads=[b_ps[7]], writes=[b_c])

        S.dma("sp", lambda e: e.dma_start(out=vecs_sb[:], in_=vecs), writes=[b_vecs])
        S.op("pe", lambda e: e.transpose(out=ps[6][:, 0:128], in_=vecs_sb[:], identity=ident[:]), reads=[b_vecs, b_c], writes=[b_ps[6]])
        S.op("act", lambda e: e.activation(out=colv[:], in_=ps[6][:, 0:128], func=AF.Copy), reads=[b_ps[6]], writes=[b_colv])
        S.op("dve", lambda e: e.tensor_tensor(out=AB[:, 0:4], in0=colv[:, R_LB0:R_LB0 + 4], in1=colv[:, R_LB1:R_LB1 + 4], op=ALU.subtract),
             reads=[b_colv], writes=[b_AB])
        S.op("act", lambda e: e.activation(out=AB[:, 4:8], in_=AB[:, 0:4], func=AF.Tanh, scale=0.5), reads=[b_AB], writes=[b_AB])
        S.op("dve", lambda e: e.tensor_scalar(out=AB[:, 0:4], in0=AB[:, 4:8], scalar1=0.25, scalar2=0.75, op0=ALU.mult, op1=ALU.add),
             reads=[b_AB], writes=[b_AB])
        S.op("dve", lambda e: e.tensor_scalar(out=nB[:], in0=AB[:, 4:8], scalar1=0.25, scalar2=-0.25, op0=ALU.mult, op1=ALU.add),
             reads=[b_AB], writes=[b_AB])
        S.op("dve", lambda e: e.tensor_scalar(out=AB[:, 4:8], in0=nB[:], scalar1=-1.0, scalar2=None, op0=ALU.mult),
             reads=[b_AB], writes=[b_AB])
        S.op("dve", lambda e: e.tensor_scalar(out=gnh[:], in0=colv[:, R_GN:R_GN + 1], scalar1=0.5, scalar2=None, op0=ALU.mult),
             reads=[b_colv], writes=[b_AB])

        S.dma("sp", lambda e: e.dma_start(out=csb, in_=call), writes=[b_csb])
        S.op("act", lambda e: e.activation(out=cth, in_=csb, func=AF.Tanh, scale=0.5), reads=[b_csb], writes=[b_cth])
        S.op("dve", lambda e: e.tensor_scalar(out=cth, in0=cth, scalar1=0.5, scalar2=0.5, op0=ALU.mult, op1=ALU.add),
             reads=[b_cth], writes=[b_cth])
        S.op("dve", lambda e: e.tensor_tensor(out=cth, in0=cth, in1=csb, op=ALU.mult), reads=[b_cth, b_csb], writes=[b_cth])
        for kc in range(8):
            S.op("pe", lambda e, kc=kc: e.transpose(out=ps[6][:, 128 + kc * 17:128 + (kc + 1) * 17], in_=cth[:, kc * 128:(kc + 1) * 128],
                                                   identity=ident[0:17, 0:17]), reads=[b_cth, b_c], writes=[b_ps[6]])
        S.op("act", lambda e: e.activation(out=scT[:], in_=ps[6][:, 128:128 + 136].rearrange("p (k s) -> p k s", s=17), func=AF.Copy),
             reads=[b_ps[6]], writes=[b_scT])

        wada_v = w_ada.rearrange("(kc p) n -> p kc n", p=128)
        for ch in range(12):
            r = ch % 2
            S.dma("pool", lambda e, ch=ch, r=r: e.dma_start(out=adaring[:, r], in_=wada_v[:, :, ch * 512:(ch + 1) * 512]), writes=[b_ada[r]])
            for jj in range(4):
                j = ch * 4 + jj
                bank = 6 + (j // 24)
                col = (j % 24) * 17
                for kc in range(8):
                    S.op("pe", lambda e, r=r, jj=jj, kc=kc, bank=bank, col=col: e.matmul(
                        ps[bank][:, col:col + 17], lhsT=adaring[:, r, kc, jj * 128:(jj + 1) * 128], rhs=scT[:, kc, :],
                        start=(kc == 0), stop=(kc == 7)), reads=[b_ada[r], b_scT], writes=[b_ps[bank]], signal=(kc == 7))
        for half in range(2):
            S.op("dve", lambda e, half=half: e.tensor_tensor(
                out=modT[:, half * 24:(half + 1) * 24, :], in0=ps[6 + half][:, 0:408].rearrange("p (j s) -> p j s", s=17),
                in1=bc(colv[:, R_BADA + half * 24:R_BADA + half * 24 + 24], [[1, 24], [0, 17]]), op=ALU.add),
                reads=[b_ps[6 + half], b_colv], writes=[b_mod])
        for (dst, mo, ro) in ((Gpre, 8, R_NPRE), (Gpf, 32, R_FPRE)):
            S.op("dve", lambda e, dst=dst, mo=mo: e.tensor_scalar(out=dst[:], in0=modT[:, mo:mo + 8, :], scalar1=1.0, scalar2=None, op0=ALU.add),
                 reads=[b_mod], writes=[b_G])
            S.op("dve", lambda e, dst=dst, ro=ro: e.tensor_tensor(out=dst[:], in0=dst[:], in1=bc(colv[:, ro:ro + 8], [[1, 8], [0, 17]]), op=ALU.mult),
                 reads=[b_G, b_colv], writes=[b_G])
        for (dst, mo, ro) in ((G1f, 16, R_NPOST), (G2f, 40, R_FPOST)):
            S.op("dve", lambda e, dst=dst, mo=mo, ro=ro: e.tensor_tensor(out=dst[:], in0=modT[:, mo:mo + 8, :],
                                                                          in1=bc(colv[:, ro:ro + 8], [[1, 8], [0, 17]]), op=ALU.mult),
                 reads=[b_mod, b_colv], writes=[b_G])

        lmi = [0]

        def bcast_block(src_fn, dst_ap, dst_buf, bank, col, extra_reads=()):
            r = lmi[0] % 2
            lmi[0] += 1
            S.op("dve", lambda e, r=r: src_fn(e, Lm[:, r, :]), reads=list(extra_reads), writes=[b_Lm[r]])
            S.op("pe", lambda e, r=r: e.matmul(ps[bank][:, col:col + 128], lhsT=Lm[:, r, :], rhs=ident[:], start=True, stop=True),
                 reads=[b_Lm[r], b_c], writes=[b_ps[bank]])

        for (Gf, Gb) in ((G1f, G1bc), (G2f, G2bc)):
            for smp in range(2):
                for half in range(2):
                    bank = 6 + half
                    for cc in range(4):
                        c = half * 4 + cc
                        if smp == 0:
                            fn = lambda e, Lo, Gf=Gf, c=c: e.tensor_copy(out=Lo, in_=bc(Gf[:, c, 0:1], [[0, 128]]))
                        else:
                            fn = lambda e, Lo, Gf=Gf, c=c: e.tensor_copy(out=Lo.rearrange("p (j i) -> p j i", i=8),
                                                                        in_=bc(Gf[:, c, 1:17], [[1, 16], [0, 8]]))
                        bcast_block(fn, None, None, bank, cc * 128, extra_reads=[b_G])
                    S.op("act", lambda e, Gb=Gb, smp=smp, half=half, bank=bank: e.activation(
                        out=Gb[:, smp, half * 512:(half + 1) * 512], in_=ps[bank][:, :], func=AF.Copy), reads=[b_ps[bank]], writes=[b_Gbc])
        for which in range(2):
            for cc in range(4):
                row = (R_LNG if which == 0 else R_LNB) + cc
                fn = lambda e, Lo, row=row: e.tensor_copy(out=Lo, in_=bc(colv[:, row:row + 1], [[0, 128]]))
                bcast_block(fn, None, None, 6, cc * 128, extra_reads=[b_colv])
            S.op("act", lambda e, which=which: e.activation(out=lnbc[:, which, :], in_=ps[6][:, :], func=AF.Copy), reads=[b_ps[6]], writes=[b_Gbc])
        for h in range(4):
            fn = lambda e, Lo, h=h: e.tensor_copy(out=Lo, in_=bc(colv[:, R_BS + h:R_BS + h + 1], [[0, 128]]))
            bcast_block(fn, None, None, 7, h * 128, extra_reads=[b_colv])
        S.op("act", lambda e: e.activation(out=bsbc[:, 0].rearrange("p h t -> p (h t)"), in_=ps[7][:, :], func=AF.Copy), reads=[b_ps[7]], writes=[b_Gbc])
        S.op("dve", lambda e: e.tensor_copy(out=bsbc[:, 1].rearrange("p h (j i) -> p h j i", i=8),
                                            in_=bc(bsbc[:, 0, 0, 0:8], [[128, 4], [0, 16], [1, 8]])), reads=[b_Gbc], writes=[b_Gbc])

        S.dma("sp", lambda e: e.dma_start(out=wsraw, in_=wspat.rearrange("h t s -> t h s")), writes=[b_wsraw])
        for h in range(4):
            S.op("pe", lambda e, h=h: e.transpose(out=ps[6][:, h * 128:(h + 1) * 128], in_=wsraw[:, h, :], identity=ident[:]),
                 reads=[b_wsraw, b_c], writes=[b_ps[6]])
        S.op("act", lambda e: e.activation(out=xsr[:, 0, 512:1024], in_=ps[6][:, :], func=AF.Copy), reads=[b_ps[6]], writes=[b_wtmp])
        S.op("dve", lambda e: e.tensor_tensor(out=wsT[:, 0], in0=wtmp, in1=bc(Mc[:, :], [[0, 4], [1, 128]]), op=ALU.mult),
             reads=[b_wtmp, b_c], writes=[b_ws])
        S.op("dve", lambda e: e.tensor_copy(out=xsr[0:8, 1, 0:512].rearrange("p (h j i) -> p h j i", h=4, i=8),
                                            in_=bc(xsr[0:8, 0, 512:520], [[128, 4], [0, 16], [1, 8]])), reads=[b_wtmp], writes=[b_xsr[1]])
        S.op("pe", lambda e: e.matmul(ps[7][:, :], lhsT=Rep8[:], rhs=xsr[0:8, 1, 0:512], start=True, stop=True),
             reads=[b_xsr[1], b_c], writes=[b_ps[7]])
        S.op("dve", lambda e: e.tensor_tensor(out=wsT[:, 1], in0=ps[7][:, :].rearrange("p (h t) -> p h t", h=4),
                                              in1=bc(Ms[:, :], [[0, 4], [1, 128]]), op=ALU.mult), reads=[b_ps[7], b_c], writes=[b_ws])
        S.op("pool", lambda e: e.memset(Sst[:], 0.0), writes=[b_S])
        S.op("pool", lambda e: e.memset(Sbf[:, 0], 0.0), writes=[b_Sbf[0]])

        xp_t = xp.rearrange("(n p) d -> n p d", p=128)
        yp_t = yp.rearrange("(n p) d -> n p d", p=128)
        win_v = w_in.rearrange("(kc p) n -> p kc n", p=128)
        wout_v = w_out.rearrange("(kc p) n -> p kc n", p=128)
        wg_v = w_gate.rearrange("(kc p) n -> p kc n", p=128)
        wu_v = w_up.rearrange("(kc p) n -> p kc n", p=128)
        ssq_i = [0]
        po_i = [0]
        wr_i = [0]
        rr = {"xsr": 0, "ys": 0, "ft": 0, "zq": 0, "th": 0, "scm": 0, "kt": 0, "gv": 0, "e": 0}

        def rot(name, n=2):
            v = rr[name] % n
            rr[name] += 1
            return v

        def pre_norm(tiles, sample_flags, Gt, sh_off, hook=None):
            base = (ssq_i[0] % 6) * 8
            ssq_i[0] += 1
            nt = len(tiles)
            stb = [b_st[base + i] for i in range(nt)]
            for i, li in enumerate(tiles):
                col = base + i
                S.op("act", lambda e, li=li, col=col: e.activation(out=junk[:], in_=xres[:, li, :], func=AF.Square,
                                                                  accum_out=ssq[:, col:col + 1]), reads=[b_x[li]], writes=[b_st[col], b_junk])
            S.op("dve", lambda e: e.tensor_scalar(out=rstd[:, base:base + nt], in0=ssq[:, base:base + nt], scalar1=1.0 / D, scalar2=EPS,
                                                  op0=ALU.mult, op1=ALU.add), reads=stb, writes=stb)
            S.op("pool", lambda e: e.tensor_tensor(out=rstd[:, base:base + nt], in0=rstd[:, base:base + nt], in1=bc(nhalf[:, 0:1], [[0, nt]]), op=ALU.pow),
                 reads=stb + [b_c], writes=stb)
            slots = {}

            def scale_tile(i):
                li = tiles[i]
                col = base + i
                r = rot("xsr")
                slots[i] = r
                S.op("dve", lambda e, li=li, r=r, col=col: e.tensor_scalar(out=xsr[:, r, :], in0=xres[:, li, :], scalar1=rstd[:, col:col + 1],
                                                                          scalar2=None, op0=ALU.mult),
                     reads=[b_x[li], b_st[col]], writes=[b_xsr[r], b_s0[r], b_snew[r]])

            scale_tile(0)
            for i, (li, is_s) in enumerate(zip(tiles, sample_flags)):
                r = slots[i]
                for half in range(2):
                    bank = next_bank(6)
                    for cc in range(4):
                        c = half * 4 + cc
                        S.op("pe", lambda e, r=r, c=c, cc=cc, bank=bank: e.transpose(out=ps[bank][:, cc * 128:(cc + 1) * 128],
                                                                                   in_=xsr[:, r, c * 128:(c + 1) * 128], identity=ident[:]),
                             reads=[b_xsr[r], b_c], writes=[b_ps[bank]], signal=(cc == 3))
                    if half == 0 and i + 1 < nt:
                        scale_tile(i + 1)
                    k = rot("ft")
                    if not is_s:
                        g_ap = bc(Gt[:, half * 4, 0:1], [[17, 4], [0, 128]])
                        s_ap = bc(modT[:, sh_off + half * 4, 0:1], [[17, 4], [0, 128]])
                        pat = "p (c t) -> p c t"
                        kw = dict(c=4)
                        opat = "p c t -> p c t"
                    else:
                        g_ap = bc(Gt[:, half * 4, 1:17], [[17, 4], [1, 16], [0, 8]])
                        s_ap = bc(modT[:, sh_off + half * 4, 1:17], [[17, 4], [1, 16], [0, 8]])
                        pat = "p (c j i) -> p c j i"
                        kw = dict(c=4, i=8)
                        opat = "p c (j i) -> p c j i"
                    S.op("dve", lambda e, bank=bank, k=k, g_ap=g_ap, pat=pat, kw=kw: e.tensor_tensor(
                        out=ftmp[:, k, :].rearrange(pat, **kw), in0=ps[bank][:, :].rearrange(pat, **kw), in1=g_ap, op=ALU.mult),
                        reads=[b_ps[bank], b_G], writes=[b_ft[k]])
                    okw = dict(i=8) if is_s else {}
                    S.op("pool", lambda e, half=half, k=k, li=li, s_ap=s_ap, pat=pat, kw=kw, opat=opat, okw=okw: e.tensor_tensor(
                        out=aT[:, half * 4:half * 4 + 4, li * 128:(li + 1) * 128].rearrange(opat, **okw) if okw else aT[:, half * 4:half * 4 + 4, li * 128:(li + 1) * 128],
                        in0=ftmp[:, k, :].rearrange(pat, **kw), in1=s_ap, op=ALU.add),
                        reads=[b_ft[k], b_mod], writes=[b_aT[li]])
                if hook is not None:
                    hook(i)

        WQ = []
        WQ_tag = []
        for _p in range(3):
            for hp in range(2):
                WQ += [(win_v, hp * 256), (win_v, 512 + hp * 256), (win_v, 2048 + hp * 256)]
            WQ += [(win_v, 1536 + hp * 256) for hp in range(2)]
            WQ_tag += [(_p, "B")] * 8
            for jp in range(NJ // 2):
                WQ += [(wg_v, jp * 256), (wu_v, jp * 256)]
            WQ_tag += [(_p, "F")] * NJ
        xsr_ok = set()
        wq_pos = [0]
        wq_issued = [0]
        wq_restrict = [True]
        wq_rr = [0]
        wring_flat = wring[:].rearrange("p a k n -> p (a k n)")
        wslot = [wring_flat[:, 0:2048], wring_flat[:, 2048:4096], xsr[:, 0, :].bitcast(BF16), xsr[:, 1, :].bitcast(BF16)]
        wslot_r = [b_wring[0], b_wring[1], b_xsr[0], b_xsr[1]]
        wslot_w = [[b_wring[0]], [b_wring[1]], [b_xsr[0], b_s0[0], b_snew[0]], [b_xsr[1], b_s0[1], b_snew[1]]]
        slot_owner = [None] * 4
        chunk_slot = {}
        consumed = set()

        def wq_issue(upto):
            while wq_issued[0] <= min(upto, len(WQ) - 1):
                cands = [0, 1, 2, 3] if WQ_tag[wq_issued[0]] in xsr_ok else [0, 1]
                free = [sl for sl in cands if slot_owner[sl] is None or slot_owner[sl] in consumed]
                if not free:
                    return
                free.sort(key=lambda sl: -1 if slot_owner[sl] is None else slot_owner[sl])
                sl = free[0]
                k = wq_issued[0]
                wq_issued[0] += 1
                slot_owner[sl] = k
                chunk_slot[k] = sl
                src_v, col0 = WQ[k]
                S.dma("pool", lambda e, sl=sl, src_v=src_v, col0=col0: e.dma_start(out=wslot[sl].rearrange("p (k n) -> p k n", k=8),
                                                                                  in_=src_v[:, :, col0:col0 + 256]),
                      writes=wslot_w[sl])

        def wring_load(src_v, col0):
            k = wq_pos[0]
            wq_pos[0] += 1
            assert WQ[k][1] == col0 and WQ[k][0] is src_v, (k, col0)
            wq_issue(k)
            assert k in chunk_slot, "weight chunk could not be issued (no free staging slot)"
            return k

        def wq_done():
            for k in range(wq_pos[0]):
                consumed.add(k)
            wq_issue(wq_pos[0] - 1 + 4)

        def fm_matmul(k, sub, grp, bank):
            c0, n, tl, _s = grp
            sl = chunk_slot[k]
            for kc in range(8):
                S.op("pe", lambda e, kc=kc: e.matmul(ps[bank][:, 0:n], lhsT=wslot[sl][:, kc * 256 + sub * 128:kc * 256 + (sub + 1) * 128],
                                                    rhs=aT[:, kc, c0:c0 + n], start=(kc == 0), stop=(kc == 7)),
                     reads=[wslot_r[sl]] + [b_aT[t] for t in tl], writes=[b_ps[bank]], signal=(kc == 7))

        PASS_TILES = [6, 6, 5]
        for pas in range(3):
            ntile = PASS_TILES[pas]
            tiles = list(range(ntile))
            sflags = [False] * ntile
            if pas == 2:
                sflags[4] = True
                groups = [(0, 512, [0, 1, 2, 3], False), (512, 128, [4], True)]
            else:
                groups = [(0, 512, [0, 1, 2, 3], False), (512, 256, [4, 5], False)]
            gtile0 = pas * 6

            for li in tiles:
                src = xs if sflags[li] else xp_t[gtile0 + li]
                S.dma("sp", lambda e, li=li, src=src: e.dma_start(out=xres[:, li, :], in_=src), writes=[b_x[li]])
            wzi_guard = [b_kb[0], b_Pb[0], b_Pi[0]] + b_ket + b_scm
            if pas == 0:
                S.dma("pool", lambda e: e.dma_start(out=wz_i.rearrange("p (k n) -> p k n", k=8), in_=win_v[:, :, 1024:1536]), writes=[b_wzi] + wzi_guard)
            S.dma("pool", lambda e: e.dma_start(out=wz_v.rearrange("p (k n) -> p k n", k=8), in_=win_v[:, :, 2560:3072]),
                  writes=[b_wzv] + b_wd[4:8] + (b_ada if pas == 0 else []))
            S.dma("pool", lambda e: e.dma_start(out=wo_sb.rearrange("p (k n) -> p k n", k=8), in_=wout_v), writes=[b_wo] + b_wd[8:16])

            def wd_load(j, guards):
                S.dma("pool", lambda e, j=j: e.dma_start(out=wd_sb(j), in_=w_down[j * 128:(j + 1) * 128, :]), writes=[b_wd[j]] + list(guards))

            for j in range(0, 4):
                wd_load(j, b_ada if pas == 0 else [])
            wq_issue(wq_pos[0] + 3)
            first_mix = [True]

            def mixw(bufs):
                if first_mix[0]:
                    first_mix[0] = False
                    return list(bufs) + act_bufs + b_ada
                return list(bufs)

            def zi_tile(li):
                bank = 6 + (li % 2)
                for kc in range(8):
                    S.op("pe", lambda e, kc=kc, li=li, bank=bank: e.matmul(ps[bank][:, :], lhsT=aT[:, kc, li * 128:(li + 1) * 128],
                                                                         rhs=wz_i[:, kc * 512:(kc + 1) * 512], start=(kc == 0), stop=(kc == 7)),
                         reads=[b_aT[li], b_wzi], writes=[b_ps[bank]], signal=(kc == 7))
                S.op("act", lambda e, li=li, bank=bank: e.activation(out=vtok[:, li * 512:(li + 1) * 512], in_=ps[bank][:, :], func=AF.Copy),
                     reads=[b_ps[bank]], writes=mixw([b_vtok[li]]))

            pre_norm(tiles, sflags, Gpre, 0, hook=lambda i: zi_tile(tiles[i - 1]) if i >= 1 else None)
            zi_tile(tiles[-1])
            xsr_ok.add((pas, "B"))
            wq_issue(wq_pos[0] + 3)

            S.op("pool", lambda e: e.memset(vout[:], 0.0), writes=[b_vout])
            for h in range(4):
                if h % 2 == 0:
                    rq = wring_load(win_v, (h // 2) * 256)
                    rf = wring_load(win_v, 512 + (h // 2) * 256)
                    ru = wring_load(win_v, 2048 + (h // 2) * 256)
                for gi, grp in enumerate(groups):
                    c0, n, tl, is_s = grp
                    bq = next_bank()
                    fm_matmul(rq, h % 2, grp, bq)
                    zi_ = rot("zq")
                    S.op("act", lambda e, bq=bq, zi_=zi_, n=n: e.activation(out=zq[zi_][:, 0:n], in_=ps[bq][:, 0:n], func=AF.Copy),
                         reads=[b_ps[bq]], writes=[b_zq[zi_]])
                    bf_ = next_bank()
                    fm_matmul(rf, h % 2, grp, bf_)
                    bu = 2 + ((h * 2 + gi) % 2)
                    fm_matmul(ru, h % 2, grp, bu)
                    S.op("act", lambda e, bu=bu, n=n, h=h, c0=c0: e.activation(out=uu[:, h * TB + c0:h * TB + c0 + n], in_=ps[bu][:, 0:n],
                                                                              func=AF.Gelu_apprx_tanh), reads=[b_ps[bu]], writes=[b_uu] + wd_tail + b_ada)
                    if gi == len(groups) - 1 and h % 2 == 1:
                        wq_done()
                    ti = rot("th")
                    S.op("act", lambda e, bf_=bf_, ti=ti, n=n: e.activation(out=thb[ti][:, 0:n], in_=ps[bf_][:, 0:n], func=AF.Tanh, scale=0.5),
                         reads=[b_ps[bf_]], writes=[b_th[ti]])
                    S.op("act", lambda e, ti=ti, n=n, h=h: e.activation(out=fb[ti][:, 0:n], in_=thb[ti][:, 0:n], func=AF.Identity,
                                                                       scale=AB[:, 4 + h:5 + h], bias=AB[:, h:h + 1]),
                         reads=[b_th[ti], b_AB], writes=[b_fb[ti]])
                    S.op("pool", lambda e, ti=ti, n=n: e.tensor_scalar(out=kb[ti][:, 0:n], in0=fb[ti][:, 0:n], scalar1=-1.0,
                                                                      scalar2=1.0, op0=ALU.mult, op1=ALU.add),
                         reads=[b_fb[ti]], writes=[b_kb[ti]])
                    csz = 8 if is_s else 64
                    nch = n // csz
                    S.op("act", lambda e, ti=ti, csz=csz, nch=nch: e.activation(out=bc(vout[:, 0:1], [[csz, nch]]), in_=bc(fb[ti][:, 0:1], [[csz, nch]]), func=AF.Copy),
                         reads=[b_fb[ti]], writes=[b_vout])
                    S.op("dve", lambda e, ti=ti, n=n: e.tensor_tensor_scan(out=Pb[ti][:, 0:n], data0=fb[ti][:, 0:n], data1=vout[:, 0:n],
                                                                          initial=1.0, op0=ALU.mult, op1=ALU.max), reads=[b_fb[ti], b_vout], writes=[b_Pb[ti]])
                    if is_s:
                        S.op("pool", lambda e: e.memset(vout[:, 0:128], 0.0), writes=[b_vout])
                    pl0 = 16 if is_s else gi * 8
                    S.op("pool", lambda e, ti=ti, csz=csz, nch=nch, pl0=pl0, h=h: e.tensor_copy(
                        out=PL[:, pl0:pl0 + nch, h], in_=bc(Pb[ti][:, csz - 1:csz], [[csz, nch]])), reads=[b_Pb[ti]], writes=[b_PL])
                    S.op("dve", lambda e, ti=ti, n=n: e.reciprocal(out=Pinv[ti][:, 0:n], in_=Pb[ti][:, 0:n]), reads=[b_Pb[ti]], writes=[b_Pi[ti]])
                    S.op("pool", lambda e, ti=ti, zi_=zi_, n=n, c0=c0, h=h: e.tensor_tensor(out=qdT[h][:, c0:c0 + n], in0=zq[zi_][:, 0:n],
                                                                                          in1=Pb[ti][:, 0:n], op=ALU.mult),
                         reads=[b_zq[zi_], b_Pb[ti]], writes=[b_qd[h]])
                    S.op("pool", lambda e, ti=ti, n=n, c0=c0, h=h: e.tensor_tensor(out=kdT[h][:, c0:c0 + n], in0=kb[ti][:, 0:n],
                                                                                  in1=Pinv[ti][:, 0:n], op=ALU.mult),
                         reads=[b_kb[ti], b_Pi[ti]], writes=[b_kd[h]])

            for h in range(4):
                if h % 2 == 0:
                    rg = wring_load(win_v, 1536 + (h // 2) * 256)
                for gi, grp in enumerate(groups):
                    c0, n, tl, is_s = grp
                    bg = next_bank()
                    fm_matmul(rg, h % 2, grp, bg)
                    if gi == len(groups) - 1 and h % 2 == 1:
                        wq_done()
                    ti = rot("th")
                    S.op("act", lambda e, bg=bg, ti=ti, n=n: e.activation(out=thb[ti][:, 0:n], in_=ps[bg][:, 0:n], func=AF.Tanh, scale=0.5),
                         reads=[b_ps[bg]], writes=[b_th[ti]])
                    S.op("dve", lambda e, bg=bg, ti=ti, n=n, h=h, c0=c0: e.scalar_tensor_tensor(
                        out=sg2[:, h * TB + c0:h * TB + c0 + n], in0=thb[ti][:, 0:n], scalar=1.0, in1=ps[bg][:, 0:n], op0=ALU.add, op1=ALU.mult),
                        reads=[b_th[ti], b_ps[bg]], writes=[b_sg])

            tstate = {}

            def front(li):
                is_s = sflags[li]
                c0 = li * 128
                mask = Ms if is_s else M64
                for h in range(4):
                    S.op("pe", lambda e, h=h, c0=c0: e.matmul(ps[2][:, h * 128:(h + 1) * 128], lhsT=kdT[h][:, c0:c0 + 128], rhs=qdT[h][:, c0:c0 + 128],
                                                            start=True, stop=True), reads=[b_kd[h], b_qd[h]], writes=[b_ps[2]], signal=(h == 3))
                si = rot("scm")
                S.op("dve", lambda e, si=si, mask=mask: e.tensor_tensor(out=v4(scm_[si]), in0=ps[2][:, :].rearrange("p (h t) -> p h t", h=4),
                                                                       in1=bc(mask[:, :], [[0, 4], [1, 128]]), op=ALU.mult),
                     reads=[b_ps[2], b_c], writes=[b_scm[si]])
                ke_i = rot("kt")
                if not is_s:
                    S.op("pool", lambda e, ke_i=ke_i, c0=c0, li=li: e.tensor_tensor(
                        out=ket[ke_i].rearrange("p (h c i) -> p h c i", h=4, i=64), in0=bc(kd_all[:, c0:c0 + 1], [[TB, 4], [64, 2], [1, 64]]),
                        in1=bc(PL[:, li * 2, 0:1], [[1, 4], [4, 2], [0, 64]]), op=ALU.mult), reads=b_kd + [b_PL], writes=[b_ket[ke_i]])
                else:
                    S.op("pool", lambda e, ke_i=ke_i, c0=c0: e.tensor_tensor(
                        out=ket[ke_i].rearrange("p (h c i) -> p h c i", h=4, i=8), in0=bc(kd_all[:, c0:c0 + 1], [[TB, 4], [8, 16], [1, 8]]),
                        in1=bc(PL[:, 16, 0:1], [[1, 4], [4, 16], [0, 8]]), op=ALU.mult), reads=b_kd + [b_PL], writes=[b_ket[ke_i]])
                tstate[li] = {"si": si, "ki": ke_i, "ob": 4 + (li % 2)}

            def frontB(li):
                ke_i = tstate[li]["ki"]
                psk = ps[3].bitcast(BF16)
                for h in range(4):
                    S.op("pe", lambda e, h=h, ke_i=ke_i, psk=psk: e.transpose(out=psk[:, h * 128:(h + 1) * 128], in_=ket[ke_i][:, h * 128:(h + 1) * 128],
                                                                            identity=identb[:]),
                         reads=[b_ket[ke_i], b_c], writes=[b_ps[3]], signal=(h == 3))
                S.op("act", lambda e, ki=ke_i, psk=psk: e.activation(out=ktok_[ki], in_=psk[:, 0:512], func=AF.Copy),
                     reads=[b_ps[3]], writes=[b_ktok[ke_i]])

            def ds_and_update(li, c, nxt):
                ki = tstate[li]["ki"]
                chunk = (li * 2 + c)
                S.op("dve", lambda e, chunk=chunk: e.tensor_tensor(out=Stmp[:], in0=Sst[:], in1=bc(PL[:, chunk, 0:1], [[1, 4], [0, 128]]), op=ALU.mult),
                     reads=[b_S, b_PL], writes=[b_Stmp])
                for h in range(4):
                    S.op("pe", lambda e, h=h, c=c, li=li, ki=ki: e.matmul(
                        ps[6][:, h * 128:(h + 1) * 128], lhsT=ktok_[ki][c * 64:(c + 1) * 64, h * 128:(h + 1) * 128],
                        rhs=vtok[c * 64:(c + 1) * 64, li * 512 + h * 128:li * 512 + (h + 1) * 128], start=True, stop=True),
                        reads=[b_ktok[ki], b_vtok[li]], writes=[b_ps[6]], signal=(h == 3))
                S.op("dve", lambda e, nxt=nxt: e.tensor_tensor(out=Sbf[:, nxt], in0=Stmp[:], in1=ps[6][:, :].rearrange("p (h v) -> p h v", h=4), op=ALU.add),
                     reads=[b_Stmp, b_ps[6]], writes=[b_Sbf[nxt]])
                S.op("dve", lambda e: e.tensor_tensor(out=Sst[:], in0=Stmp[:], in1=ps[6][:, :].rearrange("p (h v) -> p h v", h=4), op=ALU.add),
                     reads=[b_Stmp, b_ps[6]], writes=[b_S])

            def intra(li, h, stop):
                st = tstate[li]
                si, ob = st["si"], st["ob"]
                S.op("pe", lambda e, h=h, li=li, si=si, ob=ob, stop=stop: e.matmul(
                    ps[ob][:, h * 128:(h + 1) * 128], lhsT=vtok[:, li * 512 + h * 128:li * 512 + (h + 1) * 128],
                    rhs=scm_[si][:, h * 128:(h + 1) * 128], start=True, stop=stop, skip_group_check=True),
                    reads=[b_vtok[li], b_scm[si]], writes=[b_ps[ob]], signal=stop)

            def chainA(li):
                if not sflags[li]:
                    ds_and_update(li, 0, 1)

            def chainB(li):
                st = tstate[li]
                si, ki, ob = st["si"], st["ki"], st["ob"]
                c0 = li * 128
                if not sflags[li]:
                    for h in range(4):
                        intra(li, h, False)
                        for c in range(2):
                            S.op("pe", lambda e, h=h, c=c, c0=c0, ob=ob: e.matmul(
                                ps[ob][:, h * 128 + c * 64:h * 128 + (c + 1) * 64], lhsT=Sbf[:, c, h, :],
                                rhs=qdT[h][:, c0 + c * 64:c0 + (c + 1) * 64], start=False, stop=True, skip_group_check=True),
                                reads=[b_Sbf[c], b_qd[h]], writes=[b_ps[ob]], signal=(h == 3 and c == 1))
                    ds_and_update(li, 1, 0)
                    if pas == 2 and li == 3:
                        S.dma("sp", lambda e: e.dma_start(out=o_sp.rearrange("h k v -> k h v"), in_=Sst[:]), reads=[b_S], final=True)
                    st["o_src"], st["o_buf"] = ps[ob][:, :], b_ps[ob]
                else:
                    for h in range(4):
                        intra(li, h, True)
                    for j in range(NSAMP):
                        r = j % 2
                        sl = j % 4
                        db = 6 if r == 0 else 3
                        S.op("act", lambda e, r=r, sl=sl: e.activation(out=s0bf[:, r].rearrange("p h v -> p (h v)"), in_=s0slot[sl],
                                                                      func=AF.Copy), reads=[b_s0[sl]], writes=[b_s0bf[r]])
                        for h in range(4):
                            S.op("pe", lambda e, h=h, j=j, r=r, c0=c0: e.matmul(
                                ps[2][:, h * 128 + j * 8:h * 128 + (j + 1) * 8], lhsT=s0bf[:, r, h, :], rhs=qdT[h][:, c0 + j * 8:c0 + (j + 1) * 8],
                                start=True, stop=True, skip_group_check=True), reads=[b_s0bf[r], b_qd[h]], writes=[b_ps[2]],
                                signal=(h == 3))
                        S.op("dve", lambda e, j=j, r=r, ki=ki: e.tensor_scalar(out=ktm_[r], in0=ktok_[ki],
                                                                              scalar1=Esel[:, j:j + 1], scalar2=None, op0=ALU.mult),
                             reads=[b_ktok[ki], b_c], writes=[b_ktm[r]])
                        for h in range(4):
                            S.op("pe", lambda e, h=h, r=r, li=li, db=db: e.matmul(ps[db][:, h * 128:(h + 1) * 128], lhsT=ktm_[r][:, h * 128:(h + 1) * 128],
                                                                                 rhs=vtok[:, li * 512 + h * 128:li * 512 + (h + 1) * 128], start=True, stop=True),
                                 reads=[b_ktm[r], b_vtok[li]], writes=[b_ps[db]], signal=(h == 3))
                        S.op("dve", lambda e, j=j, r=r, sl=sl: e.tensor_tensor(out=snew[r], in0=s0slot[sl].rearrange("p (h v) -> p h v", h=4),
                                                                              in1=bc(PL[:, 16 + j, 0:1], [[1, 4], [0, 128]]), op=ALU.mult),
                             reads=[b_s0[sl], b_PL], writes=[b_snew[r]] + ([b_xsr[r]] if j < 2 else []))
                        if j + 4 < NSAMP:
                            s0_load(j + 4)
                        S.op("dve", lambda e, r=r, db=db: e.tensor_tensor(out=snew[r], in0=snew[r], in1=ps[db][:, :].rearrange("p (h v) -> p h v", h=4), op=ALU.add),
                             reads=[b_snew[r], b_ps[db]], writes=[b_snew[r]])
                        S.dma("sp", lambda e, j=j, r=r: e.dma_start(out=o_ss[j].rearrange("h k v -> k h v"), in_=snew[r]), reads=[b_snew[r]], final=True)
                    S.op("act", lambda e: e.activation(out=ftmp2[:, 0, :], in_=ps[2][:, :], func=AF.Copy), reads=[b_ps[2]], writes=[b_ft2[0]])
                    S.op("dve", lambda e, ob=ob: e.tensor_tensor(out=ftmp2[:, 0, :], in0=ps[ob][:, :], in1=ftmp2[:, 0, :], op=ALU.add),
                         reads=[b_ps[ob], b_ft2[0]], writes=[b_ft2[0]])
                    st["o_src"], st["o_buf"] = ftmp2[:, 0, :], b_ft2[0]

            def epi(li):
                st = tstate[li]
                o_src, o_buf = st["o_src"], st["o_buf"]
                c0 = li * 128
                e1 = rot("e")
                S.op("act", lambda e, o_src=o_src, e1=e1: e.activation(out=et1[e1][:, :], in_=o_src, func=AF.Square), reads=[o_buf], writes=[b_e1[e1]])
                for h in range(4):
                    S.op("pe", lambda e, e1=e1, h=h: e.matmul(ps[7][:, h:h + 1], lhsT=et1[e1][:, h * 128:(h + 1) * 128], rhs=ones[:, 0:1],
                                                            start=True, stop=True), reads=[b_c, b_e1[e1]], writes=[b_ps[7]], signal=(h == 3))
                S.op("dve", lambda e: e.tensor_scalar(out=st4[:], in0=ps[7][:, 0:4], scalar1=1.0 / 128, scalar2=EPS, op0=ALU.mult, op1=ALU.add),
                     reads=[b_ps[7]], writes=[b_st4])
                S.op("pool", lambda e: e.tensor_tensor(out=st4[:], in0=st4[:], in1=bc(nhalf[:, 0:1], [[0, 4]]), op=ALU.pow),
                     reads=[b_st4, b_c], writes=[b_st4])
                S.op("pool", lambda e, e1=e1: e.tensor_tensor(out=et2[e1][:, :].rearrange("p (h t) -> p h t", h=4), in0=bc(ident[:, :], [[0, 4], [1, 128]]),
                                                              in1=bc(st4[:, 0:4], [[1, 4], [0, 128]]), op=ALU.mult), reads=[b_st4, b_c], writes=[b_e2[e1]])
                st["e1"] = e1

            def epiB(li):
                st = tstate[li]
                o_src, o_buf = st["o_src"], st["o_buf"]
                c0 = li * 128
                e1 = st["e1"]
                S.op("pe", lambda e, e1=e1: e.matmul(ps[7][:, :], lhsT=ones[:], rhs=et2[e1][:, :], start=True, stop=True),
                     reads=[b_c, b_e2[e1]], writes=[b_ps[7]])
                S.op("act", lambda e, e1=e1: e.activation(out=et2[e1][:, :], in_=ps[7][:, :], func=AF.Copy), reads=[b_ps[7]], writes=[b_e2[e1]])
                S.op("dve", lambda e, e1=e1, o_src=o_src: e.tensor_tensor(out=et1[e1][:, :], in0=o_src, in1=et2[e1][:, :], op=ALU.mult),
                     reads=[o_buf, b_e2[e1]], writes=[b_e1[e1]])
                S.op("dve", lambda e, e1=e1, c0=c0: e.scalar_tensor_tensor(
                    out=oaT[:, 0:4, c0:c0 + 128], in0=et1[e1][:, :].rearrange("p (h t) -> p h t", h=4), scalar=gnh[:, 0:1],
                    in1=bc(sg2[:, c0:c0 + 1], [[TB, 4], [1, 128]]), op0=ALU.mult, op1=ALU.mult),
                    reads=[b_e1[e1], b_sg, b_AB], writes=[b_oa[li]])


            lnst = {}
            gvr = [zq[0], zq[1], Pinv[0], kb[0], Pb[0]]
            b_gvr = [b_zq[0], b_zq[1], b_Pi[0], b_kb[0], b_Pb[0]]
            lnc = [0]

            def ln_front(li):
                bank = 0
                for kc in range(8):
                    S.op("pe", lambda e, kc=kc, li=li, bank=bank: e.matmul(ps[bank][:, :], lhsT=aT[:, kc, li * 128:(li + 1) * 128],
                                                                         rhs=wz_v[:, kc * 512:(kc + 1) * 512], start=(kc == 0), stop=(kc == 7)),
                         reads=[b_aT[li], b_wzv], writes=[b_ps[bank]], signal=(kc == 7))
                g_ = lnc[0] % 5
                lnc[0] += 1
                lnst[li] = g_
                S.op("act", lambda e, g_=g_, bank=bank: e.activation(out=gvr[g_][:, :], in_=ps[bank][:, :], func=AF.Gelu_apprx_tanh),
                     reads=[b_ps[bank]], writes=[b_gvr[g_]])
                S.op("dve", lambda e, g_=g_: e.bn_stats(out=bnst[:, g_, :], in_=gvr[g_][:, :]), reads=[b_gvr[g_]], writes=[b_bn[g_]])
                S.op("dve", lambda e, g_=g_: e.bn_aggr(out=bnag[:, g_, 0:2], in_=bnst[:, g_, :]), reads=[b_bn[g_]], writes=[b_bn[g_]])
                S.op("dve", lambda e, g_=g_: e.tensor_scalar(out=bnag[:, g_, 2:3], in0=bnag[:, g_, 1:2], scalar1=EPS, scalar2=None, op0=ALU.add),
                     reads=[b_bn[g_]], writes=[b_bn[g_]])
                S.op("pool", lambda e, g_=g_: e.tensor_tensor(out=bnag[:, g_, 3:4], in0=bnag[:, g_, 2:3], in1=nhalf[:], op=ALU.pow),
                     reads=[b_bn[g_], b_c], writes=[b_bn[g_]])

            def ln_back(li):
                g_ = lnst[li]
                S.op("dve", lambda e, g_=g_: e.scalar_tensor_tensor(out=gvr[g_][:, :], in0=gvr[g_][:, :], scalar=bnag[:, g_, 0:1], in1=lnbc[:, 0, :],
                                                                    op0=ALU.subtract, op1=ALU.mult), reads=[b_gvr[g_], b_bn[g_], b_Gbc], writes=[b_gvr[g_]])
                is_out = (pas == 2 and li in (3, 4))
                if is_out:
                    S.op("dve", lambda e, g_=g_: e.scalar_tensor_tensor(out=vout[:], in0=gvr[g_][:, :], scalar=bnag[:, g_, 3:4], in1=lnbc[:, 1, :],
                                                                        op0=ALU.mult, op1=ALU.add), reads=[b_gvr[g_], b_bn[g_], b_Gbc], writes=[b_vout])
                    dst = o_vp if li == 3 else o_vs
                    S.dma("sp", lambda e, dst=dst: e.dma_start(out=dst, in_=vout[:]), reads=[b_vout], final=True)
                    S.op("act", lambda e, li=li: e.activation(out=vln[:, li * 512:(li + 1) * 512], in_=vout[:], func=AF.Copy),
                         reads=[b_vout], writes=[b_vln[li]] + wd_tail + b_ada)
                else:
                    S.op("dve", lambda e, g_=g_, li=li: e.scalar_tensor_tensor(out=vln[:, li * 512:(li + 1) * 512], in0=gvr[g_][:, :], scalar=bnag[:, g_, 3:4],
                                                                              in1=lnbc[:, 1, :], op0=ALU.mult, op1=ALU.add),
                         reads=[b_gvr[g_], b_bn[g_], b_Gbc], writes=[b_vln[li]] + wd_tail + b_ada)
                wsel = 1 if sflags[li] else 0
                cb = 1
                for h in range(4):
                    S.op("pe", lambda e, h=h, li=li, wsel=wsel, cb=cb: e.matmul(ps[cb][:, h * 128:(h + 1) * 128], lhsT=vln[:, li * 512 + h * 128:li * 512 + (h + 1) * 128],
                                                                               rhs=wsT[:, wsel, h, :], start=True, stop=True),
                         reads=[b_vln[li], b_ws], writes=[b_ps[cb]], signal=(h == 3))

            def ln_tail(li):
                c0 = li * 128
                wsel = 1 if sflags[li] else 0
                cb = 1
                g_ = lnst[li]
                S.op("dve", lambda e, g_=g_, wsel=wsel, cb=cb: e.tensor_tensor(out=gvr[g_][:, :], in0=ps[cb][:, :], in1=bsbc[:, wsel].rearrange("p h t -> p (h t)"), op=ALU.add),
                     reads=[b_ps[cb], b_Gbc], writes=[b_gvr[g_]])
                S.op("pool", lambda e, g_=g_, c0=c0: e.tensor_tensor(out=aT[:, 4:8, c0:c0 + 128], in0=gvr[g_][:, :].rearrange("p (h t) -> p h t", h=4),
                                                                    in1=bc(uu[:, c0:c0 + 1], [[TB, 4], [1, 128]]), op=ALU.mult),
                     reads=[b_gvr[g_], b_uu], writes=[b_aT[li]])

            LA = 4
            nt_ = len(tiles)
            if pas == 2:
                for j in range(4):
                    s0_load(j)
            front(tiles[0])
            frontB(tiles[0])
            for idx, li in enumerate(tiles):
                chainA(li)
                if idx >= 2:
                    epiB(tiles[idx - 2])
                if idx + 1 < nt_:
                    front(tiles[idx + 1])
                if idx >= 1:
                    epi(tiles[idx - 1])
                if idx + 1 < nt_:
                    frontB(tiles[idx + 1])
                chainB(li)
            if nt_ >= 2:
                epiB(tiles[nt_ - 2])
            epi(tiles[-1])
            epiB(tiles[-1])
            for i in range(min(LA, len(tiles))):
                ln_front(tiles[i])
            for idx, li in enumerate(tiles):
                if idx >= 1:
                    ln_tail(tiles[idx - 1])
                if idx + LA < len(tiles):
                    ln_front(tiles[idx + LA])
                ln_back(li)
            ln_tail(tiles[-1])


            def post_norm_res(li, banks, Gbc_t, is_s):
                col = 48 + (po_i[0] % 8) * 2
                po_i[0] += 1
                for half in range(2):
                    S.op("act", lambda e, half=half, col=col: e.activation(out=junk[:, half * 512:(half + 1) * 512], in_=ps[banks[half]][:, :], func=AF.Square,
                                                                          accum_out=ssq[:, col + half:col + half + 1]),
                         reads=[b_ps[banks[half]]], writes=[b_st[col], b_junk])
                S.op("dve", lambda e, col=col: e.tensor_tensor(out=rstd[:, col:col + 1], in0=ssq[:, col:col + 1], in1=ssq[:, col + 1:col + 2], op=ALU.add),
                     reads=[b_st[col]], writes=[b_st[col]])
                S.op("dve", lambda e, col=col: e.tensor_scalar(out=rstd[:, col:col + 1], in0=rstd[:, col:col + 1], scalar1=1.0 / D, scalar2=EPS,
                                                               op0=ALU.mult, op1=ALU.add), reads=[b_st[col]], writes=[b_st[col]])
                S.op("pool", lambda e, col=col: e.tensor_tensor(out=rstd[:, col:col + 1], in0=rstd[:, col:col + 1], in1=nhalf[:], op=ALU.pow),
                     reads=[b_st[col], b_c], writes=[b_st[col]])
                smp = 1 if is_s else 0
                for half in range(2):
                    k = rot("ft")
                    S.op("dve", lambda e, half=half, k=k, col=col, smp=smp: e.scalar_tensor_tensor(
                        out=ftmp[:, k, :], in0=ps[banks[half]][:, :], scalar=rstd[:, col:col + 1], in1=Gbc_t[:, smp, half * 512:(half + 1) * 512],
                        op0=ALU.mult, op1=ALU.mult), reads=[b_ps[banks[half]], b_st[col], b_Gbc], writes=[b_ft[k]])
                    S.op("pool" if half == 0 else "dve", lambda e, half=half, k=k, li=li: e.tensor_tensor(out=xres[:, li, half * 512:(half + 1) * 512], in0=ftmp[:, k, :],
                                                                                 in1=xres[:, li, half * 512:(half + 1) * 512], op=ALU.add),
                         reads=[b_ft[k], b_x[li]], writes=[b_x[li]])

            for li in tiles:
                banks = [next_bank(6), next_bank(6)]
                for half in range(2):
                    for kc in range(8):
                        S.op("pe", lambda e, kc=kc, li=li, half=half, bank=banks[half]: e.matmul(
                            ps[bank][:, :], lhsT=(oaT if kc < 4 else aT)[:, kc, li * 128:(li + 1) * 128],
                            rhs=wo_sb[:, kc * D + half * 512:kc * D + (half + 1) * 512],
                            start=(kc == 0), stop=(kc == 7)), reads=[b_aT[li], b_oa[li], b_wo], writes=[b_ps[banks[half]]], signal=(kc == 7))
                post_norm_res(li, banks, G1bc, sflags[li])

            if debug and pas == 2:
                for li in tiles:
                    S.dma("sp", lambda e, li=li: e.dma_start(out=d_x1[li * 128:(li + 1) * 128, :], in_=xres[:, li, :]), reads=[b_x[li]], final=True)
                S.dma("sp", lambda e: e.dma_start(out=d_oa, in_=oaT[:]), reads=b_oa, final=True)
                S.dma("sp", lambda e: e.dma_start(out=d_ob, in_=aT[:, 4:8, :]), reads=b_aT, final=True)
            pre_norm(tiles, sflags, Gpf, 24)
            xsr_ok.add((pas, "F"))
            wq_issue(wq_pos[0] + 3)
            if debug and pas == 2:
                S.dma("sp", lambda e: e.dma_start(out=d_h2, in_=aT[:]), reads=b_aT, final=True)

            first_act = [True]
            for j in range(NJ):
                if j % 2 == 0:
                    rg = wring_load(wg_v, (j // 2) * 256)
                    ru = wring_load(wu_v, (j // 2) * 256)
                for gi, grp in enumerate(groups):
                    c0, n, tl, is_s = grp
                    bg = 2 + (mmring[0] % 2) * 2
                    mmring[0] += 1
                    bu = bg + 1
                    fm_matmul(rg, j % 2, grp, bg)
                    fm_matmul(ru, j % 2, grp, bu)
                    if gi == len(groups) - 1:
                        if j % 2 == 1:
                            wq_done()
                        if j >= 4:
                            wd_load(j, ([b_wzv, b_wo, b_uu] + b_vln + b_ada) if j == 4 else [])
                    k = rot("ft")
                    S.op("act", lambda e, bg=bg, k=k, n=n: e.activation(out=ftmp[:, k, 0:n], in_=ps[bg][:, 0:n], func=AF.Tanh, scale=0.5),
                         reads=[b_ps[bg]], writes=[b_ft[k]])
                    S.op("dve", lambda e, bg=bg, k=k, n=n: e.scalar_tensor_tensor(out=ftmp2[:, k, 0:n], in0=ftmp[:, k, 0:n], scalar=1.0, in1=ps[bg][:, 0:n],
                                                                                 op0=ALU.add, op1=ALU.mult), reads=[b_ft[k], b_ps[bg]], writes=[b_ft2[k]])
                    wl = [b_act[j][gi]]
                    if first_act[0]:
                        first_act[0] = False
                        wl = wl + mix_bufs
                    S.op("dve", lambda e, bu=bu, k=k, n=n, j=j, c0=c0: e.scalar_tensor_tensor(out=actT(j, c0, n), in0=ftmp2[:, k, 0:n], scalar=0.5, in1=ps[bu][:, 0:n],
                                                                                              op0=ALU.mult, op1=ALU.mult), reads=[b_ft2[k], b_ps[bu]], writes=wl)

            if pas < 2:
                S.dma("pool", lambda e: e.dma_start(out=wz_i.rearrange("p (k n) -> p k n", k=8), in_=win_v[:, :, 1024:1536]), writes=[b_wzi] + wzi_guard)
            for li in tiles:
                gi = li // 4
                banks = [next_bank(6), next_bank(6)]
                for half in range(2):
                    for j in range(NJ):
                        S.op("pe", lambda e, j=j, li=li, half=half, bank=banks[half]: e.matmul(
                            ps[bank][:, :], lhsT=actT(j, li * 128, 128), rhs=wd_sb(j)[:, half * 512:(half + 1) * 512],
                            start=(j == 0), stop=(j == NJ - 1)), reads=[b_act[j][gi], b_wd[j]], writes=[b_ps[banks[half]]], signal=(j == NJ - 1))
                post_norm_res(li, banks, G2bc, sflags[li])
                dst = ys if sflags[li] else yp_t[gtile0 + li]
                S.dma("sp", lambda e, li=li, dst=dst: e.dma_start(out=dst, in_=xres[:, li, :]), reads=[b_x[li]], final=True)

        S.emit()
    return nc


_NC_CACHE = {}
_IN_MAPS_ONLY = False


def kernel(x_prompt, x_sample, state_hgrn, c_prompt, c_sample, lb_logits, w_ada, b_ada, norm_pre_mix, norm_post_mix,
           w_in, gnorm_w, ln_v_g, ln_v_b, w_spatial, b_spatial, w_out, norm_pre_ffn, norm_post_ffn, w_gate, w_up, w_down):
    f = lambda a: np.ascontiguousarray(np.asarray(a, dtype=np.float32))
    x_prompt, x_sample, state_hgrn, c_prompt, c_sample = map(f, (x_prompt, x_sample, state_hgrn, c_prompt, c_sample))
    vec = np.zeros((128, 128), np.float32)
    vec[R_NPRE:R_NPRE + 8] = f(norm_pre_mix)[0].reshape(8, 128)
    vec[R_NPOST:R_NPOST + 8] = f(norm_post_mix)[0].reshape(8, 128)
    vec[R_FPRE:R_FPRE + 8] = f(norm_pre_ffn)[0].reshape(8, 128)
    vec[R_FPOST:R_FPOST + 8] = f(norm_post_ffn)[0].reshape(8, 128)
    vec[R_BADA:R_BADA + 48] = f(b_ada)[0].reshape(48, 128)
    vec[R_LB0:R_LB0 + 4] = f(lb_logits)[0].reshape(4, 128)
    vec[R_LB1:R_LB1 + 4] = f(lb_logits)[1].reshape(4, 128)
    vec[R_GN] = f(gnorm_w)[0]
    vec[R_LNG:R_LNG + 4] = f(ln_v_g)[0].reshape(4, 128)
    vec[R_LNB:R_LNB + 4] = f(ln_v_b)[0].reshape(4, 128)
    vec[R_BS:R_BS + 4] = f(b_spatial)[0]
    shared = {"vecs": vec, "wspat": f(w_spatial)[0], "w_ada": f(w_ada)[0], "w_in": f(w_in)[0], "w_out": f(w_out)[0],
              "w_gate": f(w_gate)[0], "w_up": f(w_up)[0], "w_down": f(w_down)[0]}
    in_maps = []
    for c in range(NCORES):
        m = dict(shared)
        m["xp"] = x_prompt[c]
        m["xs"] = x_sample[c * NSAMP:(c + 1) * NSAMP].reshape(128, D)
        m["s0"] = state_hgrn[0, c * NSAMP:(c + 1) * NSAMP]
        m["call"] = np.concatenate([c_prompt[c:c + 1], c_sample[c * NSAMP:(c + 1) * NSAMP]], axis=0)
        in_maps.append(m)
    if _IN_MAPS_ONLY:
        return in_maps
    if "nc" not in _NC_CACHE:
        _NC_CACHE["nc"] = build_nc()
    nc = _NC_CACHE["nc"]
    res = run_bass_kernel_spmd(nc, in_maps, core_ids=list(range(NCORES)))
    R = res.results
    y_p = np.stack([R[c]["yp"] for c in range(NCORES)], 0)
    y_s = np.concatenate([R[c]["ys"].reshape(NSAMP, 8, D) for c in range(NCORES)], 0)
    s_p = np.stack([R[c]["o_sp"] for c in range(NCORES)], 0)[None]
    s_s = np.concatenate([R[c]["o_ss"] for c in range(NCORES)], 0)[None]
    v_p = np.stack([R[c]["o_vp"].reshape(128, 4, 128) for c in range(NCORES)], 0)[None]
    v_s = np.concatenate([R[c]["o_vs"].reshape(NSAMP, 8, 4, 128) for c in range(NCORES)], 0)[None]
    return (y_p.astype(np.float32), y_s.astype(np.float32), s_p.astype(np.float32), s_s.astype(np.float32),
            v_p.astype(np.float32), v_s.astype(np.float32))
```

```python
import numpy as np
from contextlib import ExitStack
import concourse.bass as bass
import concourse.mybir as mybir
from concourse.bass_utils import run_bass_kernel_spmd

F32 = mybir.dt.float32
BF16 = mybir.dt.bfloat16
AF = mybir.ActivationFunctionType
ALU = mybir.AluOpType

D = 1024
NCORES = 8
SEQ = 2048
NSAMP = 16
DFF = 2816
NJ = DFF // 128
EPS = 1e-6
TBMAX = 768
NTL = 6


class Buf:
    __slots__ = ("name", "lw", "rd")

    def __init__(self, name):
        self.name = name
        self.lw = None
        self.rd = []


class Sched:
    ENGS = ("pe", "act", "dve", "pool", "sp")

    def __init__(self, nc, es, n_dma_sems=32):
        self.nc = nc
        self.lists = {e: [] for e in self.ENGS}
        self.sems = {}
        for e in ("pe", "act", "dve", "pool"):
            self.sems[e] = es.enter_context(nc.semaphore("s_" + e))
        self.cnt = {e: 0 for e in ("pe", "act", "dve", "pool")}
        self.dsems = [es.enter_context(nc.semaphore("s_dma%d" % i)) for i in range(n_dma_sems)]
        self.dcnt = [0] * n_dma_sems
        self.dnext = 0
        self.dnext_pool = 0
        self.waited = {e: {} for e in self.ENGS}
        self.final_waits = []

    def _sem(self, key):
        return self.sems[key] if isinstance(key, str) else self.dsems[key]

    def _need(self, eng, tk):
        if tk is None:
            return
        key, val = tk
        if key == eng:
            if eng == "pe":
                return
            if val > self.cnt[eng]:
                return
        w = self.waited[eng]
        if w.get(key, 0) >= val:
            return
        w[key] = val
        sem = self._sem(key)
        self.lists[eng].append(lambda e, sem=sem, val=val: e.wait_ge(sem, val))

    def _deps(self, eng, reads, writes):
        for b in reads:
            self._need(eng, b.lw)
        for b in writes:
            self._need(eng, b.lw)
            for t in b.rd:
                self._need(eng, t)

    def op(self, eng, fn, reads=(), writes=(), signal=True):
        self._deps(eng, reads, writes)
        if signal:
            self.cnt[eng] += 1
            tk = (eng, self.cnt[eng])
            sem = self.sems[eng]
            self.lists[eng].append(lambda e, fn=fn, sem=sem: fn(e).then_inc(sem, 1))
        else:
            tk = (eng, self.cnt[eng] + 1)
            self.lists[eng].append(lambda e, fn=fn: fn(e))
        for b in reads:
            b.rd.append(tk)
        for b in writes:
            b.lw = tk
            b.rd = []
        return tk

    def dma(self, eng, fn, reads=(), writes=(), final=False):
        self._deps(eng, reads, writes)
        half = len(self.dsems) // 2
        if eng == "pool":
            k = half + self.dnext_pool
            self.dnext_pool = (self.dnext_pool + 1) % (len(self.dsems) - half)
        else:
            k = self.dnext
            self.dnext = (self.dnext + 1) % half
        if self.dcnt[k] > 0:
            self._need(eng, (k, self.dcnt[k]))
        self.dcnt[k] += 16
        tk = (k, self.dcnt[k])
        sem = self.dsems[k]
        self.lists[eng].append(lambda e, fn=fn, sem=sem: fn(e).then_inc(sem, 16))
        for b in reads:
            b.rd.append(tk)
        for b in writes:
            b.lw = tk
            b.rd = []
        if final:
            self.final_waits.append(tk)
        return tk

    def emit(self):
        for tk in self.final_waits:
            self._need("sp", tk)
        nc = self.nc
        lists = self.lists
        with nc.Block() as block:
            @block.tensor
            def _(e):
                for f in lists["pe"]:
                    f(e)

            @block.scalar
            def _(e):
                for f in lists["act"]:
                    f(e)

            @block.vector
            def _(e):
                for f in lists["dve"]:
                    f(e)

            @block.gpsimd
            def _(e):
                for f in lists["pool"]:
                    f(e)

            @block.sync
            def _(e):
                for f in lists["sp"]:
                    f(e)


def bc(ap, dims):
    return bass.AP(ap.tensor, ap.offset, [ap.ap[0]] + [list(d) for d in dims])


R_NPRE, R_NPOST, R_FPRE, R_FPOST, R_BADA, R_LB0, R_LB1, R_GN, R_LNG, R_LNB, R_BS = 0, 8, 16, 24, 32, 80, 84, 88, 89, 93, 97


def build_nc(debug=False):
    nc = bass.Bass("TRN2", target_bir_lowering=False)
    din = lambda n, s: nc.dram_tensor(n, s, F32, kind="ExternalInput").ap()
    dout = lambda n, s: nc.dram_tensor(n, s, F32, kind="ExternalOutput").ap()
    xp = din("xp", [SEQ, D]); xs = din("xs", [128, D]); s0 = din("s0", [NSAMP, 4, 128, 128])
    call = din("call", [17, D]); vecs = din("vecs", [128, 128]); wspat = din("wspat", [4, 128, 128])
    w_ada = din("w_ada", [D, 6 * D]); w_in = din("w_in", [D, 3072]); w_out = din("w_out", [D, D])
    w_gate = din("w_gate", [D, DFF]); w_up = din("w_up", [D, DFF]); w_down = din("w_down", [DFF, D])
    yp = dout("yp", [SEQ, D]); ys = dout("ys", [128, D]); o_sp = dout("o_sp", [4, 128, 128])
    o_ss = dout("o_ss", [NSAMP, 4, 128, 128]); o_vp = dout("o_vp", [128, 512]); o_vs = dout("o_vs", [128, 512])
    if debug:
        d_x1 = dout("d_x1", [NTL * 128, D])
        d_oa = nc.dram_tensor("d_oa", [128, 4, TBMAX], BF16, kind="ExternalOutput").ap()
        d_ob = nc.dram_tensor("d_ob", [128, 4, TBMAX], BF16, kind="ExternalOutput").ap()
        d_h2 = nc.dram_tensor("d_h2", [128, 8, TBMAX], BF16, kind="ExternalOutput").ap()

    with ExitStack() as es:
        S = Sched(nc, es)
        T = lambda name, shape, dt: es.enter_context(nc.sbuf_tensor(name, shape, dt))

        xres = T("xres", [128, NTL, D], F32);          b_x = [Buf("x%d" % i) for i in range(NTL)]
        aT = T("aT", [128, 8, TBMAX], BF16);           b_aT = [Buf("aT%d" % i) for i in range(NTL)]
        oaT = T("oaT", [128, 4, TBMAX], BF16);         b_oa = [Buf("oa%d" % i) for i in range(NTL)]
        ARENA = 23552
        arena = T("arena", [128, ARENA], BF16)
        b_act = [[Buf("act%d_%d" % (j, g)) for g in range(3)] for j in range(NJ)]
        warena = T("warena", [128, NJ * D], BF16)
        b_wd = [Buf("wd%d" % j) for j in range(NJ)]
        wd_tail = b_wd[16:NJ]
        wring = T("wring", [128, 4, 8, 128], BF16);    b_wring = [Buf("wr%d" % i) for i in range(4)]
        b_ada = [Buf("ada%d" % i) for i in range(2)]
        ident = T("ident", [128, 128], F32);           b_c = Buf("consts")
        identb = T("identb", [128, 128], BF16)
        ones = T("ones", [128, 128], F32)
        Mc = T("Mc", [128, 128], F32); M64 = T("M64", [128, 128], F32); Ms = T("Ms", [128, 128], F32)
        Esel = T("Esel", [128, 16], F32); E16 = T("E16", [16, 128], F32)
        nhalf = T("nhalf", [128, 1], F32)
        vecs_sb = T("vecs_sb", [128, 128], F32);       b_vecs = Buf("vecs")
        colv = T("colv", [128, 128], F32);             b_colv = Buf("colv")
        AB = T("AB", [128, 8], F32);                   b_AB = Buf("AB")
        nB = T("nB", [128, 4], F32)
        gnh = T("gnh", [128, 1], F32)
        st4 = T("st4", [128, 4], F32);                 b_st4 = Buf("st4")
        scT = T("scT", [128, 8, 17], BF16);            b_scT = Buf("scT")
        modT = T("modT", [128, 48, 17], F32);          b_mod = Buf("modT")
        Gpre = T("Gpre", [128, 8, 17], F32); Gpf = T("Gpf", [128, 8, 17], F32)
        G1f = T("G1f", [128, 8, 17], F32); G2f = T("G2f", [128, 8, 17], F32)
        b_G = Buf("Gs")
        Lm = T("Lm", [128, 2, 128], F32);              b_Lm = [Buf("Lm0"), Buf("Lm1")]
        G1bc = T("G1bc", [128, 2, D], F32); G2bc = T("G2bc", [128, 2, D], F32)
        b_Gbc = Buf("Gbc")
        lnbc = T("lnbc", [128, 2, 512], F32)
        bsbc = T("bsbc", [128, 2, 4, 128], F32)
        wsT = T("wsT", [128, 2, 4, 128], BF16);        b_ws = Buf("ws")
        Rep8 = T("Rep8", [8, 128], F32)
        Sst = T("Sst", [128, 4, 128], F32);            b_S = Buf("S")
        Sbf = T("Sbf", [128, 2, 4, 128], BF16);        b_Sbf = [Buf("Sbf0"), Buf("Sbf1")]
        Stmp = T("Stmp", [128, 4, 128], F32);          b_Stmp = Buf("Stmp")
        b_s0 = [Buf("s0_%d" % i) for i in range(4)]
        s0bf = T("s0bf", [128, 2, 4, 128], BF16);      b_s0bf = [Buf("s0bf0"), Buf("s0bf1")]
        b_snew = [Buf("snew0"), Buf("snew1")]
        PL = T("PL", [128, 40, 4], F32);               b_PL = Buf("PL")
        ssq = T("ssq", [128, 64], F32);                b_st = [Buf("st%d" % i) for i in range(64)]
        rstd = T("rstd", [128, 64], F32)
        junk = T("junk", [128, D], BF16);              b_junk = Buf("junk")
        xsr = T("xsr", [128, 2, D], F32);              b_xsr = [Buf("xsr0"), Buf("xsr1")]
        ftmp = T("ftmp", [128, 2, 512], F32);          b_ft = [Buf("ft0"), Buf("ft1")]
        csb = xsr[0:17, 0, :]; cth = xsr[0:17, 1, :]; b_csb = b_xsr[0]; b_cth = b_xsr[1]
        wsraw = xsr[:, 0, 0:512].rearrange("p (h t) -> p h t", h=4); b_wsraw = b_xsr[0]
        wtmp = xsr[:, 0, 512:1024].rearrange("p (h t) -> p h t", h=4); b_wtmp = b_xsr[0]
        s0slot = [xsr[:, 0, 0:512], xsr[:, 1, 0:512], xres[:, 5, 0:512], xres[:, 5, 512:1024]]

        def s0_load(j):
            sl = j % 4
            guards = ([b_xsr[sl]] if sl < 2 else [b_x[5]]) if j < 4 else []
            S.dma("sp", lambda e, j=j, sl=sl: e.dma_start(out=s0slot[sl].rearrange("p (h v) -> p h v", h=4), in_=s0[j].rearrange("h k v -> k h v")),
                  writes=[b_s0[sl]] + guards)
        snew = [xsr[:, r, 512:1024].rearrange("p (h v) -> p h v", h=4) for r in range(2)]
        ftmp2 = T("ftmp2", [128, 2, 512], F32);        b_ft2 = [Buf("ft20"), Buf("ft21")]
        vout = T("vout", [128, 512], F32);             b_vout = Buf("vout")
        bnst = T("bnst", [128, 5, 6], F32); bnag = T("bnag", [128, 5, 4], F32); b_bn = [Buf("bn%d" % i) for i in range(5)]
        b_scm = [Buf("scm0"), Buf("scm1")]; b_ktok = [Buf("kt0"), Buf("kt1")]; b_ktm = [Buf("ktm0"), Buf("ktm1")]
        b_ket = [Buf("ket0"), Buf("ket1")]

        off = [0]

        def carve(n_elems, dt):
            mult = 2 if dt == F32 else 1
            start = off[0]
            off[0] += n_elems * mult
            assert off[0] <= ARENA, off[0]
            reg = arena[:, start:start + n_elems * mult]
            return reg.bitcast(dt) if dt == F32 else reg

        TB = TBMAX
        assert off[0] == 0
        qd_all = carve(4 * TB, BF16)
        kd_all = carve(4 * TB, BF16)
        qdT = [qd_all[:, h * TB:(h + 1) * TB] for h in range(4)]
        kdT = [kd_all[:, h * TB:(h + 1) * TB] for h in range(4)]
        uu = warena[:, 16384:16384 + 4 * TB]
        vln = warena[:, 16384 + 4 * TB:16384 + 4 * TB + NTL * 512]
        sg2 = carve(4 * TB, BF16)
        vtok = carve(NTL * 512, BF16)
        zq = [carve(512, F32) for _ in range(2)]
        thb = [carve(512, F32) for _ in range(2)]
        fb = thb
        kb = [carve(512, F32)] * 2
        Pb = [carve(512, F32)] * 2
        Pinv = [carve(512, F32)] * 2
        Eb = Pinv
        gv = zq
        et1 = thb
        et2 = [kb[0], Pb[0]]
        ket = [carve(512, BF16) for _ in range(2)]
        scm_ = [carve(512, BF16) for _ in range(2)]
        ktok_ = [carve(512, BF16) for _ in range(2)]
        ktm_ = [carve(512, BF16) for _ in range(2)]
        b_qd = [Buf("qd%d" % h) for h in range(4)]; b_kd = [Buf("kd%d" % h) for h in range(4)]
        b_sg = Buf("sg2"); b_uu = Buf("uu")
        b_vtok = [Buf("vtok%d" % i) for i in range(NTL)]; b_vln = [Buf("vln%d" % i) for i in range(NTL)]
        b_zq = [Buf("zq0"), Buf("zq1")]; b_th = [Buf("th0"), Buf("th1")]; b_fb = b_th
        b_kb = [Buf("kb")] * 2; b_Pb = [Buf("Pb")] * 2; b_Pi = [Buf("Pi")] * 2; b_Eb = b_Pi
        b_gv = b_zq; b_e1 = b_th; b_e2 = [b_kb[0], b_Pb[0]]
        mix_bufs = (b_qd + b_kd + [b_sg, b_uu] + b_vtok + b_vln + b_zq + b_th + [b_kb[0], b_Pb[0], b_Pi[0]]
                    + b_scm + b_ktok + b_ktm + b_ket)
        act_bufs = [b for row in b_act for b in row]

        def v4(ap):
            return ap.rearrange("p (h k) -> p h k", h=4)

        assert NJ * TBMAX <= ARENA

        def actT(j, c0, n):
            return arena[:, j * TBMAX + c0: j * TBMAX + c0 + n]

        WZI0 = NJ * TBMAX
        wz_i = arena[:, WZI0:WZI0 + 4096]
        wz_v = warena[:, 4096:8192]
        wo_sb = warena[:, 8192:16384]
        b_wzi = Buf("wzi"); b_wzv = Buf("wzv"); b_wo = Buf("wo")

        def wd_sb(j):
            return warena[:, j * D:(j + 1) * D]

        adaring = arena[:, 0:8192].rearrange("p (r k n) -> p r k n", r=2, k=8)

        ps = [es.enter_context(nc.psum_tensor("ps%d" % i, [128, 512], F32)) for i in range(8)]
        b_ps = [Buf("ps%d" % i) for i in range(8)]
        mmring = [0]

        def next_bank(n=2):
            k = mmring[0] % n
            mmring[0] += 1
            return k

        S.op("pool", lambda e: e.memset(ones[:], 1.0), writes=[b_c])
        S.op("pool", lambda e: e.memset(nhalf[:], -0.5), writes=[b_c])
        S.op("pool", lambda e: e.affine_select(out=ident[:], in_=ones[:], pattern=[[1, 128]], compare_op=ALU.is_equal,
                                               fill=0.0, base=0, channel_multiplier=-1), reads=[b_c], writes=[b_c])
        S.op("pool", lambda e: e.affine_select(out=Mc[:], in_=ones[:], pattern=[[1, 128]], compare_op=ALU.is_ge,
                                               fill=0.0, base=0, channel_multiplier=-1), reads=[b_c], writes=[b_c])
        S.op("pool", lambda e: e.tensor_copy(out=identb[:], in_=ident[:]), reads=[b_c], writes=[b_c])
        S.op("pool", lambda e: e.memset(M64[:], 0.0), writes=[b_c])
        S.op("pool", lambda e: e.tensor_copy(out=M64[0:64, 0:64], in_=Mc[0:64, 0:64]), reads=[b_c], writes=[b_c])
        S.op("pool", lambda e: e.tensor_copy(out=M64[64:128, 64:128], in_=Mc[64:128, 64:128]), reads=[b_c], writes=[b_c])
        S.op("pool", lambda e: e.tensor_copy(out=E16[:].rearrange("p (j i) -> p j i", i=8),
                                             in_=bc(ident[0:16, 0:16], [[1, 16], [0, 8]])), reads=[b_c], writes=[b_c])
        S.op("pool", lambda e: e.tensor_copy(out=Rep8[:].rearrange("p (j i) -> p j i", i=8),
                                             in_=bc(ident[0:8, 0:8], [[0, 16], [1, 8]])), reads=[b_c], writes=[b_c])
        S.op("pe", lambda e: e.matmul(ps[7][:, 0:128], lhsT=E16[:], rhs=E16[:], start=True, stop=True), reads=[b_c], writes=[b_ps[7]])
        S.op("dve", lambda e: e.tensor_tensor(out=Ms[:], in0=ps[7][:, 0:128], in1=Mc[:], op=ALU.mult), reads=[b_ps[7], b_c], writes=[b_c])
        S.op("pe", lambda e: e.transpose(out=ps[7][:, 128:144], in_=E16[:], identity=ident[0:16, 0:16]), reads=[b_c], writes=[b_ps[7]])
        S.op("dve", lambda e: e.tensor_copy(out=Esel[:], in_=ps[7][:, 128:144]), reads=[b_ps[7]], writes=[b_c])

        S.dma("sp", lambda e: e.dma_start(out=vecs_sb[:], in_=vecs), writes=[b_vecs])
        S.op("pe", lambda e: e.transpose(out=ps[6][:, 0:128], in_=vecs_sb[:], identity=ident[:]), reads=[b_vecs, b_c], writes=[b_ps[6]])
        S.op("act", lambda e: e.activation(out=colv[:], in_=ps[6][:, 0:128], func=AF.Copy), reads=[b_ps[6]], writes=[b_colv])
        S.op("dve", lambda e: e.tensor_tensor(out=AB[:, 0:4], in0=colv[:, R_LB0:R_LB0 + 4], in1=colv[:, R_LB1:R_LB1 + 4], op=ALU.subtract),
             reads=[b_colv], writes=[b_AB])
        S.op("act", lambda e: e.activation(out=AB[:, 4:8], in_=AB[:, 0:4], func=AF.Tanh, scale=0.5), reads=[b_AB], writes=[b_AB])
        S.op("dve", lambda e: e.tensor_scalar(out=AB[:, 0:4], in0=AB[:, 4:8], scalar1=0.25, scalar2=0.75, op0=ALU.mult, op1=ALU.add),
             reads=[b_AB], writes=[b_AB])
        S.op("dve", lambda e: e.tensor_scalar(out=nB[:], in0=AB[:, 4:8], scalar1=0.25, scalar2=-0.25, op0=ALU.mult, op1=ALU.add),
             reads=[b_AB], writes=[b_AB])
        S.op("dve", lambda e: e.tensor_scalar(out=AB[:, 4:8], in0=nB[:], scalar1=-1.0, scalar2=None, op0=ALU.mult),
             reads=[b_AB], writes=[b_AB])
        S.op("dve", lambda e: e.tensor_scalar(out=gnh[:], in0=colv[:, R_GN:R_GN + 1], scalar1=0.5, scalar2=None, op0=ALU.mult),
             reads=[b_colv], writes=[b_AB])

        S.dma("sp", lambda e: e.dma_start(out=csb, in_=call), writes=[b_csb])
        S.op("act", lambda e: e.activation(out=cth, in_=csb, func=AF.Tanh, scale=0.5), reads=[b_csb], writes=[b_cth])
        S.op("dve", lambda e: e.tensor_scalar(out=cth, in0=cth, scalar1=0.5, scalar2=0.5, op0=ALU.mult, op1=ALU.add),
             reads=[b_cth], writes=[b_cth])
        S.op("dve", lambda e: e.tensor_tensor(out=cth, in0=cth, in1=csb, op=ALU.mult), reads=[b_cth, b_csb], writes=[b_cth])
        for kc in range(8):
            S.op("pe", lambda e, kc=kc: e.transpose(out=ps[6][:, 128 + kc * 17:128 + (kc + 1) * 17], in_=cth[:, kc * 128:(kc + 1) * 128],
                                                   identity=ident[0:17, 0:17]), reads=[b_cth, b_c], writes=[b_ps[6]])
        S.op("act", lambda e: e.activation(out=scT[:], in_=ps[6][:, 128:128 + 136].rearrange("p (k s) -> p k s", s=17), func=AF.Copy),
             reads=[b_ps[6]], writes=[b_scT])

        wada_v = w_ada.rearrange("(kc p) n -> p kc n", p=128)
        for ch in range(12):
            r = ch % 2
            S.dma("pool", lambda e, ch=ch, r=r: e.dma_start(out=adaring[:, r], in_=wada_v[:, :, ch * 512:(ch + 1) * 512]), writes=[b_ada[r]])
            for jj in range(4):
                j = ch * 4 + jj
                bank = 6 + (j // 24)
                col = (j % 24) * 17
                for kc in range(8):
                    S.op("pe", lambda e, r=r, jj=jj, kc=kc, bank=bank, col=col: e.matmul(
                        ps[bank][:, col:col + 17], lhsT=adaring[:, r, kc, jj * 128:(jj + 1) * 128], rhs=scT[:, kc, :],
                        start=(kc == 0), stop=(kc == 7)), reads=[b_ada[r], b_scT], writes=[b_ps[bank]], signal=(kc == 7))
        for half in range(2):
            S.op("dve", lambda e, half=half: e.tensor_tensor(
                out=modT[:, half * 24:(half + 1) * 24, :], in0=ps[6 + half][:, 0:408].rearrange("p (j s) -> p j s", s=17),
                in1=bc(colv[:, R_BADA + half * 24:R_BADA + half * 24 + 24], [[1, 24], [0, 17]]), op=ALU.add),
                reads=[b_ps[6 + half], b_colv], writes=[b_mod])
        for (dst, mo, ro) in ((Gpre, 8, R_NPRE), (Gpf, 32, R_FPRE)):
            S.op("dve", lambda e, dst=dst, mo=mo: e.tensor_scalar(out=dst[:], in0=modT[:, mo:mo + 8, :], scalar1=1.0, scalar2=None, op0=ALU.add),
                 reads=[b_mod], writes=[b_G])
            S.op("dve", lambda e, dst=dst, ro=ro: e.tensor_tensor(out=dst[:], in0=dst[:], in1=bc(colv[:, ro:ro + 8], [[1, 8], [0, 17]]), op=ALU.mult),
                 reads=[b_G, b_colv], writes=[b_G])
        for (dst, mo, ro) in ((G1f, 16, R_NPOST), (G2f, 40, R_FPOST)):
            S.op("dve", lambda e, dst=dst, mo=mo, ro=ro: e.tensor_tensor(out=dst[:], in0=modT[:, mo:mo + 8, :],
                                                                          in1=bc(colv[:, ro:ro + 8], [[1, 8], [0, 17]]), op=ALU.mult),
                 reads=[b_mod, b_colv], writes=[b_G])

        lmi = [0]

        def bcast_block(src_fn, dst_ap, dst_buf, bank, col, extra_reads=()):
            r = lmi[0] % 2
            lmi[0] += 1
            S.op("dve", lambda e, r=r: src_fn(e, Lm[:, r, :]), reads=list(extra_reads), writes=[b_Lm[r]])
            S.op("pe", lambda e, r=r: e.matmul(ps[bank][:, col:col + 128], lhsT=Lm[:, r, :], rhs=ident[:], start=True, stop=True),
                 reads=[b_Lm[r], b_c], writes=[b_ps[bank]])

        for (Gf, Gb) in ((G1f, G1bc), (G2f, G2bc)):
            for smp in range(2):
                for half in range(2):
                    bank = 6 + half
                    for cc in range(4):
                        c = half * 4 + cc
                        if smp == 0:
                            fn = lambda e, Lo, Gf=Gf, c=c: e.tensor_copy(out=Lo, in_=bc(Gf[:, c, 0:1], [[0, 128]]))
                        else:
                            fn = lambda e, Lo, Gf=Gf, c=c: e.tensor_copy(out=Lo.rearrange("p (j i) -> p j i", i=8),
                                                                        in_=bc(Gf[:, c, 1:17], [[1, 16], [0, 8]]))
                        bcast_block(fn, None, None, bank, cc * 128, extra_reads=[b_G])
                    S.op("act", lambda e, Gb=Gb, smp=smp, half=half, bank=bank: e.activation(
                        out=Gb[:, smp, half * 512:(half + 1) * 512], in_=ps[bank][:, :], func=AF.Copy), reads=[b_ps[bank]], writes=[b_Gbc])
        for which in range(2):
            for cc in range(4):
                row = (R_LNG if which == 0 else R_LNB) + cc
                fn = lambda e, Lo, row=row: e.tensor_copy(out=Lo, in_=bc(colv[:, row:row + 1], [[0, 128]]))
                bcast_block(fn, None, None, 6, cc * 128, extra_reads=[b_colv])
            S.op("act", lambda e, which=which: e.activation(out=lnbc[:, which, :], in_=ps[6][:, :], func=AF.Copy), reads=[b_ps[6]], writes=[b_Gbc])
        for h in range(4):
            fn = lambda e, Lo, h=h: e.tensor_copy(out=Lo, in_=bc(colv[:, R_BS + h:R_BS + h + 1], [[0, 128]]))
            bcast_block(fn, None, None, 7, h * 128, extra_reads=[b_colv])
        S.op("act", lambda e: e.activation(out=bsbc[:, 0].rearrange("p h t -> p (h t)"), in_=ps[7][:, :], func=AF.Copy), reads=[b_ps[7]], writes=[b_Gbc])
        S.op("dve", lambda e: e.tensor_copy(out=bsbc[:, 1].rearrange("p h (j i) -> p h j i", i=8),
                                            in_=bc(bsbc[:, 0, 0, 0:8], [[128, 4], [0, 16], [1, 8]])), reads=[b_Gbc], writes=[b_Gbc])

        S.dma("sp", lambda e: e.dma_start(out=wsraw, in_=wspat.rearrange("h t s -> t h s")), writes=[b_wsraw])
        for h in range(4):
            S.op("pe", lambda e, h=h: e.transpose(out=ps[6][:, h * 128:(h + 1) * 128], in_=wsraw[:, h, :], identity=ident[:]),
                 reads=[b_wsraw, b_c], writes=[b_ps[6]])
        S.op("act", lambda e: e.activation(out=xsr[:, 0, 512:1024], in_=ps[6][:, :], func=AF.Copy), reads=[b_ps[6]], writes=[b_wtmp])
        S.op("dve", lambda e: e.tensor_tensor(out=wsT[:, 0], in0=wtmp, in1=bc(Mc[:, :], [[0, 4], [1, 128]]), op=ALU.mult),
             reads=[b_wtmp, b_c], writes=[b_ws])
        S.op("dve", lambda e: e.tensor_copy(out=xsr[0:8, 1, 0:512].rearrange("p (h j i) -> p h j i", h=4, i=8),
                                            in_=bc(xsr[0:8, 0, 512:520], [[128, 4], [0, 16], [1, 8]])), reads=[b_wtmp], writes=[b_xsr[1]])
        S.op("pe", lambda e: e.matmul(ps[7][:, :], lhsT=Rep8[:], rhs=xsr[0:8, 1, 0:512], start=True, stop=True),
             reads=[b_xsr[1], b_c], writes=[b_ps[7]])
        S.op("dve", lambda e: e.tensor_tensor(out=wsT[:, 1], in0=ps[7][:, :].rearrange("p (h t) -> p h t", h=4),
                                              in1=bc(Ms[:, :], [[0, 4], [1, 128]]), op=ALU.mult), reads=[b_ps[7], b_c], writes=[b_ws])
        S.op("pool", lambda e: e.memset(Sst[:], 0.0), writes=[b_S])
        S.op("pool", lambda e: e.memset(Sbf[:, 0], 0.0), writes=[b_Sbf[0]])

        xp_t = xp.rearrange("(n p) d -> n p d", p=128)
        yp_t = yp.rearrange("(n p) d -> n p d", p=128)
        win_v = w_in.rearrange("(kc p) n -> p kc n", p=128)
        wout_v = w_out.rearrange("(kc p) n -> p kc n", p=128)
        wg_v = w_gate.rearrange("(kc p) n -> p kc n", p=128)
        wu_v = w_up.rearrange("(kc p) n -> p kc n", p=128)
        ssq_i = [0]
        po_i = [0]
        wr_i = [0]
        rr = {"xsr": 0, "ys": 0, "ft": 0, "zq": 0, "th": 0, "scm": 0, "kt": 0, "gv": 0, "e": 0}

        def rot(name, n=2):
            v = rr[name] % n
            rr[name] += 1
            return v

        def pre_norm(tiles, sample_flags, Gt, sh_off, hook=None):
            base = (ssq_i[0] % 6) * 8
            ssq_i[0] += 1
            nt = len(tiles)
            stb = [b_st[base + i] for i in range(nt)]
            for i, li in enumerate(tiles):
                col = base + i
                S.op("act", lambda e, li=li, col=col: e.activation(out=junk[:], in_=xres[:, li, :], func=AF.Square,
                                                                  accum_out=ssq[:, col:col + 1]), reads=[b_x[li]], writes=[b_st[col], b_junk])
            S.op("dve", lambda e: e.tensor_scalar(out=rstd[:, base:base + nt], in0=ssq[:, base:base + nt], scalar1=1.0 / D, scalar2=EPS,
                                                  op0=ALU.mult, op1=ALU.add), reads=stb, writes=stb)
            S.op("pool", lambda e: e.tensor_tensor(out=rstd[:, base:base + nt], in0=rstd[:, base:base + nt], in1=bc(nhalf[:, 0:1], [[0, nt]]), op=ALU.pow),
                 reads=stb + [b_c], writes=stb)
            slots = {}

            def scale_tile(i):
                li = tiles[i]
                col = base + i
                r = rot("xsr")
                slots[i] = r
                S.op("dve", lambda e, li=li, r=r, col=col: e.tensor_scalar(out=xsr[:, r, :], in0=xres[:, li, :], scalar1=rstd[:, col:col + 1],
                                                                          scalar2=None, op0=ALU.mult),
                     reads=[b_x[li], b_st[col]], writes=[b_xsr[r], b_s0[r], b_snew[r]])

            scale_tile(0)
            for i, (li, is_s) in enumerate(zip(tiles, sample_flags)):
                r = slots[i]
                for half in range(2):
                    bank = next_bank(6)
                    for cc in range(4):
                        c = half * 4 + cc
                        S.op("pe", lambda e, r=r, c=c, cc=cc, bank=bank: e.transpose(out=ps[bank][:, cc * 128:(cc + 1) * 128],
                                                                                   in_=xsr[:, r, c * 128:(c + 1) * 128], identity=ident[:]),
                             reads=[b_xsr[r], b_c], writes=[b_ps[bank]], signal=(cc == 3))
                    if half == 0 and i + 1 < nt:
                        scale_tile(i + 1)
                    k = rot("ft")
                    if not is_s:
                        g_ap = bc(Gt[:, half * 4, 0:1], [[17, 4], [0, 128]])
                        s_ap = bc(modT[:, sh_off + half * 4, 0:1], [[17, 4], [0, 128]])
                        pat = "p (c t) -> p c t"
                        kw = dict(c=4)
                        opat = "p c t -> p c t"
                    else:
                        g_ap = bc(Gt[:, half * 4, 1:17], [[17, 4], [1, 16], [0, 8]])
                        s_ap = bc(modT[:, sh_off + half * 4, 1:17], [[17, 4], [1, 16], [0, 8]])
                        pat = "p (c j i) -> p c j i"
                        kw = dict(c=4, i=8)
                        opat = "p c (j i) -> p c j i"
                    S.op("dve", lambda e, bank=bank, k=k, g_ap=g_ap, pat=pat, kw=kw: e.tensor_tensor(
                        out=ftmp[:, k, :].rearrange(pat, **kw), in0=ps[bank][:, :].rearrange(pat, **kw), in1=g_ap, op=ALU.mult),
                        reads=[b_ps[bank], b_G], writes=[b_ft[k]])
                    okw = dict(i=8) if is_s else {}
                    S.op("pool", lambda e, half=half, k=k, li=li, s_ap=s_ap, pat=pat, kw=kw, opat=opat, okw=okw: e.tensor_tensor(
                        out=aT[:, half * 4:half * 4 + 4, li * 128:(li + 1) * 128].rearrange(opat, **okw) if okw else aT[:, half * 4:half * 4 + 4, li * 128:(li + 1) * 128],
                        in0=ftmp[:, k, :].rearrange(pat, **kw), in1=s_ap, op=ALU.add),
                        reads=[b_ft[k], b_mod], writes=[b_aT[li]])
                if hook is not None:
                    hook(i)

        WQ = []
        WQ_tag = []
        for _p in range(3):
            for hp in range(2):
                WQ += [(win_v, hp * 256), (win_v, 512 + hp * 256)]
            WQ += [(win_v, 1536 + hp * 256) for hp in range(2)]
            WQ += [(win_v, 2048 + hp * 256) for hp in range(2)]
            WQ_tag += [(_p, "B")] * 6 + [(_p, "B") if _p < 2 else (_p, "U")] * 2
            for jp in range(NJ // 2):
                WQ += [(wg_v, jp * 256), (wu_v, jp * 256)]
            WQ_tag += [(_p, "F")] * NJ
        xsr_ok = set()
        wq_pos = [0]
        wq_issued = [0]
        wq_restrict = [True]
        wq_rr = [0]
        wring_flat = wring[:].rearrange("p a k n -> p (a k n)")
        wslot = [wring_flat[:, 0:2048], wring_flat[:, 2048:4096], xsr[:, 0, :].bitcast(BF16), xsr[:, 1, :].bitcast(BF16)]
        wslot_r = [b_wring[0], b_wring[1], b_xsr[0], b_xsr[1]]
        wslot_w = [[b_wring[0]], [b_wring[1]], [b_xsr[0], b_s0[0], b_snew[0]], [b_xsr[1], b_s0[1], b_snew[1]]]
        slot_owner = [None] * 4
        chunk_slot = {}
        consumed = set()

        def wq_issue(upto):
            while wq_issued[0] <= min(upto, len(WQ) - 1):
                cands = [0, 1, 2, 3] if WQ_tag[wq_issued[0]] in xsr_ok else [0, 1]
                free = [sl for sl in cands if slot_owner[sl] is None or slot_owner[sl] in consumed]
                if not free:
                    return
                free.sort(key=lambda sl: -1 if slot_owner[sl] is None else slot_owner[sl])
                sl = free[0]
                k = wq_issued[0]
                wq_issued[0] += 1
                slot_owner[sl] = k
                chunk_slot[k] = sl
                src_v, col0 = WQ[k]
                S.dma("pool", lambda e, sl=sl, src_v=src_v, col0=col0: e.dma_start(out=wslot[sl].rearrange("p (k n) -> p k n", k=8),
                                                                                  in_=src_v[:, :, col0:col0 + 256]),
                      writes=wslot_w[sl])

        def wring_load(src_v, col0):
            k = wq_pos[0]
            wq_pos[0] += 1
            assert WQ[k][1] == col0 and WQ[k][0] is src_v, (k, col0)
            wq_issue(k)
            assert k in chunk_slot, "weight chunk could not be issued (no free staging slot)"
            return k

        def wq_done():
            for k in range(wq_pos[0]):
                consumed.add(k)
            wq_issue(wq_pos[0] - 1 + 4)

        def fm_matmul(k, sub, grp, bank):
            c0, n, tl, _s = grp
            sl = chunk_slot[k]
            for kc in range(8):
                S.op("pe", lambda e, kc=kc: e.matmul(ps[bank][:, 0:n], lhsT=wslot[sl][:, kc * 256 + sub * 128:kc * 256 + (sub + 1) * 128],
                                                    rhs=aT[:, kc, c0:c0 + n], start=(kc == 0), stop=(kc == 7)),
                     reads=[wslot_r[sl]] + [b_aT[t] for t in tl], writes=[b_ps[bank]], signal=(kc == 7))

        PASS_TILES = [6, 6, 5]
        for pas in range(3):
            ntile = PASS_TILES[pas]
            tiles = list(range(ntile))
            sflags = [False] * ntile
            if pas == 2:
                sflags[4] = True
                groups = [(0, 512, [0, 1, 2, 3], False), (512, 128, [4], True)]
            else:
                groups = [(0, 512, [0, 1, 2, 3], False), (512, 256, [4, 5], False)]
            gtile0 = pas * 6

            for li in tiles:
                src = xs if sflags[li] else xp_t[gtile0 + li]
                S.dma("sp", lambda e, li=li, src=src: e.dma_start(out=xres[:, li, :], in_=src), writes=[b_x[li]])
            wzi_guard = [b_kb[0], b_Pb[0], b_Pi[0]] + b_ket + b_scm
            if pas == 0:
                S.dma("pool", lambda e: e.dma_start(out=wz_i.rearrange("p (k n) -> p k n", k=8), in_=win_v[:, :, 1024:1536]), writes=[b_wzi] + wzi_guard)
            S.dma("pool", lambda e: e.dma_start(out=wz_v.rearrange("p (k n) -> p k n", k=8), in_=win_v[:, :, 2560:3072]),
                  writes=[b_wzv] + b_wd[4:8] + (b_ada if pas == 0 else []))
            S.dma("pool", lambda e: e.dma_start(out=wo_sb.rearrange("p (k n) -> p k n", k=8), in_=wout_v), writes=[b_wo] + b_wd[8:16])

            def wd_load(j, guards):
                S.dma("pool", lambda e, j=j: e.dma_start(out=wd_sb(j), in_=w_down[j * 128:(j + 1) * 128, :]), writes=[b_wd[j]] + list(guards))

            for j in range(0, 4):
                wd_load(j, b_ada if pas == 0 else [])
            wq_issue(wq_pos[0] + 3)
            first_mix = [True]

            def mixw(bufs):
                if first_mix[0]:
                    first_mix[0] = False
                    return list(bufs) + act_bufs + b_ada
                return list(bufs)

            def zi_tile(li):
                bank = 6 + (li % 2)
                for kc in range(8):
                    S.op("pe", lambda e, kc=kc, li=li, bank=bank: e.matmul(ps[bank][:, :], lhsT=aT[:, kc, li * 128:(li + 1) * 128],
                                                                         rhs=wz_i[:, kc * 512:(kc + 1) * 512], start=(kc == 0), stop=(kc == 7)),
                         reads=[b_aT[li], b_wzi], writes=[b_ps[bank]], signal=(kc == 7))
                S.op("act", lambda e, li=li, bank=bank: e.activation(out=vtok[:, li * 512:(li + 1) * 512], in_=ps[bank][:, :], func=AF.Copy),
                     reads=[b_ps[bank]], writes=mixw([b_vtok[li]]))

            pre_norm(tiles, sflags, Gpre, 0, hook=lambda i: zi_tile(tiles[i - 1]) if i >= 1 else None)
            zi_tile(tiles[-1])
            xsr_ok.add((pas, "B"))
            wq_issue(wq_pos[0] + 3)

            S.op("pool", lambda e: e.memset(vout[:], 0.0), writes=[b_vout])
            for h in range(4):
                if h % 2 == 0:
                    rq = wring_load(win_v, (h // 2) * 256)
                    rf = wring_load(win_v, 512 + (h // 2) * 256)
                for gi, grp in enumerate(groups):
                    c0, n, tl, is_s = grp
                    bq = next_bank()
                    fm_matmul(rq, h % 2, grp, bq)
                    zi_ = rot("zq")
                    S.op("act", lambda e, bq=bq, zi_=zi_, n=n: e.activation(out=zq[zi_][:, 0:n], in_=ps[bq][:, 0:n], func=AF.Copy),
                         reads=[b_ps[bq]], writes=[b_zq[zi_]])
                    bf_ = next_bank()
                    fm_matmul(rf, h % 2, grp, bf_)
                    if gi == len(groups) - 1 and h % 2 == 1:
                        wq_done()
                    ti = rot("th")
                    S.op("act", lambda e, bf_=bf_, ti=ti, n=n: e.activation(out=thb[ti][:, 0:n], in_=ps[bf_][:, 0:n], func=AF.Tanh, scale=0.5),
                         reads=[b_ps[bf_]], writes=[b_th[ti]])
                    S.op("act", lambda e, ti=ti, n=n, h=h: e.activation(out=fb[ti][:, 0:n], in_=thb[ti][:, 0:n], func=AF.Identity,
                                                                       scale=AB[:, 4 + h:5 + h], bias=AB[:, h:h + 1]),
                         reads=[b_th[ti], b_AB], writes=[b_fb[ti]])
                    S.op("pool", lambda e, ti=ti, n=n: e.tensor_scalar(out=kb[ti][:, 0:n], in0=fb[ti][:, 0:n], scalar1=-1.0,
                                                                      scalar2=1.0, op0=ALU.mult, op1=ALU.add),
                         reads=[b_fb[ti]], writes=[b_kb[ti]])
                    csz = 8 if is_s else 64
                    nch = n // csz
                    S.op("act", lambda e, ti=ti, csz=csz, nch=nch: e.activation(out=bc(vout[:, 0:1], [[csz, nch]]), in_=bc(fb[ti][:, 0:1], [[csz, nch]]), func=AF.Copy),
                         reads=[b_fb[ti]], writes=[b_vout])
                    S.op("dve", lambda e, ti=ti, n=n: e.tensor_tensor_scan(out=Pb[ti][:, 0:n], data0=fb[ti][:, 0:n], data1=vout[:, 0:n],
                                                                          initial=1.0, op0=ALU.mult, op1=ALU.max), reads=[b_fb[ti], b_vout], writes=[b_Pb[ti]])
                    if is_s:
                        S.op("pool", lambda e: e.memset(vout[:, 0:128], 0.0), writes=[b_vout])
                    pl0 = 16 if is_s else gi * 8
                    S.op("pool", lambda e, ti=ti, csz=csz, nch=nch, pl0=pl0, h=h: e.tensor_copy(
                        out=PL[:, pl0:pl0 + nch, h], in_=bc(Pb[ti][:, csz - 1:csz], [[csz, nch]])), reads=[b_Pb[ti]], writes=[b_PL])
                    S.op("dve", lambda e, ti=ti, n=n: e.reciprocal(out=Pinv[ti][:, 0:n], in_=Pb[ti][:, 0:n]), reads=[b_Pb[ti]], writes=[b_Pi[ti]])
                    S.op("pool", lambda e, ti=ti, zi_=zi_, n=n, c0=c0, h=h: e.tensor_tensor(out=qdT[h][:, c0:c0 + n], in0=zq[zi_][:, 0:n],
                                                                                          in1=Pb[ti][:, 0:n], op=ALU.mult),
                         reads=[b_zq[zi_], b_Pb[ti]], writes=[b_qd[h]])
                    S.op("pool", lambda e, ti=ti, n=n, c0=c0, h=h: e.tensor_tensor(out=kdT[h][:, c0:c0 + n], in0=kb[ti][:, 0:n],
                                                                                  in1=Pinv[ti][:, 0:n], op=ALU.mult),
                         reads=[b_kb[ti], b_Pi[ti]], writes=[b_kd[h]])

            for h in range(4):
                if h % 2 == 0:
                    rg = wring_load(win_v, 1536 + (h // 2) * 256)
                for gi, grp in enumerate(groups):
                    c0, n, tl, is_s = grp
                    bg = next_bank()
                    fm_matmul(rg, h % 2, grp, bg)
                    if gi == len(groups) - 1 and h % 2 == 1:
                        wq_done()
                    ti = rot("th")
                    S.op("act", lambda e, bg=bg, ti=ti, n=n: e.activation(out=thb[ti][:, 0:n], in_=ps[bg][:, 0:n], func=AF.Tanh, scale=0.5),
                         reads=[b_ps[bg]], writes=[b_th[ti]])
                    S.op("dve", lambda e, bg=bg, ti=ti, n=n, h=h, c0=c0: e.scalar_tensor_tensor(
                        out=sg2[:, h * TB + c0:h * TB + c0 + n], in0=thb[ti][:, 0:n], scalar=1.0, in1=ps[bg][:, 0:n], op0=ALU.add, op1=ALU.mult),
                        reads=[b_th[ti], b_ps[bg]], writes=[b_sg])

            tstate = {}

            def front(li):
                is_s = sflags[li]
                c0 = li * 128
                mask = Ms if is_s else M64
                for h in range(4):
                    S.op("pe", lambda e, h=h, c0=c0: e.matmul(ps[2][:, h * 128:(h + 1) * 128], lhsT=kdT[h][:, c0:c0 + 128], rhs=qdT[h][:, c0:c0 + 128],
                                                            start=True, stop=True), reads=[b_kd[h], b_qd[h]], writes=[b_ps[2]], signal=(h == 3))
                si = rot("scm")
                S.op("dve", lambda e, si=si, mask=mask: e.tensor_tensor(out=v4(scm_[si]), in0=ps[2][:, :].rearrange("p (h t) -> p h t", h=4),
                                                                       in1=bc(mask[:, :], [[0, 4], [1, 128]]), op=ALU.mult),
                     reads=[b_ps[2], b_c], writes=[b_scm[si]])
                ke_i = rot("kt")
                if not is_s:
                    S.op("pool", lambda e, ke_i=ke_i, c0=c0, li=li: e.tensor_tensor(
                        out=ket[ke_i].rearrange("p (h c i) -> p h c i", h=4, i=64), in0=bc(kd_all[:, c0:c0 + 1], [[TB, 4], [64, 2], [1, 64]]),
                        in1=bc(PL[:, li * 2, 0:1], [[1, 4], [4, 2], [0, 64]]), op=ALU.mult), reads=b_kd + [b_PL], writes=[b_ket[ke_i]])
                else:
                    S.op("pool", lambda e, ke_i=ke_i, c0=c0: e.tensor_tensor(
                        out=ket[ke_i].rearrange("p (h c i) -> p h c i", h=4, i=8), in0=bc(kd_all[:, c0:c0 + 1], [[TB, 4], [8, 16], [1, 8]]),
                        in1=bc(PL[:, 16, 0:1], [[1, 4], [4, 16], [0, 8]]), op=ALU.mult), reads=b_kd + [b_PL], writes=[b_ket[ke_i]])
                tstate[li] = {"si": si, "ki": ke_i, "ob": 4 + (li % 2)}

            def frontB(li):
                ke_i = tstate[li]["ki"]
                psk = ps[3].bitcast(BF16)
                for h in range(4):
                    S.op("pe", lambda e, h=h, ke_i=ke_i, psk=psk: e.transpose(out=psk[:, h * 128:(h + 1) * 128], in_=ket[ke_i][:, h * 128:(h + 1) * 128],
                                                                            identity=identb[:]),
                         reads=[b_ket[ke_i], b_c], writes=[b_ps[3]], signal=(h == 3))
                S.op("act", lambda e, ki=ke_i, psk=psk: e.activation(out=ktok_[ki], in_=psk[:, 0:512], func=AF.Copy),
                     reads=[b_ps[3]], writes=[b_ktok[ke_i]])

            def ds_and_update(li, c, nxt):
                ki = tstate[li]["ki"]
                chunk = (li * 2 + c)
                S.op("dve", lambda e, chunk=chunk: e.tensor_tensor(out=Stmp[:], in0=Sst[:], in1=bc(PL[:, chunk, 0:1], [[1, 4], [0, 128]]), op=ALU.mult),
                     reads=[b_S, b_PL], writes=[b_Stmp])
                for h in range(4):
                    S.op("pe", lambda e, h=h, c=c, li=li, ki=ki: e.matmul(
                        ps[6][:, h * 128:(h + 1) * 128], lhsT=ktok_[ki][c * 64:(c + 1) * 64, h * 128:(h + 1) * 128],
                        rhs=vtok[c * 64:(c + 1) * 64, li * 512 + h * 128:li * 512 + (h + 1) * 128], start=True, stop=True),
                        reads=[b_ktok[ki], b_vtok[li]], writes=[b_ps[6]], signal=(h == 3))
                S.op("dve", lambda e, nxt=nxt: e.tensor_tensor(out=Sbf[:, nxt], in0=Stmp[:], in1=ps[6][:, :].rearrange("p (h v) -> p h v", h=4), op=ALU.add),
                     reads=[b_Stmp, b_ps[6]], writes=[b_Sbf[nxt]])
                S.op("dve", lambda e: e.tensor_tensor(out=Sst[:], in0=Stmp[:], in1=ps[6][:, :].rearrange("p (h v) -> p h v", h=4), op=ALU.add),
                     reads=[b_Stmp, b_ps[6]], writes=[b_S])

            def intra(li, h, stop):
                st = tstate[li]
                si, ob = st["si"], st["ob"]
                S.op("pe", lambda e, h=h, li=li, si=si, ob=ob, stop=stop: e.matmul(
                    ps[ob][:, h * 128:(h + 1) * 128], lhsT=vtok[:, li * 512 + h * 128:li * 512 + (h + 1) * 128],
                    rhs=scm_[si][:, h * 128:(h + 1) * 128], start=True, stop=stop, skip_group_check=True),
                    reads=[b_vtok[li], b_scm[si]], writes=[b_ps[ob]], signal=stop)

            def chainA(li):
                if not sflags[li]:
                    ds_and_update(li, 0, 1)

            def chainB(li):
                st = tstate[li]
                si, ki, ob = st["si"], st["ki"], st["ob"]
                c0 = li * 128
                if not sflags[li]:
                    for h in range(4):
                        intra(li, h, False)
                        for c in range(2):
                            S.op("pe", lambda e, h=h, c=c, c0=c0, ob=ob: e.matmul(
                                ps[ob][:, h * 128 + c * 64:h * 128 + (c + 1) * 64], lhsT=Sbf[:, c, h, :],
                                rhs=qdT[h][:, c0 + c * 64:c0 + (c + 1) * 64], start=False, stop=True, skip_group_check=True),
                                reads=[b_Sbf[c], b_qd[h]], writes=[b_ps[ob]], signal=(h == 3 and c == 1))
                    ds_and_update(li, 1, 0)
                    if pas == 2 and li == 3:
                        S.dma("sp", lambda e: e.dma_start(out=o_sp.rearrange("h k v -> k h v"), in_=Sst[:]), reads=[b_S], final=True)
                    st["o_src"], st["o_buf"] = ps[ob][:, :], b_ps[ob]
                else:
                    for h in range(4):
                        intra(li, h, True)
                    for j in range(NSAMP):
                        r = j % 2
                        sl = j % 4
                        db = 6 if r == 0 else 3
                        S.op("act", lambda e, r=r, sl=sl: e.activation(out=s0bf[:, r].rearrange("p h v -> p (h v)"), in_=s0slot[sl],
                                                                      func=AF.Copy), reads=[b_s0[sl]], writes=[b_s0bf[r]])
                        for h in range(4):
                            S.op("pe", lambda e, h=h, j=j, r=r, c0=c0: e.matmul(
                                ps[2][:, h * 128 + j * 8:h * 128 + (j + 1) * 8], lhsT=s0bf[:, r, h, :], rhs=qdT[h][:, c0 + j * 8:c0 + (j + 1) * 8],
                                start=True, stop=True, skip_group_check=True), reads=[b_s0bf[r], b_qd[h]], writes=[b_ps[2]],
                                signal=(h == 3))
                        S.op("dve", lambda e, j=j, r=r, ki=ki: e.tensor_scalar(out=ktm_[r], in0=ktok_[ki],
                                                                              scalar1=Esel[:, j:j + 1], scalar2=None, op0=ALU.mult),
                             reads=[b_ktok[ki], b_c], writes=[b_ktm[r]])
                        for h in range(4):
                            S.op("pe", lambda e, h=h, r=r, li=li, db=db: e.matmul(ps[db][:, h * 128:(h + 1) * 128], lhsT=ktm_[r][:, h * 128:(h + 1) * 128],
                                                                                 rhs=vtok[:, li * 512 + h * 128:li * 512 + (h + 1) * 128], start=True, stop=True),
                                 reads=[b_ktm[r], b_vtok[li]], writes=[b_ps[db]], signal=(h == 3))
                        S.op("dve", lambda e, j=j, r=r, sl=sl: e.tensor_tensor(out=snew[r], in0=s0slot[sl].rearrange("p (h v) -> p h v", h=4),
                                                                              in1=bc(PL[:, 16 + j, 0:1], [[1, 4], [0, 128]]), op=ALU.mult),
                             reads=[b_s0[sl], b_PL], writes=[b_snew[r]] + ([b_xsr[r]] if j < 2 else []))
                        if j + 4 < NSAMP:
                            s0_load(j + 4)
                        S.op("dve", lambda e, r=r, db=db: e.tensor_tensor(out=snew[r], in0=snew[r], in1=ps[db][:, :].rearrange("p (h v) -> p h v", h=4), op=ALU.add),
                             reads=[b_snew[r], b_ps[db]], writes=[b_snew[r]])
                        S.dma("sp", lambda e, j=j, r=r: e.dma_start(out=o_ss[j].rearrange("h k v -> k h v"), in_=snew[r]), reads=[b_snew[r]], final=True)
                    S.op("act", lambda e: e.activation(out=ftmp2[:, 0, :], in_=ps[2][:, :], func=AF.Copy), reads=[b_ps[2]], writes=[b_ft2[0]])
                    S.op("dve", lambda e, ob=ob: e.tensor_tensor(out=ftmp2[:, 0, :], in0=ps[ob][:, :], in1=ftmp2[:, 0, :], op=ALU.add),
                         reads=[b_ps[ob], b_ft2[0]], writes=[b_ft2[0]])
                    st["o_src"], st["o_buf"] = ftmp2[:, 0, :], b_ft2[0]

            def epi(li):
                st = tstate[li]
                o_src, o_buf = st["o_src"], st["o_buf"]
                c0 = li * 128
                e1 = rot("e")
                S.op("act", lambda e, o_src=o_src, e1=e1: e.activation(out=et1[e1][:, :], in_=o_src, func=AF.Square), reads=[o_buf], writes=[b_e1[e1]])
                for h in range(4):
                    S.op("pe", lambda e, e1=e1, h=h: e.matmul(ps[7][:, h:h + 1], lhsT=et1[e1][:, h * 128:(h + 1) * 128], rhs=ones[:, 0:1],
                                                            start=True, stop=True), reads=[b_c, b_e1[e1]], writes=[b_ps[7]], signal=(h == 3))
                S.op("dve", lambda e: e.tensor_scalar(out=st4[:], in0=ps[7][:, 0:4], scalar1=1.0 / 128, scalar2=EPS, op0=ALU.mult, op1=ALU.add),
                     reads=[b_ps[7]], writes=[b_st4])
                S.op("pool", lambda e: e.tensor_tensor(out=st4[:], in0=st4[:], in1=bc(nhalf[:, 0:1], [[0, 4]]), op=ALU.pow),
                     reads=[b_st4, b_c], writes=[b_st4])
                S.op("pool", lambda e, e1=e1: e.tensor_tensor(out=et2[e1][:, :].rearrange("p (h t) -> p h t", h=4), in0=bc(ident[:, :], [[0, 4], [1, 128]]),
                                                              in1=bc(st4[:, 0:4], [[1, 4], [0, 128]]), op=ALU.mult), reads=[b_st4, b_c], writes=[b_e2[e1]])
                st["e1"] = e1

            def epiB(li):
                st = tstate[li]
                o_src, o_buf = st["o_src"], st["o_buf"]
                c0 = li * 128
                e1 = st["e1"]
                S.op("pe", lambda e, e1=e1: e.matmul(ps[7][:, :], lhsT=ones[:], rhs=et2[e1][:, :], start=True, stop=True),
                     reads=[b_c, b_e2[e1]], writes=[b_ps[7]])
                S.op("act", lambda e, e1=e1: e.activation(out=et2[e1][:, :], in_=ps[7][:, :], func=AF.Copy), reads=[b_ps[7]], writes=[b_e2[e1]])
                S.op("dve", lambda e, e1=e1, o_src=o_src: e.tensor_tensor(out=et1[e1][:, :], in0=o_src, in1=et2[e1][:, :], op=ALU.mult),
                     reads=[o_buf, b_e2[e1]], writes=[b_e1[e1]])
                S.op("dve", lambda e, e1=e1, c0=c0: e.scalar_tensor_tensor(
                    out=oaT[:, 0:4, c0:c0 + 128], in0=et1[e1][:, :].rearrange("p (h t) -> p h t", h=4), scalar=gnh[:, 0:1],
                    in1=bc(sg2[:, c0:c0 + 1], [[TB, 4], [1, 128]]), op0=ALU.mult, op1=ALU.mult),
                    reads=[b_e1[e1], b_sg, b_AB], writes=[b_oa[li]])


            for h in range(4):
                if h % 2 == 0:
                    ru = wring_load(win_v, 2048 + (h // 2) * 256)
                for gi, grp in enumerate(groups):
                    c0, n, tl, is_s = grp
                    bu = next_bank()
                    fm_matmul(ru, h % 2, grp, bu)
                    if gi == len(groups) - 1 and h % 2 == 1:
                        wq_done()
                    S.op("act", lambda e, bu=bu, n=n, h=h, c0=c0: e.activation(out=uu[:, h * TB + c0:h * TB + c0 + n], in_=ps[bu][:, 0:n],
                                                                              func=AF.Gelu_apprx_tanh), reads=[b_ps[bu]], writes=[b_uu] + wd_tail + b_ada)
            lnst = {}
            gvr = [zq[0], zq[1], Pinv[0], kb[0], Pb[0]]
            b_gvr = [b_zq[0], b_zq[1], b_Pi[0], b_kb[0], b_Pb[0]]
            lnc = [0]

            def ln_front(li):
                bank = 0
                for kc in range(8):
                    S.op("pe", lambda e, kc=kc, li=li, bank=bank: e.matmul(ps[bank][:, :], lhsT=aT[:, kc, li * 128:(li + 1) * 128],
                                                                         rhs=wz_v[:, kc * 512:(kc + 1) * 512], start=(kc == 0), stop=(kc == 7)),
                         reads=[b_aT[li], b_wzv], writes=[b_ps[bank]], signal=(kc == 7))
                g_ = lnc[0] % 5
                lnc[0] += 1
                lnst[li] = g_
                S.op("act", lambda e, g_=g_, bank=bank: e.activation(out=gvr[g_][:, :], in_=ps[bank][:, :], func=AF.Gelu_apprx_tanh),
                     reads=[b_ps[bank]], writes=[b_gvr[g_]])
                S.op("dve", lambda e, g_=g_: e.bn_stats(out=bnst[:, g_, :], in_=gvr[g_][:, :]), reads=[b_gvr[g_]], writes=[b_bn[g_]])
                S.op("dve", lambda e, g_=g_: e.bn_aggr(out=bnag[:, g_, 0:2], in_=bnst[:, g_, :]), reads=[b_bn[g_]], writes=[b_bn[g_]])
                S.op("dve", lambda e, g_=g_: e.tensor_scalar(out=bnag[:, g_, 2:3], in0=bnag[:, g_, 1:2], scalar1=EPS, scalar2=None, op0=ALU.add),
                     reads=[b_bn[g_]], writes=[b_bn[g_]])
                S.op("pool", lambda e, g_=g_: e.tensor_tensor(out=bnag[:, g_, 3:4], in0=bnag[:, g_, 2:3], in1=nhalf[:], op=ALU.pow),
                     reads=[b_bn[g_], b_c], writes=[b_bn[g_]])

            def ln_back(li):
                g_ = lnst[li]
                S.op("dve", lambda e, g_=g_: e.scalar_tensor_tensor(out=gvr[g_][:, :], in0=gvr[g_][:, :], scalar=bnag[:, g_, 0:1], in1=lnbc[:, 0, :],
                                                                    op0=ALU.subtract, op1=ALU.mult), reads=[b_gvr[g_], b_bn[g_], b_Gbc], writes=[b_gvr[g_]])
                is_out = (pas == 2 and li in (3, 4))
                if is_out:
                    S.op("dve", lambda e, g_=g_: e.scalar_tensor_tensor(out=vout[:], in0=gvr[g_][:, :], scalar=bnag[:, g_, 3:4], in1=lnbc[:, 1, :],
                                                                        op0=ALU.mult, op1=ALU.add), reads=[b_gvr[g_], b_bn[g_], b_Gbc], writes=[b_vout])
                    dst = o_vp if li == 3 else o_vs
                    S.dma("sp", lambda e, dst=dst: e.dma_start(out=dst, in_=vout[:]), reads=[b_vout], final=True)
                    S.op("act", lambda e, li=li: e.activation(out=vln[:, li * 512:(li + 1) * 512], in_=vout[:], func=AF.Copy),
                         reads=[b_vout], writes=[b_vln[li]] + wd_tail + b_ada)
                else:
                    S.op("dve", lambda e, g_=g_, li=li: e.scalar_tensor_tensor(out=vln[:, li * 512:(li + 1) * 512], in0=gvr[g_][:, :], scalar=bnag[:, g_, 3:4],
                                                                              in1=lnbc[:, 1, :], op0=ALU.mult, op1=ALU.add),
                         reads=[b_gvr[g_], b_bn[g_], b_Gbc], writes=[b_vln[li]] + wd_tail + b_ada)
                wsel = 1 if sflags[li] else 0
                cb = 1
                for h in range(4):
                    S.op("pe", lambda e, h=h, li=li, wsel=wsel, cb=cb: e.matmul(ps[cb][:, h * 128:(h + 1) * 128], lhsT=vln[:, li * 512 + h * 128:li * 512 + (h + 1) * 128],
                                                                               rhs=wsT[:, wsel, h, :], start=True, stop=True),
                         reads=[b_vln[li], b_ws], writes=[b_ps[cb]], signal=(h == 3))

            def ln_tail(li):
                c0 = li * 128
                wsel = 1 if sflags[li] else 0
                cb = 1
                g_ = lnst[li]
                S.op("dve", lambda e, g_=g_, wsel=wsel, cb=cb: e.tensor_tensor(out=gvr[g_][:, :], in0=ps[cb][:, :], in1=bsbc[:, wsel].rearrange("p h t -> p (h t)"), op=ALU.add),
                     reads=[b_ps[cb], b_Gbc], writes=[b_gvr[g_]])
                S.op("pool", lambda e, g_=g_, c0=c0: e.tensor_tensor(out=aT[:, 4:8, c0:c0 + 128], in0=gvr[g_][:, :].rearrange("p (h t) -> p h t", h=4),
                                                                    in1=bc(uu[:, c0:c0 + 1], [[TB, 4], [1, 128]]), op=ALU.mult),
                     reads=[b_gvr[g_], b_uu], writes=[b_aT[li]])

            LA = 4
            nt_ = len(tiles)
            if pas == 2:
                for j in range(4):
                    s0_load(j)
            front(tiles[0])
            frontB(tiles[0])
            for idx, li in enumerate(tiles):
                chainA(li)
                if idx >= 2:
                    epiB(tiles[idx - 2])
                if idx + 1 < nt_:
                    front(tiles[idx + 1])
                if idx >= 1:
                    epi(tiles[idx - 1])
                if idx + 1 < nt_:
                    frontB(tiles[idx + 1])
                chainB(li)
            if nt_ >= 2:
                epiB(tiles[nt_ - 2])
            epi(tiles[-1])
            epiB(tiles[-1])
            for i in range(min(LA, len(tiles))):
                ln_front(tiles[i])
            for idx, li in enumerate(tiles):
                if idx >= 1:
                    ln_tail(tiles[idx - 1])
                if idx + LA < len(tiles):
                    ln_front(tiles[idx + LA])
                ln_back(li)
            ln_tail(tiles[-1])


            def post_norm_res(li, banks, Gbc_t, is_s):
                col = 48 + (po_i[0] % 8) * 2
                po_i[0] += 1
                for half in range(2):
                    S.op("act", lambda e, half=half, col=col: e.activation(out=junk[:, half * 512:(half + 1) * 512], in_=ps[banks[half]][:, :], func=AF.Square,
                                                                          accum_out=ssq[:, col + half:col + half + 1]),
                         reads=[b_ps[banks[half]]], writes=[b_st[col], b_junk])
                S.op("dve", lambda e, col=col: e.tensor_tensor(out=rstd[:, col:col + 1], in0=ssq[:, col:col + 1], in1=ssq[:, col + 1:col + 2], op=ALU.add),
                     reads=[b_st[col]], writes=[b_st[col]])
                S.op("dve", lambda e, col=col: e.tensor_scalar(out=rstd[:, col:col + 1], in0=rstd[:, col:col + 1], scalar1=1.0 / D, scalar2=EPS,
                                                               op0=ALU.mult, op1=ALU.add), reads=[b_st[col]], writes=[b_st[col]])
                S.op("pool", lambda e, col=col: e.tensor_tensor(out=rstd[:, col:col + 1], in0=rstd[:, col:col + 1], in1=nhalf[:], op=ALU.pow),
                     reads=[b_st[col], b_c], writes=[b_st[col]])
                smp = 1 if is_s else 0
                for half in range(2):
                    k = rot("ft")
                    S.op("dve", lambda e, half=half, k=k, col=col, smp=smp: e.scalar_tensor_tensor(
                        out=ftmp[:, k, :], in0=ps[banks[half]][:, :], scalar=rstd[:, col:col + 1], in1=Gbc_t[:, smp, half * 512:(half + 1) * 512],
                        op0=ALU.mult, op1=ALU.mult), reads=[b_ps[banks[half]], b_st[col], b_Gbc], writes=[b_ft[k]])
                    S.op("pool" if half == 0 else "dve", lambda e, half=half, k=k, li=li: e.tensor_tensor(out=xres[:, li, half * 512:(half + 1) * 512], in0=ftmp[:, k, :],
                                                                                 in1=xres[:, li, half * 512:(half + 1) * 512], op=ALU.add),
                         reads=[b_ft[k], b_x[li]], writes=[b_x[li]])

            for li in tiles:
                banks = [next_bank(6), next_bank(6)]
                for half in range(2):
                    for kc in range(8):
                        S.op("pe", lambda e, kc=kc, li=li, half=half, bank=banks[half]: e.matmul(
                            ps[bank][:, :], lhsT=(oaT if kc < 4 else aT)[:, kc, li * 128:(li + 1) * 128],
                            rhs=wo_sb[:, kc * D + half * 512:kc * D + (half + 1) * 512],
                            start=(kc == 0), stop=(kc == 7)), reads=[b_aT[li], b_oa[li], b_wo], writes=[b_ps[banks[half]]], signal=(kc == 7))
                post_norm_res(li, banks, G1bc, sflags[li])

            if debug and pas == 2:
                for li in tiles:
                    S.dma("sp", lambda e, li=li: e.dma_start(out=d_x1[li * 128:(li + 1) * 128, :], in_=xres[:, li, :]), reads=[b_x[li]], final=True)
                S.dma("sp", lambda e: e.dma_start(out=d_oa, in_=oaT[:]), reads=b_oa, final=True)
                S.dma("sp", lambda e: e.dma_start(out=d_ob, in_=aT[:, 4:8, :]), reads=b_aT, final=True)
            pre_norm(tiles, sflags, Gpf, 24)
            xsr_ok.add((pas, "F"))
            wq_issue(wq_pos[0] + 3)
            if debug and pas == 2:
                S.dma("sp", lambda e: e.dma_start(out=d_h2, in_=aT[:]), reads=b_aT, final=True)

            first_act = [True]
            for j in range(NJ):
                if j % 2 == 0:
                    rg = wring_load(wg_v, (j // 2) * 256)
                    ru = wring_load(wu_v, (j // 2) * 256)
                for gi, grp in enumerate(groups):
                    c0, n, tl, is_s = grp
                    bg = 2 + (mmring[0] % 2) * 2
                    mmring[0] += 1
                    bu = bg + 1
                    fm_matmul(rg, j % 2, grp, bg)
                    fm_matmul(ru, j % 2, grp, bu)
                    if gi == len(groups) - 1:
                        if j % 2 == 1:
                            wq_done()
                        if j >= 4:
                            wd_load(j, ([b_wzv, b_wo, b_uu] + b_vln + b_ada) if j == 4 else [])
                    k = rot("ft")
                    S.op("act", lambda e, bg=bg, k=k, n=n: e.activation(out=ftmp[:, k, 0:n], in_=ps[bg][:, 0:n], func=AF.Tanh, scale=0.5),
                         reads=[b_ps[bg]], writes=[b_ft[k]])
                    S.op("dve", lambda e, bg=bg, k=k, n=n: e.scalar_tensor_tensor(out=ftmp2[:, k, 0:n], in0=ftmp[:, k, 0:n], scalar=1.0, in1=ps[bg][:, 0:n],
                                                                                 op0=ALU.add, op1=ALU.mult), reads=[b_ft[k], b_ps[bg]], writes=[b_ft2[k]])
                    wl = [b_act[j][gi]]
                    if first_act[0]:
                        first_act[0] = False
                        wl = wl + mix_bufs
                    S.op("dve", lambda e, bu=bu, k=k, n=n, j=j, c0=c0: e.scalar_tensor_tensor(out=actT(j, c0, n), in0=ftmp2[:, k, 0:n], scalar=0.5, in1=ps[bu][:, 0:n],
                                                                                              op0=ALU.mult, op1=ALU.mult), reads=[b_ft2[k], b_ps[bu]], writes=wl)

            if pas < 2:
                S.dma("pool", lambda e: e.dma_start(out=wz_i.rearrange("p (k n) -> p k n", k=8), in_=win_v[:, :, 1024:1536]), writes=[b_wzi] + wzi_guard)
            for li in tiles:
                gi = li // 4
                banks = [next_bank(6), next_bank(6)]
                for half in range(2):
                    for j in range(NJ):
                        S.op("pe", lambda e, j=j, li=li, half=half, bank=banks[half]: e.matmul(
                            ps[bank][:, :], lhsT=actT(j, li * 128, 128), rhs=wd_sb(j)[:, half * 512:(half + 1) * 512],
                            start=(j == 0), stop=(j == NJ - 1)), reads=[b_act[j][gi], b_wd[j]], writes=[b_ps[banks[half]]], signal=(j == NJ - 1))
                post_norm_res(li, banks, G2bc, sflags[li])
                dst = ys if sflags[li] else yp_t[gtile0 + li]
                S.dma("pool", lambda e, li=li, dst=dst: e.dma_start(out=dst, in_=xres[:, li, :]), reads=[b_x[li]], final=True)

        S.emit()
    return nc


_NC_CACHE = {}
_IN_MAPS_ONLY = False


def kernel(x_prompt, x_sample, state_hgrn, c_prompt, c_sample, lb_logits, w_ada, b_ada, norm_pre_mix, norm_post_mix,
           w_in, gnorm_w, ln_v_g, ln_v_b, w_spatial, b_spatial, w_out, norm_pre_ffn, norm_post_ffn, w_gate, w_up, w_down):
    f = lambda a: np.ascontiguousarray(np.asarray(a, dtype=np.float32))
    x_prompt, x_sample, state_hgrn, c_prompt, c_sample = map(f, (x_prompt, x_sample, state_hgrn, c_prompt, c_sample))
    vec = np.zeros((128, 128), np.float32)
    vec[R_NPRE:R_NPRE + 8] = f(norm_pre_mix)[0].reshape(8, 128)
    vec[R_NPOST:R_NPOST + 8] = f(norm_post_mix)[0].reshape(8, 128)
    vec[R_FPRE:R_FPRE + 8] = f(norm_pre_ffn)[0].reshape(8, 128)
    vec[R_FPOST:R_FPOST + 8] = f(norm_post_ffn)[0].reshape(8, 128)
    vec[R_BADA:R_BADA + 48] = f(b_ada)[0].reshape(48, 128)
    vec[R_LB0:R_LB0 + 4] = f(lb_logits)[0].reshape(4, 128)
    vec[R_LB1:R_LB1 + 4] = f(lb_logits)[1].reshape(4, 128)
    vec[R_GN] = f(gnorm_w)[0]
    vec[R_LNG:R_LNG + 4] = f(ln_v_g)[0].reshape(4, 128)
    vec[R_LNB:R_LNB + 4] = f(ln_v_b)[0].reshape(4, 128)
    vec[R_BS:R_BS + 4] = f(b_spatial)[0]
    shared = {"vecs": vec, "wspat": f(w_spatial)[0], "w_ada": f(w_ada)[0], "w_in": f(w_in)[0], "w_out": f(w_out)[0],
              "w_gate": f(w_gate)[0], "w_up": f(w_up)[0], "w_down": f(w_down)[0]}
    in_maps = []
    for c in range(NCORES):
        m = dict(shared)
        m["xp"] = x_prompt[c]
        m["xs"] = x_sample[c * NSAMP:(c + 1) * NSAMP].reshape(128, D)
        m["s0"] = state_hgrn[0, c * NSAMP:(c + 1) * NSAMP]
        m["call"] = np.concatenate([c_prompt[c:c + 1], c_sample[c * NSAMP:(c + 1) * NSAMP]], axis=0)
        in_maps.append(m)
    if _IN_MAPS_ONLY:
        return in_maps
    if "nc" not in _NC_CACHE:
        _NC_CACHE["nc"] = build_nc()
    nc = _NC_CACHE["nc"]
    res = run_bass_kernel_spmd(nc, in_maps, core_ids=list(range(NCORES)))
    R = res.results
    y_p = np.stack([R[c]["yp"] for c in range(NCORES)], 0)
    y_s = np.concatenate([R[c]["ys"].reshape(NSAMP, 8, D) for c in range(NCORES)], 0)
    s_p = np.stack([R[c]["o_sp"] for c in range(NCORES)], 0)[None]
    s_s = np.concatenate([R[c]["o_ss"] for c in range(NCORES)], 0)[None]
    v_p = np.stack([R[c]["o_vp"].reshape(128, 4, 128) for c in range(NCORES)], 0)[None]
    v_s = np.concatenate([R[c]["o_vs"].reshape(NSAMP, 8, 4, 128) for c in range(NCORES)], 0)[None]
    return (y_p.astype(np.float32), y_s.astype(np.float32), s_p.astype(np.float32), s_s.astype(np.float32),
            v_p.astype(np.float32), v_s.astype(np.float32))
```
